# Optimizing a Trainium2 kernel written in Bass

```python
import math
import jax, jax.numpy as jnp
from jax import lax
import numpy as np

D_MODEL = 2048
BATCH = 8
SEQ = 2048
DEPTH = 2

CHUNK = 128
NORM_EPS = 1e-5

GMLP_WIDTH = D_MODEL
GMLP_GROUPS = 16
GMLP_GROUP_DIM = GMLP_WIDTH // GMLP_GROUPS
SSD_WIDTH = D_MODEL
SSD_HEAD_DIM = 64
SSD_HEADS = SSD_WIDTH // SSD_HEAD_DIM
SSD_GROUPS = 4
SSD_STATE = 128
SSD_CONV = 4
SSD_BC_DIM = SSD_GROUPS * SSD_STATE
SSD_CONV_DIM = SSD_WIDTH + 2 * SSD_BC_DIM
EVEN_MIX = GMLP_WIDTH + SSD_WIDTH
EVEN_SPLITS = (2 * GMLP_WIDTH,
               3 * GMLP_WIDTH,
               3 * GMLP_WIDTH + SSD_WIDTH,
               3 * GMLP_WIDTH + SSD_WIDTH + SSD_CONV_DIM)
EVEN_IN = EVEN_SPLITS[-1] + SSD_HEADS

DIFF_HEADS = 16
DIFF_HEAD_DIM = 64
DIFF_V_DIM = 2 * DIFF_HEAD_DIM
DIFF_WIDTH = DIFF_HEADS * DIFF_V_DIM
ODD_IN = 4 * DIFF_WIDTH

kernel_name = "hybrid_gmlp_ssd_diffattn_block"


def rms_norm(x, g):
    xf = x.astype(jnp.float32)
    y = xf * lax.rsqrt(jnp.mean(xf * xf, axis=-1, keepdims=True) + NORM_EPS)
    return (y * g.astype(jnp.float32)).astype(x.dtype)


def layer_norm(x, g, b):
    xf = x.astype(jnp.float32)
    mu = jnp.mean(xf, axis=-1, keepdims=True)
    var = jnp.mean(jnp.square(xf - mu), axis=-1, keepdims=True)
    y = (xf - mu) * lax.rsqrt(var + NORM_EPS)
    return (y * g.astype(jnp.float32) + b.astype(jnp.float32)).astype(x.dtype)


def gated_group_rms_norm(y, z, g, n_groups):
    yz = y.astype(jnp.float32) * jax.nn.silu(z.astype(jnp.float32))
    shp = yz.shape
    yz = yz.reshape(shp[:-1] + (n_groups, shp[-1] // n_groups))
    yz = yz * lax.rsqrt(jnp.mean(yz * yz, axis=-1, keepdims=True) + NORM_EPS)
    return (yz.reshape(shp) * g.astype(jnp.float32)).astype(z.dtype)


def causal_depthwise_conv(x, w, bias):
    k = w.shape[0]
    out = lax.conv_general_dilated(
        x, w[:, None, :].astype(x.dtype), window_strides=(1,),
        padding=((k - 1, 0),), dimension_numbers=('NWC', 'WIO', 'NWC'),
        feature_group_count=x.shape[-1])
    return out + bias.astype(x.dtype)


def ssd_chunked_scan(x, dt, a, bmat, cmat):
    b, L, H, P = x.shape
    G, N = bmat.shape[2], bmat.shape[3]
    R = H // G
    nc = L // CHUNK
    xdt = (x.astype(jnp.float32) * dt[..., None]).reshape(b, nc, CHUNK, G, R, P)
    adt = (dt * a).reshape(b, nc, CHUNK, G, R)
    a_cum = jnp.cumsum(adt, axis=2)
    bm = bmat.astype(jnp.float32).reshape(b, nc, CHUNK, G, N)
    cm = cmat.astype(jnp.float32).reshape(b, nc, CHUNK, G, N)
    causal = jnp.tril(jnp.ones((CHUNK, CHUNK), dtype=bool))
    seg = a_cum[:, :, :, None] - a_cum[:, :, None, :]
    decay = jnp.exp(jnp.where(causal[None, None, :, :, None, None], seg, -jnp.inf))
    cb = jnp.einsum('bclgn,bcsgn->bclsg', cm, bm)
    y_diag = jnp.einsum('bclsg,bclsgr,bcsgrp->bclgrp', cb, decay, xdt)
    decay_to_end = jnp.exp(a_cum[:, :, -1:] - a_cum)
    states = jnp.einsum('bclgn,bclgr,bclgrp->bcgrpn', bm, decay_to_end, xdt)
    chunk_decay = jnp.exp(a_cum[:, :, -1])

    def step(carry, inp):
        st, dec = inp
        return carry * dec[..., None, None] + st, carry

    init = jnp.zeros((b, G, R, P, N), jnp.float32)
    _, prev = lax.scan(step, init, (jnp.moveaxis(states, 1, 0), jnp.moveaxis(chunk_decay, 1, 0)))
    prev = jnp.moveaxis(prev, 0, 1)
    y_off = jnp.einsum('bclgn,bcgrpn,bclgr->bclgrp', cm, prev, jnp.exp(a_cum))
    return (y_diag + y_off).reshape(b, L, H, P)


def even_layer(x, norm_g, w_in, gmlp_ln_g, gmlp_ln_b, spatial_w, spatial_b,
               conv_w, conv_b, dt_bias, a_log, d_skip, ssm_norm_g, w_out):
    b, L, _ = x.shape
    h = rms_norm(x, norm_g)
    proj = h @ w_in.astype(h.dtype)
    uv, z_a, z_b, xbc, dt_raw = jnp.split(proj, list(EVEN_SPLITS), axis=-1)

    u, v = jnp.split(jax.nn.gelu(uv), 2, axis=-1)
    v = layer_norm(v, gmlp_ln_g, gmlp_ln_b)
    nc = L // CHUNK
    v = v.reshape(b, nc, CHUNK, GMLP_GROUPS, GMLP_GROUP_DIM)
    causal = jnp.tril(jnp.ones((CHUNK, CHUNK), dtype=bool))
    w_s = jnp.where(causal[None], spatial_w, 0).astype(v.dtype)
    v_mix = jnp.einsum('gts,bnsgc->bntgc', w_s, v) + spatial_b.T.astype(v.dtype)[None, None, :, :, None]
    y_a = u * v_mix.reshape(b, L, GMLP_WIDTH) * jax.nn.silu(z_a)

    xbc = jax.nn.silu(causal_depthwise_conv(xbc, conv_w, conv_b))
    xs, bmat, cmat = jnp.split(xbc, [SSD_WIDTH, SSD_WIDTH + SSD_BC_DIM], axis=-1)
    xs = xs.reshape(b, L, SSD_HEADS, SSD_HEAD_DIM)
    bmat = bmat.reshape(b, L, SSD_GROUPS, SSD_STATE)
    cmat = cmat.reshape(b, L, SSD_GROUPS, SSD_STATE)
    dt = jax.nn.softplus(dt_raw.astype(jnp.float32) + dt_bias.astype(jnp.float32))
    a = -jnp.exp(a_log.astype(jnp.float32))
    y = ssd_chunked_scan(xs, dt, a, bmat, cmat) + d_skip.astype(jnp.float32)[:, None] * xs.astype(jnp.float32)
    y_b = gated_group_rms_norm(y.reshape(b, L, SSD_WIDTH), z_b, ssm_norm_g, SSD_GROUPS)

    y_cat = jnp.concatenate([y_a, y_b.astype(y_a.dtype)], axis=-1)
    return x + (y_cat @ w_out.astype(y_cat.dtype)).astype(x.dtype)


def odd_layer(x, norm_g, w_in, lambda_q1, lambda_k1, lambda_q2, lambda_k2,
              subln_g, w_out, layer_idx):
    b, L, _ = x.shape
    h = rms_norm(x, norm_g)
    proj = h @ w_in.astype(h.dtype)
    q, k, v, gate = jnp.split(proj, 4, axis=-1)
    q = q.reshape(b, L, DIFF_HEADS, 2, DIFF_HEAD_DIM).transpose(0, 2, 3, 1, 4)
    k = k.reshape(b, L, DIFF_HEADS, 2, DIFF_HEAD_DIM).transpose(0, 2, 3, 1, 4)
    v = v.reshape(b, L, DIFF_HEADS, DIFF_V_DIM).transpose(0, 2, 1, 3)

    lambda_init = 0.8 - 0.6 * math.exp(-0.3 * layer_idx)
    lam = (jnp.exp(jnp.sum(lambda_q1.astype(jnp.float32) * lambda_k1.astype(jnp.float32)))
           - jnp.exp(jnp.sum(lambda_q2.astype(jnp.float32) * lambda_k2.astype(jnp.float32)))
           + lambda_init)
    slopes = 2.0 ** (-8.0 * (jnp.arange(DIFF_HEADS, dtype=jnp.float32) + 1.0) / DIFF_HEADS)
    scale = DIFF_HEAD_DIM ** -0.5

    outs = []
    for i in range(L // CHUNK):
        n_k = (i + 1) * CHUNK
        q_blk = q[:, :, :, i * CHUNK:n_k]
        s = jnp.einsum('bhiqd,bhikd->bhiqk', q_blk, k[:, :, :, :n_k]).astype(jnp.float32) * scale
        q_pos = i * CHUNK + jnp.arange(CHUNK)
        dist = (q_pos[:, None] - jnp.arange(n_k)[None, :]).astype(jnp.float32)
        s = s - slopes[:, None, None, None] * dist
        s = jnp.where(dist >= 0, s, -jnp.inf)
        p = jax.nn.softmax(s, axis=-1)
        attn = p[:, :, 0] - lam * p[:, :, 1]
        outs.append(jnp.einsum('bhqk,bhkv->bhqv', attn.astype(v.dtype), v[:, :, :n_k]))
    o = jnp.concatenate(outs, axis=2)
    o = rms_norm(o, subln_g) * (1.0 - lambda_init)
    o = o.transpose(0, 2, 1, 3).reshape(b, L, DIFF_WIDTH) * jax.nn.silu(gate)
    return x + (o @ w_out.astype(o.dtype)).astype(x.dtype)


def setup_inputs(seed: int = 0) -> dict:
    key = jax.random.key(seed)
    ks = jax.random.split(key, 24)
    f32 = jnp.float32
    nrm = lambda k, shp, s: jax.random.normal(k, shp, f32) * s
    x = jax.random.normal(ks[0], (BATCH, SEQ, D_MODEL), f32)
    causal = jnp.tril(jnp.ones((CHUNK, CHUNK), f32))
    dt = jnp.exp(jax.random.uniform(ks[9], (SSD_HEADS,), f32) * (math.log(0.1) - math.log(0.001)) + math.log(0.001))
    return {
        "x": x,
        "l0_norm_g": 1.0 + nrm(ks[1], (D_MODEL,), 0.02),
        "l0_w_in": nrm(ks[2], (D_MODEL, EVEN_IN), D_MODEL ** -0.5),
        "l0_gmlp_ln_g": 1.0 + nrm(ks[3], (GMLP_WIDTH,), 0.02),
        "l0_gmlp_ln_b": nrm(ks[4], (GMLP_WIDTH,), 0.02),
        "l0_spatial_w": nrm(ks[5], (GMLP_GROUPS, CHUNK, CHUNK), 1.0) * causal * lax.rsqrt(jnp.arange(1, CHUNK + 1, dtype=f32))[:, None],
        "l0_spatial_b": 1.0 + nrm(ks[6], (GMLP_GROUPS, CHUNK), 0.1),
        "l0_conv_w": nrm(ks[7], (SSD_CONV, SSD_CONV_DIM), SSD_CONV ** -0.5),
        "l0_conv_b": nrm(ks[8], (SSD_CONV_DIM,), 0.02),
        "l0_dt_bias": dt + jnp.log(-jnp.expm1(-dt)),
        "l0_a_log": jnp.log(jax.random.uniform(ks[10], (SSD_HEADS,), f32, 1.0, 16.0)),
        "l0_d_skip": 1.0 + nrm(ks[11], (SSD_HEADS,), 0.1),
        "l0_ssm_norm_g": 1.0 + nrm(ks[12], (SSD_WIDTH,), 0.02),
        "l0_w_out": nrm(ks[13], (EVEN_MIX, D_MODEL), EVEN_MIX ** -0.5),
        "l1_norm_g": 1.0 + nrm(ks[14], (D_MODEL,), 0.02),
        "l1_w_in": nrm(ks[15], (D_MODEL, ODD_IN), D_MODEL ** -0.5),
        "l1_lambda_q1": nrm(ks[16], (DIFF_HEAD_DIM,), 0.1),
        "l1_lambda_k1": nrm(ks[17], (DIFF_HEAD_DIM,), 0.1),
        "l1_lambda_q2": nrm(ks[18], (DIFF_HEAD_DIM,), 0.1),
        "l1_lambda_k2": nrm(ks[19], (DIFF_HEAD_DIM,), 0.1),
        "l1_subln_g": 1.0 + nrm(ks[20], (DIFF_V_DIM,), 0.02),
        "l1_w_out": nrm(ks[21], (DIFF_WIDTH, D_MODEL), DIFF_WIDTH ** -0.5),
        "final_norm_g": 1.0 + nrm(ks[22], (D_MODEL,), 0.02),
    }


def reference(x, l0_norm_g, l0_w_in, l0_gmlp_ln_g, l0_gmlp_ln_b, l0_spatial_w,
              l0_spatial_b, l0_conv_w, l0_conv_b, l0_dt_bias, l0_a_log, l0_d_skip,
              l0_ssm_norm_g, l0_w_out, l1_norm_g, l1_w_in, l1_lambda_q1,
              l1_lambda_k1, l1_lambda_q2, l1_lambda_k2, l1_subln_g, l1_w_out,
              final_norm_g):
    for layer in range(DEPTH):
        if layer % 2 == 0:
            x = even_layer(x, l0_norm_g, l0_w_in, l0_gmlp_ln_g, l0_gmlp_ln_b,
                           l0_spatial_w, l0_spatial_b, l0_conv_w, l0_conv_b,
                           l0_dt_bias, l0_a_log, l0_d_skip, l0_ssm_norm_g, l0_w_out)
        else:
            x = odd_layer(x, l1_norm_g, l1_w_in, l1_lambda_q1, l1_lambda_k1,
                          l1_lambda_q2, l1_lambda_k2, l1_subln_g, l1_w_out, layer)
    return rms_norm(x, final_norm_g)
```

```python
import math
from contextlib import ExitStack
import numpy as np
import concourse.bass as bass
import concourse.mybir as mybir
from concourse.bass_utils import run_bass_kernel_spmd

F32 = mybir.dt.float32
BF16 = mybir.dt.bfloat16
AF = mybir.ActivationFunctionType
ALU = mybir.AluOpType
AX = mybir.AxisListType

ENGS = ["pe", "act", "dve", "pool", "sp"]
N_DMA_SEMS = 56
N_HW_SEMS = 40
EPS = 1e-5
D = 2048
L = 2048
LAMBDA_INIT = 0.8 - 0.6 * math.exp(-0.3 * 1)


class Buf:
    __slots__ = ("name", "w", "r", "excl")

    def __init__(self, name, excl=False):
        self.name = name
        self.w = None
        self.r = []
        self.excl = excl


class Op:
    __slots__ = ("eng", "fn", "deps", "signal", "sig", "is_dma", "slot", "cnt")

    def __init__(self, eng, fn, is_dma=False):
        self.eng = eng
        self.fn = fn
        self.deps = []
        self.signal = False
        self.sig = 0
        self.is_dma = is_dma
        self.slot = -1
        self.cnt = 0


class Sched:
    def __init__(self, nc):
        self.nc = nc
        self.ops = {e: [] for e in ENGS}
        self.n_dma = 0
        self.n_sw = 0
        self.slot_last = [None] * N_DMA_SEMS
        self.slot_uses = [0] * N_DMA_SEMS

    def _track(self, o, reads, writes):
        ex = [b for b in reads if b.excl]
        if ex:
            reads = [b for b in reads if not b.excl]
            writes = list(writes) + ex
        deps = []
        for b in reads:
            if b.w is not None:
                deps.append(b.w)
        for b in writes:
            if b.w is not None:
                deps.append(b.w)
            deps.extend(b.r)
        seen = set()
        for d in deps:
            if d is o or id(d) in seen:
                continue
            seen.add(id(d))
            if d.eng == "pe" and o.eng == "pe" and not d.is_dma and not o.is_dma:
                continue
            o.deps.append(d)
            d.signal = True
        for b in reads:
            if not o.is_dma:
                b.r = [x for x in b.r if x.is_dma or x.eng != o.eng]
            b.r.append(o)
        for b in writes:
            b.w = o
            b.r = []

    def op(self, eng, fn, reads=(), writes=()):
        o = Op(eng, fn)
        self._track(o, reads, writes)
        self.ops[eng].append(o)
        return o

    def dma(self, out, in_, reads=(), writes=(), eng="sp"):
        def fn(e, out=out, in_=in_):
            return e.dma_start(out=out, in_=in_)
        o = Op(eng, fn, is_dma=True)
        if eng == "pool":
            slot = N_HW_SEMS + self.n_sw % (N_DMA_SEMS - N_HW_SEMS)
            self.n_sw += 1
        else:
            slot = self.n_dma % N_HW_SEMS
            self.n_dma += 1
        o.slot = slot
        self.slot_uses[slot] += 1
        o.cnt = 16 * self.slot_uses[slot]
        prev = self.slot_last[slot]
        self._track(o, reads, writes)
        if prev is not None and all(d is not prev for d in o.deps):
            o.deps.append(prev)
        self.slot_last[slot] = o
        o.signal = True
        self.ops[eng].append(o)
        return o

    def emit(self, stack):
        nc = self.nc
        esem = {e: stack.enter_context(nc.semaphore("s_" + e)) for e in ENGS}
        dsem = [stack.enter_context(nc.semaphore("d%d" % i)) for i in range(N_DMA_SEMS)]
        for e in ENGS:
            c = 0
            for o in self.ops[e]:
                if o.is_dma:
                    continue
                if o.signal:
                    c += 1
                    o.sig = c
        block = stack.enter_context(nc.Block())

        def run(e, eng):
            waited = {}
            for o in self.ops[e]:
                for d in o.deps:
                    if d.is_dma:
                        key, val, sem = ("d", d.slot), d.cnt, dsem[d.slot]
                    else:
                        key, val, sem = ("e", d.eng), d.sig, esem[d.eng]
                    if waited.get(key, 0) >= val:
                        continue
                    waited[key] = val
                    eng.wait_ge(sem, val)
                ins = o.fn(eng)
                if o.is_dma:
                    ins.then_inc(dsem[o.slot], 16)
                elif o.signal:
                    ins.then_inc(esem[e], 1)
            if e == "sp":
                for s in range(N_DMA_SEMS):
                    if self.slot_uses[s]:
                        eng.wait_ge(dsem[s], 16 * self.slot_uses[s])

        @block.tensor
        def _(eng):
            run("pe", eng)

        @block.scalar
        def _(eng):
            run("act", eng)

        @block.vector
        def _(eng):
            run("dve", eng)

        @block.gpsimd
        def _(eng):
            run("pool", eng)

        @block.sync
        def _(eng):
            run("sp", eng)


class Cx:
    pass


def act(S, out, in_, func, reads, writes, **kw):
    S.op("act", lambda e: e.activation(out=out, in_=in_, func=func, **kw), reads, writes)


def tt(S, eng, out, in0, in1, op, reads, writes):
    S.op(eng, lambda e: e.tensor_tensor(out=out, in0=in0, in1=in1, op=op), reads, writes)


def ts(S, eng, out, in0, s1, s2, op0, op1, reads, writes):
    if op1 is None:
        S.op(eng, lambda e: e.tensor_scalar(out=out, in0=in0, scalar1=s1, scalar2=None, op0=op0), reads, writes)
    else:
        S.op(eng, lambda e: e.tensor_scalar(out=out, in0=in0, scalar1=s1, scalar2=s2, op0=op0, op1=op1), reads, writes)


def stt(S, out, in0, scalar, in1, op0, op1, reads, writes):
    S.op("dve", lambda e: e.scalar_tensor_tensor(out=out, in0=in0, scalar=scalar, in1=in1, op0=op0, op1=op1), reads, writes)


def cp(S, eng, out, in_, reads, writes):
    if eng == "act":
        S.op("act", lambda e: e.activation(out=out, in_=in_, func=AF.Copy), reads, writes)
    else:
        S.op(eng, lambda e: e.tensor_copy(out=out, in_=in_), reads, writes)


def recip(S, out, in_, reads, writes):
    S.op("dve", lambda e: e.reciprocal(out=out, in_=in_), reads, writes)


def memset(S, eng, ap, val, writes):
    S.op(eng, lambda e: e.memset(ap, val), (), writes)


def mmg(S, out, pairs, reads, writes, start=True, stop=True):
    n = len(pairs)

    def fn(e):
        ins = None
        for i, (l, r) in enumerate(pairs):
            ins = e.matmul(out, lhsT=l, rhs=r, start=(start and i == 0), stop=(stop and i == n - 1))
        return ins
    S.op("pe", fn, reads, writes)


def transp(S, out, in_, ident, reads, writes):
    S.op("pe", lambda e: e.transpose(out, in_, ident), reads, writes)


def bc(ap, shape):
    return ap.broadcast_to(shape)


def fence(cx, olds, news):
    cx.S.op("pool", lambda e: e.memset(cx.fz[:, 0:1], 0.0), (), list(olds) + list(news) + [cx.fzB])


def emit_rmsnorm_fm(cx, xT, gcol, gB, tag, have_rs=False, xdep=()):
    S, F, PS = cx.S, cx.F, cx.PS
    for kt in range(0 if have_rs else 16):
        xt, xb = F[kt % 2]
        S.dma(xt[:, 0:L], xT[kt], writes=[xb])
        sq, sqb = F[2]
        act(S, sq[:, 0:L], xt[:, 0:L], AF.Square, [xb], [sqb])
        for tc in range(4):
            ps, pb = PS[tc]
            S.op("pe", lambda e, ps=ps, sq=sq, tc=tc, kt=kt: e.matmul(
                ps[:, :], lhsT=cx.ones_f[:, :], rhs=sq[:, tc * 512:(tc + 1) * 512], start=(kt == 0), stop=(kt == 15)),
                [sqb, cx.constb], [pb])
    rs, rsb = F[2]
    for tc in range(0 if have_rs else 4):
        ps, pb = PS[tc]
        act(S, rs[:, tc * 512:(tc + 1) * 512], ps[:, :], AF.Sqrt, [pb], [rsb], scale=1.0 / D, bias=EPS)
    if not have_rs:
        recip(S, rs[:, 0:L], rs[:, 0:L], [rsb], [rsb])
    for kt in range(16):
        xt, xb = F[kt % 2]
        S.dma(xt[:, 0:L], xT[kt], reads=list(xdep), writes=[xb])
        stt(S, cx.hT[:, kt, :], xt[:, 0:L], gcol[:, kt:kt + 1], rs[:, 0:L], ALU.mult, ALU.mult,
            [xb, rsb, gB], [cx.hTb[kt]])


def load_w(cx, i, src, n):
    wt, wb = cx.W[i % 2]
    cx.S.dma(wt[:, 0:n], src, writes=[wb], eng="pool")
    return wt, wb


def tm_proj512(cx, wsrc, consume):
    S, PS = cx.S, cx.PS
    f0v = cx.F[0][0][:, :].bitcast(BF16)
    f1v = cx.F[1][0][:, :].bitcast(BF16)
    sets = [[(cx.W[0][0][:, 0:4096], cx.W[0][1]), (cx.W[1][0][:, 0:4096], cx.W[1][1])],
            [(f0v[:, 0:4096], cx.F[0][1]), (f1v[:, 0:4096], cx.F[1][1])]]
    for blk in range(4):
        halves = []
        for hf in range(2):
            wt, wb = sets[blk % 2][hf]
            S.dma(wt, wsrc[blk][:, hf * 4096:(hf + 1) * 4096], writes=[wb], eng="pool")
            halves.append((wt.rearrange("p (k c) -> p k c", k=8), wb))
        for t in range(16):
            ps, pb = PS[t % 8]
            pairs = [(cx.hT[:, kt, t * 128:(t + 1) * 128], halves[kt // 8][0][:, kt % 8, :]) for kt in range(16)]
            mmg(S, ps[:, :], pairs, [halves[0][1], halves[1][1]] + cx.hTb, [pb])
            consume(blk, t, ps, pb)


def emit_l0(cx, T):
    S, F, H, J, PS, W = cx.S, cx.F, cx.H, cx.J, cx.PS, cx.W
    nc = cx.nc
    hT, hTb = cx.hT, cx.hTb
    sm = cx.small

    g0 = sm("g0", [128, 16]); g0b = Buf("g0")
    S.dma(g0[:], T["l0_g"], writes=[g0b])
    cw = sm("cw", [128, 96]); cb_ = sm("cb", [128, 24]); cwb = Buf("cw")
    S.dma(cw[:], T["l0_cw"], writes=[cwb])
    S.dma(cb_[:], T["l0_cb"], writes=[cwb])
    dtb = sm("dtb", [128, 32]); a_b = sm("a_b", [128, 32]); dsk = sm("dsk", [128, 32]); pb3 = Buf("p3")
    S.dma(dtb[:], bc(T["l0_dtb"], [128, 32]), writes=[pb3])
    S.dma(a_b[:], bc(T["l0_alog"], [128, 32]), writes=[pb3])
    S.dma(dsk[:], bc(T["l0_dsk"], [128, 32]), writes=[pb3])
    act(S, a_b[:], a_b[:], AF.Exp, [pb3], [pb3])
    ts(S, "dve", a_b[:], a_b[:], -1.0, None, ALU.mult, None, [pb3], [pb3])
    ssg = sm("ssg", [128, 16]); ssgb = Buf("ssg")
    S.dma(ssg[:], T["l0_ssg"], writes=[ssgb])

    emit_rmsnorm_fm(cx, T["xT"], g0, g0b, "l0")

    if STOP == "a1":
        return
    R2 = cx.R2
    BT = R2[:, 0:8192].rearrange("p (g t) -> p g t", g=4)
    CT = R2[:, 8192:16384].rearrange("p (g t) -> p g t", g=4)
    Btm = R2[:, 16384:24576].rearrange("p (c g n) -> p c g n", c=16, g=4)
    BTb = [Buf("BT%d" % g) for g in range(4)]
    CTb = [Buf("CT%d" % g) for g in range(4)]
    Btmb = Buf("Btm")
    x_tm, szb_tm, yb_tm, gv_tm = T["x_tm"], T["szb_tm"], T["yb_tm"], T["gv_tm"]
    x_tmb, szb_tmb, yb_tmb, gv_tmb = Buf("x_tm"), Buf("szb_tm"), Buf("yb_tm"), Buf("gv_tm")

    for i in range(2):
        memset(S, "pool", F[i][0][:, 0:3], 0.0, [F[i][1]])
    p2tail = []
    for ft in range(24):
        wt, wb = load_w(cx, ft, T["l0_wxbc"][ft], 2048)
        wv = wt[:, 0:2048].rearrange("p (k c) -> p k c", k=16)
        xpad, xpb = F[ft % 2]
        for tc in range(4):
            ps, pb = PS[(ft * 4 + tc) % 4]
            mmg(S, ps[:, :], [(wv[:, kt, :], hT[:, kt, tc * 512:(tc + 1) * 512]) for kt in range(16)],
                [wb] + hTb, [pb])
            cp(S, "act", xpad[:, 3 + tc * 512:3 + (tc + 1) * 512], ps[:, :], [pb], [xpb])
        while p2tail:
            p2tail.pop(0)()
        acc, accb = F[2]
        ts(S, "dve", acc[:, 0:L], xpad[:, 0:L], cw[:, ft * 4:ft * 4 + 1], None, ALU.mult, None, [xpb, cwb], [accb])
        for k in range(1, 4):
            stt(S, acc[:, 0:L], xpad[:, k:k + L], cw[:, ft * 4 + k:ft * 4 + k + 1], acc[:, 0:L], ALU.mult, ALU.add,
                [xpb, cwb, accb], [accb])
        if ft < 16:
            xc, xcb = H[ft % 2]
            act(S, xc[:, :], acc[:, 0:L], AF.Silu, [accb, cwb], [xcb], bias=cb_[:, ft:ft + 1])

            def tail(ft=ft, xc=xc, xcb=xcb):
                stg, stgb = H[2]
                for half in range(2):
                    ps, pb = PS[4 + half]
                    psv = ps[:, :].bitcast(BF16)
                    for kk in range(8):
                        c = half * 8 + kk
                        transp(S, psv[:, kk * 128:(kk + 1) * 128], xc[:, c * 128:(c + 1) * 128], cx.ident[:, :],
                               [xcb, cx.constb], [pb])
                    cp(S, "dve", stg[:, half * 1024:(half + 1) * 1024], psv[:, 0:1024], [pb], [stgb])
                S.dma(x_tm.rearrange("c p n -> p c n")[:, :, ft * 128:(ft + 1) * 128],
                      stg[:, :].rearrange("p (c n) -> p c n", c=16), reads=[stgb], writes=[x_tmb])
            p2tail.append(tail)
        elif ft < 20:
            g = ft - 16
            act(S, BT[:, g, :], acc[:, 0:L], AF.Silu, [accb, cwb], [BTb[g]], bias=cb_[:, ft:ft + 1])

            def tail(g=g):
                for half in range(2):
                    ps, pb = PS[4 + half]
                    psv = ps[:, :].bitcast(BF16)
                    for kk in range(8):
                        c = half * 8 + kk
                        transp(S, psv[:, kk * 128:(kk + 1) * 128], BT[:, g, c * 128:(c + 1) * 128], cx.ident[:, :],
                               [BTb[g], cx.constb], [pb])
                    cp(S, "dve", Btm[:, half * 8:(half + 1) * 8, g, :], psv[:, 0:1024].rearrange("p (c n) -> p c n", c=8),
                       [pb], [Btmb])
            p2tail.append(tail)
        else:
            g = ft - 20
            act(S, CT[:, g, :], acc[:, 0:L], AF.Silu, [accb, cwb], [CTb[g]], bias=cb_[:, ft:ft + 1])
    while p2tail:
        p2tail.pop(0)()

    if STOP == "a2":
        return
    def zb_out(blk, t, ps, pb):
        stg, stgb = H[(t // 4) % 2]
        sl = stg[:, (t % 4) * 512:(t % 4 + 1) * 512]
        act(S, sl, ps[:, :], AF.Silu, [pb], [stgb])
        S.dma(szb_tm[t][:, blk * 512:(blk + 1) * 512], sl, reads=[stgb], writes=[szb_tmb])
    tm_proj512(cx, T["l0_wzb"], zb_out)
    dt_tm = sm("dt_tm", [128, 16, 32]); adt_tm = sm("adt_tm", [128, 16, 32]); dtB = Buf("dt")
    wt, wb = load_w(cx, 0, T["l0_wdt"], 512)
    wv = wt[:, 0:512].rearrange("p (k c) -> p k c", k=16)
    for t in range(16):
        ps, pb = PS[t % 4]
        mmg(S, ps[:, 0:32], [(hT[:, kt, t * 128:(t + 1) * 128], wv[:, kt, :]) for kt in range(16)], [wb] + hTb, [pb])
        tt(S, "dve", dt_tm[:, t, :], ps[:, 0:32], dtb[:], ALU.add, [pb, pb3], [dtB])
    act(S, dt_tm[:], dt_tm[:], AF.Exp, [dtB], [dtB])
    act(S, dt_tm[:], dt_tm[:], AF.Ln, [dtB], [dtB], bias=1.0)
    tt(S, "dve", adt_tm[:], dt_tm[:], a_b[:].unsqueeze(1).broadcast_to([128, 16, 32]), ALU.mult, [dtB, pb3], [dtB])

    if STOP == "a3":
        return
    prev, prevb_all = F[0]
    prevv = prev[:, 0:2048].rearrange("p (g n) -> p g n", g=4)
    memset(S, "pool", prev[:, 0:2048], 0.0, [prevb_all])
    prevB = [Buf("prev%d" % g) for g in range(4)]
    pbf_t, pbf_allb = W[1]
    pbf = pbf_t[:, 0:2048].rearrange("p (g n) -> p g n", g=4)
    memset(S, "pool", pbf_t[:, 0:2048], 0.0, [pbf_allb])
    pbfB = [Buf("pbf%d" % g) for g in range(4)]
    Mt_t, Mt_allb = W[0]
    Mt = [Mt_t[:, i * 1024:(i + 1) * 1024].rearrange("p (r l) -> p r l", r=8) for i in range(2)]
    MtB = [Buf("Mt0"), Buf("Mt1")]
    CBm = Mt_t[:, 2048:2560].rearrange("p (g l) -> p g l", g=4); CBmB = Buf("CBm")
    Et = [Mt_t[:, 2560 + i * 512:2560 + (i + 1) * 512].rearrange("p (r l) -> p r l", r=4) for i in range(2)]
    EtB = [Buf("Et0"), Buf("Et1")]
    f1, f1b_all = F[1]
    f2, f2b_all = F[2]
    t1s = [f1[:, 0:512], f1[:, 512:1024]]; t2s = [f1[:, 1024:1536], f1[:, 1536:2048]]
    yvs = [f2[:, 1024:1536], f2[:, 1536:2048]]
    junk = cx.R2[:, 32768:33280]
    t1Bs, t2Bs, yvBs = [Buf("t1a"), Buf("t1b")], [Buf("t2a"), Buf("t2b")], [Buf("yva"), Buf("yvb")]
    junkB = Buf("junk")
    seg = [f2[:, i * 512:(i + 1) * 512].rearrange("p (r l) -> p r l", r=4) for i in range(2)]
    segB = [Buf("seg0"), Buf("seg1")]
    acum = sm("acum", [128, 32]); nacum = sm("nacum", [128, 32]); alast = sm("alast", [128, 32])
    ea = sm("ea", [128, 32]); cdb = sm("cdb", [128, 32]); dte = sm("dte", [128, 32]); ss = sm("ss", [128, 4])
    rr = sm("rr", [128, 4])
    smB = Buf("ssd_small")
    ssBs = [Buf("ssd_ss0"), Buf("ssd_ss1")]
    p4sub = prevB + pbfB + MtB + [CBmB] + EtB + t1Bs + t2Bs + yvBs + segB
    p4whole = [prevb_all, pbf_allb, Mt_allb, f1b_all, f2b_all]
    fence(cx, p4whole, p4sub)
    xdt, xdtB = J[1]
    xdts, xdtsB = J[2]
    ybt, ybB = J[3]
    cnt = 0
    def p4_load(c):
        xc, xcb = H[c % 2]
        S.dma(xc[:, :], x_tm[c], reads=[x_tmb], writes=[xcb])
        zc, zcb = (H[2] if c % 2 == 0 else J[0])
        S.dma(zc[:, :], szb_tm[c], reads=[szb_tmb], writes=[zcb])
    p4_load(0)
    for c in range(16):
        cs = slice(c * 128, (c + 1) * 128)
        xc, xcb = H[c % 2]
        zc, zcb = (H[2] if c % 2 == 0 else J[0])
        if c < 15:
            p4_load(c + 1)
        psa, psab = PS[4]
        mmg(S, psa[:, 0:32], [(cx.tri_f[:, :], adt_tm[:, c, :])], [dtB, cx.constb], [psab])
        mmg(S, psa[:, 32:64], [(cx.ones_f[:, :], adt_tm[:, c, :])], [dtB, cx.constb], [psab])
        cp(S, "dve", acum[:], psa[:, 0:32], [psab], [smB])
        cp(S, "dve", alast[:], psa[:, 32:64], [psab], [smB])
        ts(S, "dve", nacum[:], acum[:], -1.0, None, ALU.mult, None, [smB], [smB])
        act(S, ea[:], acum[:], AF.Exp, [smB], [smB])
        act(S, cdb[:], alast[:], AF.Exp, [smB], [smB])
        tt(S, "dve", dte[:], alast[:], acum[:], ALU.subtract, [smB], [smB])
        act(S, dte[:], dte[:], AF.Exp, [smB], [smB])
        psc, pscb = PS[5]
        for g in range(4):
            mmg(S, psc[:, g * 128:(g + 1) * 128], [(BT[:, g, cs], CT[:, g, cs])], [BTb[g], CTb[g]], [pscb])
        tt(S, "dve", CBm, psc[:, :].rearrange("p (g l) -> p g l", g=4),
           cx.maskT[:, :].unsqueeze(1).broadcast_to([128, 4, 128]), ALU.mult, [pscb, cx.constb], [CBmB])
        tt(S, "dve", xdt[:, :].rearrange("p (h d) -> p h d", h=32), xc[:, :].rearrange("p (h d) -> p h d", h=32),
           dt_tm[:, c, :].unsqueeze(2).broadcast_to([128, 32, 64]), ALU.mult, [xcb, dtB], [xdtB])
        tt(S, "pool", xdts[:, :].rearrange("p (h d) -> p h d", h=32), xdt[:, :].rearrange("p (h d) -> p h d", h=32),
           dte[:].unsqueeze(2).broadcast_to([128, 32, 64]), ALU.mult, [xdtB, smB], [xdtsB])
        def stage1(g, c=c, cs=cs):
            M, MB = Mt[g % 2], MtB[g % 2]
            for half in range(2):
                psA, psAb = PS[6 + half]
                h0 = g * 8 + half * 4
                for hh in range(4):
                    h = h0 + hh
                    mmg(S, psA[:, hh * 128:(hh + 1) * 128],
                        [(adt_tm[:, c, h:h + 1].broadcast_to([128, 128]), cx.tri_f[:, :])], [dtB, cx.constb], [psAb])
                sg_, sgB_ = seg[half], segB[half]
                tt(S, "dve", sg_, psA[:, :].rearrange("p (r l) -> p r l", r=4),
                   acum[:, h0:h0 + 4].unsqueeze(2).broadcast_to([128, 4, 128]), ALU.min, [psAb, smB], [sgB_])
                E, EB = Et[half], EtB[half]
                for hh in range(4):
                    h = h0 + hh
                    act(S, E[:, hh, :], sg_[:, hh, :], AF.Exp, [sgB_, smB], [EB], bias=nacum[:, h:h + 1])
                tt(S, "pool", M[:, half * 4:(half + 1) * 4, :], E, CBm[:, g:g + 1, :].broadcast_to([128, 4, 128]),
                   ALU.mult, [EB, CBmB], [MB])

        def stage2(g, c=c, cs=cs, xc=xc, xcb=xcb, zc=zc, zcb=zcb):
            M, MB = Mt[g % 2], MtB[g % 2]
            psy, psyb = PS[g % 2]
            for r in range(8):
                h = g * 8 + r
                mmg(S, psy[:, r * 64:(r + 1) * 64], [(M[:, r, :], xdt[:, h * 64:(h + 1) * 64])], [MB, xdtB], [psyb])
            pso, psob = PS[2]
            mmg(S, pso[:, :], [(CT[:, g, cs], pbf[:, g, :])], [CTb[g], pbfB[g]], [psob])
            pss, pssb = PS[3]
            mmg(S, pss[:, :], [(Btm[:, c, g, :], xdts[:, g * 512:(g + 1) * 512])], [Btmb, xdtsB], [pssb])
            gsl = slice(g * 512, (g + 1) * 512)
            t1, t2, yv = t1s[g % 2], t2s[g % 2], yvs[g % 2]
            t1B, t2B, yvB, ssB = t1Bs[g % 2], t2Bs[g % 2], yvBs[g % 2], ssBs[g % 2]
            tt(S, "pool", t2.rearrange("p (r d) -> p r d", r=8), xc[:, gsl].rearrange("p (r d) -> p r d", r=8),
               dsk[:, g * 8:(g + 1) * 8].unsqueeze(2).broadcast_to([128, 8, 64]), ALU.mult, [xcb, pb3], [t2B])
            tt(S, "dve", t1.rearrange("p (r d) -> p r d", r=8), pso[:, :].rearrange("p (r d) -> p r d", r=8),
               ea[:, g * 8:(g + 1) * 8].unsqueeze(2).broadcast_to([128, 8, 64]), ALU.mult, [psob, smB], [t1B])
            tt(S, "dve", t1, t1, t2, ALU.add, [t1B, t2B], [t1B])
            tt(S, "dve", yv, psy[:, :], t1, ALU.add, [psyb, t1B], [yvB])
            tt(S, "dve", yv, yv, zc[:, gsl], ALU.mult, [yvB, zcb], [yvB])

            def tail(g=g, yv=yv, yvB=yvB, ssB=ssB, gsl=gsl):
                act(S, junk, yv, AF.Square, [yvB], [junkB, ssB], scale=512.0 ** -0.5, accum_out=ss[:, g:g + 1])
                act(S, rr[:, g:g + 1], ss[:, g:g + 1], AF.Ln, [ssB], [ssB], bias=EPS)
                act(S, rr[:, g:g + 1], rr[:, g:g + 1], AF.Exp, [ssB], [ssB], scale=-0.5)
                ts(S, "dve", ybt[:, gsl], yv, rr[:, g:g + 1], None, ALU.mult, None, [yvB, ssB], [ybB])
            p4tail.append(tail)
            tt(S, "dve", prevv[:, g, :].rearrange("p (r d) -> p r d", r=8), prevv[:, g, :].rearrange("p (r d) -> p r d", r=8),
               cdb[:, g * 8:(g + 1) * 8].unsqueeze(2).broadcast_to([128, 8, 64]), ALU.mult, [prevB[g], smB], [prevB[g]])
            tt(S, "dve", prevv[:, g, :], prevv[:, g, :], pss[:, :], ALU.add, [prevB[g], pssb], [prevB[g]])
            cp(S, "pool", pbf[:, g, :], prevv[:, g, :], [prevB[g]], [pbfB[g]])

        p4tail = []
        stage1(0)
        for g in range(4):
            if g < 3:
                stage1(g + 1)
            stage2(g)
            while len(p4tail) > 1:
                p4tail.pop(0)()
        while p4tail:
            p4tail.pop(0)()
        S.dma(yb_tm[c], ybt[:, :], reads=[ybB], writes=[yb_tmb])

    if STOP == "a4":
        return
    gaT = cx.R2[:, 0:32768].rearrange("p (k t) -> p k t", k=16)
    gaB = [Buf("ga%d" % k) for k in range(16)]
    r2users = BTb + CTb + [Btmb] + [J[i][1] for i in range(4)]
    fence(cx, r2users, gaB)
    fence(cx, p4sub, p4whole)
    f0, f0b = F[0]
    f1, f1b = F[1]
    for ft in range(16):
        wt, wb = load_w(cx, ft, T["l0_wuz"][ft], 4096)
        wv = wt[:, 0:4096].rearrange("p (a k c) -> p a k c", a=2, k=16)
        for tc in range(4):
            tsl = slice(tc * 512, (tc + 1) * 512)
            psu, psub = PS[(2 * tc) % 8]
            psz, pszb = PS[(2 * tc + 1) % 8]
            mmg(S, psu[:, :], [(wv[:, 0, kt, :], hT[:, kt, tsl]) for kt in range(16)], [wb] + hTb, [psub])
            mmg(S, psz[:, :], [(wv[:, 1, kt, :], hT[:, kt, tsl]) for kt in range(16)], [wb] + hTb, [pszb])
            act(S, f0[:, tsl], psu[:, :], AF.Gelu_apprx_tanh, [psub], [f0b])
            act(S, f1[:, tsl], psz[:, :], AF.Silu, [pszb], [f1b])
            tt(S, "dve", gaT[:, ft, tsl], f0[:, tsl], f1[:, tsl], ALU.mult, [f0b, f1b], [gaB[ft]])
    def gv_out(blk, t, ps, pb):
        stg, stgb = H[(t // 4) % 2]
        sl = stg[:, (t % 4) * 512:(t % 4 + 1) * 512]
        act(S, sl, ps[:, :], AF.Gelu_apprx_tanh, [pb], [stgb])
        S.dma(gv_tm[t][:, blk * 512:(blk + 1) * 512], sl, reads=[stgb], writes=[gv_tmb])
    tm_proj512(cx, T["l0_wv"], gv_out)

    if STOP == "a5":
        return
    wst_t, wstb = W[0]
    WsT = wst_t[:, 0:2048].rearrange("p (g t) -> p g t", g=16)
    S.dma(wst_t[:, 0:2048], T["l0_wsT"], writes=[wstb], eng="pool")
    tt(S, "pool", WsT, WsT, cx.maskT[:, :].unsqueeze(1).broadcast_to([128, 16, 128]), ALU.mult, [wstb, cx.constb], [wstb])
    sbB = W[1][1]
    sbhi = W[1][0][0:1, 0:2048]; sblo = W[1][0][0:1, 2048:4096]
    sbf = F[2][0][0:1, 0:2048]; sbt = F[1][0][0:1, 0:2048]
    S.dma(sbf, T["l0_sb"], writes=[F[2][1]])
    cp(S, "dve", sbhi, sbf, [F[2][1]], [sbB])
    cp(S, "dve", sbt, sbhi, [sbB], [F[1][1]])
    tt(S, "dve", sblo, sbf, sbt, ALU.subtract, [F[2][1], F[1][1]], [sbB])
    lng, lngb = F[0]
    lnb, lnbb = F[1]
    S.dma(lng[:, 0:2048], bc(T["l0_lng"], [128, 2048]), writes=[lngb])
    S.dma(lnb[:, 0:2048], bc(T["l0_lnb"], [128, 2048]), writes=[lnbb])
    tmp, tmpb = F[2]
    st6 = sm("st6", [128, 24]); mv = sm("mv", [128, 2]); lnr = sm("lnr", [128, 1]); nmr = sm("nmr", [128, 1])
    lnB = Buf("lnsmall")
    vns = [H[2], (W[0][0][:, 2048:4096], Buf("vn1"))]
    fence(cx, [W[0][1]], [vns[1][1]])

    def ln_chunk(c):
        gv, gvb = H[c % 2]
        S.dma(gv[:, :], gv_tm[c], reads=[gv_tmb], writes=[gvb])
        for q in range(4):
            S.op("dve", lambda e, q=q, gv=gv: e.bn_stats(out=st6[:, q * 6:(q + 1) * 6], in_=gv[:, q * 512:(q + 1) * 512]),
                 [gvb], [lnB])
        S.op("dve", lambda e: e.bn_aggr(out=mv[:], in_=st6[:]), [lnB], [lnB])
        act(S, lnr[:], mv[:, 1:2], AF.Sqrt, [lnB], [lnB], bias=EPS)
        recip(S, lnr[:], lnr[:], [lnB], [lnB])
        ts(S, "dve", nmr[:], mv[:, 0:1], lnr[:, 0:1], -1.0, ALU.mult, ALU.mult, [lnB], [lnB])
        act(S, tmp[:, 0:2048], gv[:, :], AF.Identity, [gvb, lnB], [tmpb], scale=lnr[:, 0:1], bias=nmr[:, 0:1])
        tt(S, "dve", tmp[:, 0:2048], tmp[:, 0:2048], lng[:, 0:2048], ALU.mult, [tmpb, lngb], [tmpb])
        vn, vnb = vns[c % 2]
        tt(S, "pool", vn[:, :], tmp[:, 0:2048], lnb[:, 0:2048], ALU.add, [tmpb, lnbb], [vnb])

    ln_chunk(0)
    for c in range(16):
        cs = slice(c * 128, (c + 1) * 128)
        if c < 15:
            ln_chunk(c + 1)
        vn, vnb = vns[c % 2]
        for gq in range(4):
            ps, pb = PS[(c * 4 + gq) % 8]
            for gi in range(4):
                g = gq * 4 + gi
                gl = slice(g * 128, (g + 1) * 128)
                mmg(S, ps[:, gi * 128:(gi + 1) * 128],
                    [(vn[:, gl], WsT[:, g, :]), (cx.ones_row[0:1, :], sbhi[0:1, gl]), (cx.ones_row[0:1, :], sblo[0:1, gl])],
                    [vnb, wstb, sbB, cx.constb], [pb])
            tt(S, "dve", gaT[:, gq * 4:(gq + 1) * 4, cs], ps[:, :].rearrange("p (g t) -> p g t", g=4),
               gaT[:, gq * 4:(gq + 1) * 4, cs], ALU.mult, [pb] + gaB[gq * 4:(gq + 1) * 4], gaB[gq * 4:(gq + 1) * 4])
    fence(cx, [vns[1][1]], [W[0][1]])

    if STOP == "a6":
        return
    ybT = cx.hT
    for c in range(16):
        cs = slice(c * 128, (c + 1) * 128)
        yb, ybb = H[c % 2]
        S.dma(yb[:, :], yb_tm[c], reads=[yb_tmb], writes=[ybb])
        for half in range(2):
            ps, pb = PS[4 + half]
            psv = ps[:, :].bitcast(BF16)
            for kk in range(8):
                kt = half * 8 + kk
                transp(S, psv[:, kk * 128:(kk + 1) * 128], yb[:, kt * 128:(kt + 1) * 128], cx.ident[:, :], [ybb, cx.constb], [pb])
            tt(S, "dve", ybT[:, half * 8:(half + 1) * 8, cs], psv[:, 0:1024].rearrange("p (k t) -> p k t", k=8),
               ssg[:, half * 8:(half + 1) * 8].unsqueeze(2).broadcast_to([128, 8, 128]), ALU.mult,
               [pb, ssgb], hTb[half * 8:(half + 1) * 8])
    for dt_ in range(16):
        wt, wb = load_w(cx, dt_, T["l0_wout"][dt_], 4096)
        wv = wt[:, 0:4096].rearrange("p (k c) -> p k c", k=32)
        if dt_ == 0:
            S.dma(F[0][0][:, 0:L], T["xT"][0], writes=[F[0][1]])
        if dt_ < 15:
            S.dma(F[(dt_ + 1) % 2][0][:, 0:L], T["xT"][dt_ + 1], writes=[F[(dt_ + 1) % 2][1]])
        xt, xb = F[dt_ % 2]
        ot, otb = F[2]
        for tc in range(4):
            tsl = slice(tc * 512, (tc + 1) * 512)
            ps, pb = PS[(dt_ * 4 + tc) % 4]
            pairs = [(wv[:, kt, :], gaT[:, kt, tsl]) for kt in range(16)] + [(wv[:, 16 + kt, :], ybT[:, kt, tsl]) for kt in range(16)]
            mmg(S, ps[:, :], pairs, [wb] + gaB + hTb, [pb])
            tt(S, "dve", ot[:, tsl], ps[:, :], xt[:, tsl], ALU.add, [pb, xb], [otb])
        S.dma(T["x1T"][dt_], ot[:, 0:L], reads=[otb], writes=[cx.x1Tb])
        if cx.fuse_stats:
            act(S, xt[:, 0:L], ot[:, 0:L], AF.Square, [otb, xb], [xb])
            for tc in range(4):
                ps, pb = PS[4 + tc]
                S.op("pe", lambda e, ps=ps, xt=xt, tc=tc, dt_=dt_: e.matmul(
                    ps[:, :], lhsT=cx.ones_f[:, :], rhs=xt[:, tc * 512:(tc + 1) * 512], start=(dt_ == 0), stop=(dt_ == 15)),
                    [xb, cx.constb], [pb])
    if cx.fuse_stats:
        rs, rsb = F[2]
        for tc in range(4):
            ps, pb = PS[4 + tc]
            act(S, rs[:, tc * 512:(tc + 1) * 512], ps[:, :], AF.Sqrt, [pb], [rsb], scale=1.0 / D, bias=EPS)
        recip(S, rs[:, 0:L], rs[:, 0:L], [rsb], [rsb])
    cx.r2all = gaB + r2users + [junkB]


def emit_l1(cx, T):
    S, F, H, PS, W = cx.S, cx.F, cx.H, cx.PS, cx.W
    hT, hTb = cx.hT, cx.hTb
    sm = cx.small
    x1T = T["x1T"]

    g1 = sm("g1", [128, 16]); gf = sm("gf", [128, 16]); g1b = Buf("g1")
    S.dma(g1[:], T["l1_g"], writes=[g1b])
    S.dma(gf[:], T["fin_g"], writes=[g1b])
    lq = sm("lq", [128, 256]); lamB = Buf("lam")
    for i, n in enumerate(["l1_lq1", "l1_lk1", "l1_lq2", "l1_lk2"]):
        S.dma(lq[:, i * 64:(i + 1) * 64], bc(T[n], [128, 64]), writes=[lamB])
    lp = sm("lp", [128, 128]); l12 = sm("l12", [128, 2]); nlam = sm("nlam", [128, 1])
    tt(S, "dve", lp[:, 0:64], lq[:, 0:64], lq[:, 64:128], ALU.mult, [lamB], [lamB])
    tt(S, "dve", lp[:, 64:128], lq[:, 128:192], lq[:, 192:256], ALU.mult, [lamB], [lamB])
    S.op("dve", lambda e: e.tensor_reduce(out=l12[:], in_=lp[:].rearrange("p (a d) -> p a d", a=2), axis=AX.X, op=ALU.add),
         [lamB], [lamB])
    act(S, l12[:], l12[:], AF.Exp, [lamB], [lamB])
    tt(S, "dve", nlam[:], l12[:, 1:2], l12[:, 0:1], ALU.subtract, [lamB], [lamB])
    ts(S, "dve", nlam[:], nlam[:], -LAMBDA_INIT, None, ALU.add, None, [lamB], [lamB])
    if STOP == "s1":
        return
    gsub = sm("gsub", [128, 128]); gsB = Buf("gsub")
    S.dma(gsub[:], bc(T["l1_subg"], [128, 128]), writes=[gsB])
    ts(S, "dve", gsub[:], gsub[:], 1.0 - LAMBDA_INIT, None, ALU.mult, None, [gsB], [gsB])
    tab = sm("tab", [128, 256]); tabB = Buf("tab")
    S.dma(tab[:], T["c_tab"], writes=[tabB])
    if STOP == "s2":
        return
    nmask = cx.smallbf("nmask", [128, 128])
    cB = Buf("l1const")
    S.dma(nmask[:], T["c_negmask"], writes=[cB], eng="pool")
    sel = cx.smallbf("sel", [128, 256])
    S.dma(sel[:], T["c_sel"], writes=[cB], eng="pool")

    if STOP == "s3":
        return
    emit_rmsnorm_fm(cx, x1T, g1, g1b, "l1", have_rs=cx.fuse_stats, xdep=[cx.x1Tb])

    if STOP == "p1":
        return
    qT_d, kT_d, sgT_d = T["qT_d"], T["kT_d"], T["sgT_d"]
    qkB = Buf("qk_d"); sgdB = Buf("sg_d")
    n2 = sm("n2", [128, 256]); n2B = Buf("n2")
    Vt = cx.R2[:, 0:33280].rearrange("p (t h v) -> p t h v", t=16, h=16)
    VB = [Buf("V%d" % t) for t in range(16)]
    memset(S, "pool", cx.R2[:, 0:33280], 1.0, VB + cx.r2all)
    f2v_ = F[2][0][:, :].bitcast(BF16)
    SQ = [H[2], (f2v_[:, 0:2048], F[2][1])]
    qktail = []
    for h in range(16):
        wt, wb = load_w(cx, h, T["l1_wqk"][h], 4096)
        wv = wt[:, 0:4096].rearrange("p (a k c) -> p a k c", a=2, k=16)
        for a in range(2):
            stg, stgb = H[a]
            for tc in range(4):
                tsl = slice(tc * 512, (tc + 1) * 512)
                ps, pb = PS[(a * 4 + tc) % 4]
                mmg(S, ps[:, :], [(wv[:, a, kt, :], hT[:, kt, tsl]) for kt in range(16)], [wb] + hTb, [pb])
                if tc % 2 == 0:
                    ts(S, "dve", stg[:, tsl], ps[:, :], 0.125 if a == 0 else 1.0, None, ALU.mult, None, [pb], [stgb])
                else:
                    act(S, stg[:, tsl], ps[:, :], AF.Copy, [pb], [stgb], scale=0.125 if a == 0 else 1.0)
            while qktail:
                qktail.pop(0)()
            sq, sqB = SQ[(h * 2 + a) % 2]
            act(S, sq[:, :], stg[:, :], AF.Square, [stgb], [sqB])
            S.dma((qT_d if a == 0 else kT_d)[h], stg[:, :], reads=[stgb], writes=[qkB])

            def tail(h=h, a=a, sq=sq, sqB=sqB):
                for m in range(2):
                    for tc in range(4):
                        pn, pnb = PS[4 + tc]
                        mmg(S, pn[:, :], [(sel[:, m * 128:(m + 1) * 128], sq[:, tc * 512:(tc + 1) * 512])], [sqB, cB], [pnb])
                        col = ((h * 2 + a) * 2 + m) * 4 + tc
                        S.op("dve", lambda e, pn=pn, col=col: e.tensor_reduce(out=n2[:, col:col + 1], in_=pn[:, :], axis=AX.X, op=ALU.max),
                             [pnb], [n2B])
            qktail.append(tail)
    while qktail:
        qktail.pop(0)()
    if STOP == "p2a":
        return
    n2m = sm("n2m", [128, 64]); Bhm = sm("Bhm", [128, 32])
    S.op("dve", lambda e: e.tensor_reduce(out=n2m[:], in_=n2[:].rearrange("p (x t) -> p x t", t=4), axis=AX.X, op=ALU.max),
         [n2B], [n2B])
    n2v = n2m[:].rearrange("p (h a m) -> p h a m", h=16, a=2)
    tt(S, "dve", Bhm[:].rearrange("p (h m) -> p h m", h=16), n2v[:, :, 0, :], n2v[:, :, 1, :], ALU.mult, [n2B], [n2B])
    act(S, Bhm[:], Bhm[:], AF.Sqrt, [n2B], [n2B])
    if STOP == "p2b":
        return
    def v_out(blk, t, ps, pb):
        cp(S, "act", Vt[:, t, 4 * blk:4 * blk + 4, 0:128], ps[:, :].rearrange("p (h v) -> p h v", h=4), [pb], [VB[t]])
    tm_proj512(cx, T["l1_wv"], v_out)
    if STOP == "p2c":
        return
    for ft in range(16):
        wt, wb = load_w(cx, ft, T["l1_wg"][ft], 2048)
        wv = wt[:, 0:2048].rearrange("p (k c) -> p k c", k=16)
        stg, stgb = H[ft % 2]
        for tc in range(4):
            tsl = slice(tc * 512, (tc + 1) * 512)
            ps, pb = PS[tc]
            mmg(S, ps[:, :], [(wv[:, kt, :], hT[:, kt, tsl]) for kt in range(16)], [wb] + hTb, [pb])
            act(S, stg[:, tsl], ps[:, :], AF.Silu, [pb], [stgb])
        S.dma(sgT_d[ft], stg[:, :], reads=[stgb], writes=[sgdB])

    if STOP == "p2":
        return
    oT = cx.hT
    f1v = F[1][0][:, :].bitcast(BF16)
    f2v = F[2][0][:, :].bitcast(BF16)
    f1a, f1b_, f2a, f2b_ = Buf("f1a"), Buf("f1b"), Buf("f2a"), Buf("f2b")
    w0a, w0b, w1a = Buf("w0a"), Buf("w0b"), Buf("w1a")
    QK8 = [[(H[0][0][:, :], H[0][1]), (H[1][0][:, :], H[1][1]), (H[2][0][:, :], H[2][1]), (W[0][0][:, 0:2048], w0a)],
           [(f1v[:, 0:2048], f1a), (f1v[:, 2048:4096], f1b_), (f2v[:, 0:2048], f2a), (f2v[:, 2048:4096], f2b_)]]
    SG2 = [(W[0][0][:, 2048:4096], w0b), (W[1][0][:, 0:2048], w1a)]
    PTt = [W[1][0][:, 2048 + i * 512:2048 + (i + 1) * 512] for i in range(4)]
    PTB = [Buf("PT%d" % i) for i in range(4)]
    attsub = [w0a, w0b, w1a, f1a, f1b_, f2a, f2b_] + PTB
    biasB2 = sm("biasB", [128, 64]); bbB = [Buf("biasB0"), Buf("biasB1")]
    f0, f0b = F[0]
    t2 = f0[:, 0:512]
    ovs = [f0[:, 512:1024], f0[:, 1024:1536]]
    junk = f0[:, 1536:2048]
    onbf = cx.smallbf("onbf", [128, 512]); onB = Buf("onbf")
    t2B, junkB = Buf("t2"), Buf("junk")
    ovB = [Buf("ov0"), Buf("ov1")]
    attsub += [t2B, junkB] + ovB
    attwhole = [W[0][1], W[1][1], f0b, F[1][1], F[2][1]]
    fence(cx, attwhole, attsub)
    for par in range(2):
        for t_i, (tl, tb) in enumerate(QK8[par]):
            memset(S, "pool", tl[64:128, :], 0.0, [tb])
            if t_i >= 2:
                memset(S, "pool", tl[64:65, :], 1.0, [tb])
    rsum = sm("rsum", [128, 8]); rsB = Buf("rsum")
    msq = sm("msq", [128, 4]); msB = Buf("msq")
    v3 = lambda ap, n: ap.rearrange("p (b c) -> p b c", b=n)

    iters = [(h, I, m, j) for h in range(16) for I in range(4) for m in range(2) for j in range(4 * I + 4)]
    NIT = len(iters)
    LA = 3
    meta = [None] * NIT
    hstate = {}

    def head_setup(h):
        par = h % 2
        (QA, qab), (QB, qbb), (KA, kab), (KB, kbb) = QK8[par]
        sg, sgb = SG2[par]
        S.dma(QA[0:64, :], qT_d[h][0:64, :], reads=[qkB], writes=[qab])
        S.dma(QB[0:64, :], qT_d[h][64:128, :], reads=[qkB], writes=[qbb])
        S.dma(KA[0:64, :], kT_d[h][0:64, :], reads=[qkB], writes=[kab])
        S.dma(KB[0:64, :], kT_d[h][64:128, :], reads=[qkB], writes=[kbb])
        S.dma(sg[:, :], sgT_d[h], reads=[sgdB], writes=[sgb])
        S.dma(QA[64:65, :], T["c_shrow"][h:h + 1, :], writes=[qab], eng="pool")
        S.dma(QB[64:65, :], T["c_shrow"][h:h + 1, :], writes=[qbb], eng="pool")
        bias = biasB2[:, par * 32:(par + 1) * 32]
        for m in range(2):
            ts(S, "dve", bias[:, m * 16:(m + 1) * 16], tab[:, h * 16:(h + 1) * 16], Bhm[:, h * 2 + m:h * 2 + m + 1], None,
               ALU.subtract, None, [tabB, n2B], [bbB[par]])
        hstate[h] = ([(QA, qab), (QB, qbb)], [(KA, kab), (KB, kbb)], sg, sgb, bias)

    def emit_qk(n):
        h, I, m, j = iters[n]
        if h not in hstate:
            head_setup(h)
        Qs, Ks, sg, sgb, bias = hstate[h]
        qT, qb = Qs[m]
        kT, kb = Ks[m]
        qs = I * 512
        b0 = max(0, j - 4 * I)
        c0 = b0 * 128
        pss, pssb = PS[4 + n % 4]
        PT, PTb = PTt[n % 4], PTB[n % 4]
        meta[n] = (PT, PTb, b0)
        diag = j >= 4 * I
        l0, r0 = kT[:, j * 128:(j + 1) * 128], qT[:, qs + c0:qs + 512]

        def fn(e):
            ins = e.matmul(pss[:, c0:512], lhsT=l0, rhs=r0, start=True, stop=not diag)
            if diag:
                ins = e.matmul(pss[:, c0:c0 + 128], lhsT=cx.ident[:, :], rhs=nmask[:, :], start=False, stop=True)
            return ins
        S.op("pe", fn, [kb, qb, cB, cx.constb], [pssb])
        off = j - 4 * I + 12
        act(S, PT[:, c0:512], pss[:, c0:512], AF.Exp, [pssb, bbB[h % 2]], [PTb], bias=bias[:, m * 16 + off:m * 16 + off + 1])

    started = set()
    pendingB = []

    def epiA(h, I, ep):
        ov, ovb = ovs[ep % 2], ovB[ep % 2]
        for bank in range(4):
            pst_, pstb_ = PS[bank]
            S.op("dve", lambda e, bank=bank, pst_=pst_, rsum=rsum: e.reciprocal(
                out=rsum[:, 2 * bank:2 * bank + 2], in_=v3(pst_[:, :], 2)[:, :, 128]), [pstb_], [rsB])
        tt(S, "dve", rsum[:, 4:8], rsum[:, 4:8], nlam[:, 0:1].broadcast_to([128, 4]), ALU.mult, [rsB, lamB], [rsB])
        for bb in range(2):
            tt(S, "dve", v3(t2, 4)[:, 2 * bb:2 * bb + 2, :], v3(PS[2 + bb][0][:, :], 2)[:, :, 0:128],
               rsum[:, 4 + 2 * bb:6 + 2 * bb].unsqueeze(2).broadcast_to([128, 2, 128]), ALU.mult, [PS[2 + bb][1], rsB], [t2B])
            tt(S, "dve", v3(ov, 4)[:, 2 * bb:2 * bb + 2, :], v3(PS[bb][0][:, :], 2)[:, :, 0:128],
               rsum[:, 2 * bb:2 + 2 * bb].unsqueeze(2).broadcast_to([128, 2, 128]), ALU.mult, [PS[bb][1], rsB], [ovb])
        tt(S, "pool", ov, ov, t2, ALU.add, [ovb, t2B], [ovb])

    def epiB(h, I, ep):
        ov, ovb = ovs[ep % 2], ovB[ep % 2]
        sg, sgb = SG2[h % 2]
        qs = I * 512
        tt(S, "pool", junk, ov, ov, ALU.mult, [ovb], [junkB])
        S.op("dve", lambda e, msq=msq, jv=v3(junk, 4): e.tensor_reduce(out=msq[:], in_=jv, axis=AX.X, op=ALU.add), [junkB], [msB])
        act(S, msq[:], msq[:], AF.Ln, [msB], [msB], scale=1.0 / 128, bias=EPS)
        act(S, msq[:], msq[:], AF.Exp, [msB], [msB], scale=-0.5)
        tt(S, "dve", v3(ov, 4), v3(ov, 4), msq[:].unsqueeze(2).broadcast_to([128, 4, 128]), ALU.mult, [ovb, msB], [ovb])
        tt(S, "dve", v3(onbf[:, :], 4), v3(ov, 4), gsub[:, :].unsqueeze(1).broadcast_to([128, 4, 128]), ALU.mult, [ovb, gsB], [onB])
        pst, pstb = PS[7]
        pstv = pst[:, :].bitcast(BF16)
        for b_ in range(4):
            transp(S, pstv[:, b_ * 128:(b_ + 1) * 128], onbf[:, b_ * 128:(b_ + 1) * 128], cx.ident[:, :], [onB, cx.constb], [pstb])
        tt(S, "dve", oT[:, h, qs:qs + 512], pstv[:, 0:512], sg[:, qs:qs + 512], ALU.mult, [pstb, sgb], [hTb[h]])

    def emit_pv(n):
        h, I, m, j = iters[n]
        PT, PTb, b0 = meta[n]
        if m == 0 and j == 0:
            started.clear()
        for b_ in range(b0, 4):
            bank = m * 2 + b_ // 2
            acc = PS[bank][0][:, (b_ % 2) * 256:(b_ % 2) * 256 + 129]
            st_ = bank not in started
            started.add(bank)
            S.op("pe", lambda e, acc=acc, PT=PT, b_=b_, j=j, h=h, st_=st_, last=(j == 4 * I + b_): e.matmul(
                acc, lhsT=PT[:, b_ * 128:(b_ + 1) * 128], rhs=Vt[:, j, h, 0:129], start=st_, stop=last, skip_group_check=True),
                [PTb, VB[j]], [PS[bank][1]])
        ep = h * 4 + I
        if m == 1 and j == 4 * I + 3:
            epiA(h, I, ep)
            pendingB.append((h, I, ep))
        elif m == 0 and j == min(2, 4 * I + 3) and pendingB:
            epiB(*pendingB.pop(0))
        if I == 0 and m == 0 and j == 2 and h + 1 < 16 and (h + 1) not in hstate:
            head_setup(h + 1)

    for n in range(NIT + LA):
        if n < NIT:
            emit_qk(n)
        if n - LA >= 0:
            emit_pv(n - LA)
    while pendingB:
        epiB(*pendingB.pop(0))

    if STOP == "p3":
        return
    fence(cx, attsub, attwhole)
    x2T = T["x2T"]; x2B = Buf("x2T")
    r2f = cx.R2[:, :].bitcast(F32)
    NRES = 8
    slots = [(r2f[:, i * 2048:(i + 1) * 2048], Buf("x2s%d" % i)) for i in range(NRES)]
    fence(cx, VB, [b_ for _, b_ in slots])

    def ld_x1(dt_):
        xt_, xb_ = F[dt_ % 2]
        S.dma(xt_[:, 0:L], x1T[dt_], reads=[cx.x1Tb], writes=[xb_])
    ld_x1(0)
    for dt_ in range(16):
        wt, wb = load_w(cx, dt_, T["l1_wout"][dt_], 2048)
        wv = wt[:, 0:2048].rearrange("p (k c) -> p k c", k=16)
        if dt_ < 15:
            ld_x1(dt_ + 1)
        xt, xb = F[dt_ % 2]
        ot, otb = slots[dt_] if dt_ < NRES else F[2]
        for tc in range(4):
            tsl = slice(tc * 512, (tc + 1) * 512)
            ps, pb = PS[tc]
            mmg(S, ps[:, :], [(wv[:, kt, :], oT[:, kt, tsl]) for kt in range(16)], [wb] + hTb, [pb])
            tt(S, "dve", ot[:, tsl], ps[:, :], xt[:, tsl], ALU.add, [pb, xb], [otb])
        if dt_ >= NRES:
            S.dma(x2T[dt_], ot[:, 0:L], reads=[otb], writes=[x2B])
        act(S, xt[:, 0:L], ot[:, 0:L], AF.Square, [otb, xb], [xb])
        for tc in range(4):
            ps, pb = PS[4 + tc]
            S.op("pe", lambda e, ps=ps, xt=xt, tc=tc, dt_=dt_: e.matmul(
                ps[:, :], lhsT=cx.ones_f[:, :], rhs=xt[:, tc * 512:(tc + 1) * 512], start=(dt_ == 0), stop=(dt_ == 15)),
                [xb, cx.constb], [pb])
    rs, rsb = F[2]
    for tc in range(4):
        ps, pb = PS[4 + tc]
        act(S, rs[:, tc * 512:(tc + 1) * 512], ps[:, :], AF.Sqrt, [pb], [rsb], scale=1.0 / D, bias=EPS)
    recip(S, rs[:, 0:L], rs[:, 0:L], [rsb], [rsb])

    def ld_x2(dt_):
        xt_, xb_ = F[dt_ % 2]
        S.dma(xt_[:, 0:L], x2T[dt_], reads=[x2B], writes=[xb_])
    ld_x2(NRES)
    ld_x2(NRES + 1)
    for dt_ in range(NRES):
        sl, slb = slots[dt_]
        stt(S, sl, sl, gf[:, dt_:dt_ + 1], rs[:, 0:L], ALU.mult, ALU.mult, [slb, rsb, g1b], [slb])
        S.dma(T["outT"][dt_], sl, reads=[slb])
    for dt_ in range(NRES, 16):
        xt, xb = F[dt_ % 2]
        stt(S, xt[:, 0:L], xt[:, 0:L], gf[:, dt_:dt_ + 1], rs[:, 0:L], ALU.mult, ALU.mult, [xb, rsb, g1b], [xb])
        S.dma(T["outT"][dt_], xt[:, 0:L], reads=[xb])
        if dt_ + 2 < 16:
            ld_x2(dt_ + 2)


L0_IN = {
    "xT": [16, 128, 2048], "l0_g": [128, 16], "l0_cw": [128, 96], "l0_cb": [128, 24], "l0_dtb": [1, 32], "l0_alog": [1, 32],
    "l0_dsk": [1, 32], "l0_ssg": [128, 16], "l0_wxbc": [24, 128, 2048], "l0_wzb": [4, 128, 8192], "l0_wdt": [128, 512],
    "l0_wuz": [16, 128, 4096], "l0_wv": [4, 128, 8192], "l0_wsT": [128, 2048], "l0_sb": [1, 2048], "l0_lng": [1, 2048],
    "l0_lnb": [1, 2048], "l0_wout": [16, 128, 4096],
}
L1_IN = {
    "l1_g": [128, 16], "fin_g": [128, 16], "l1_lq1": [1, 64], "l1_lk1": [1, 64], "l1_lq2": [1, 64], "l1_lk2": [1, 64],
    "l1_subg": [1, 128], "c_tab": [128, 256], "c_shrow": [16, 2048], "c_negmask": [128, 128],
    "c_sel": [128, 256], "l1_wqk": [16, 128, 4096], "l1_wv": [4, 128, 8192], "l1_wg": [16, 128, 2048], "l1_wout": [16, 128, 2048],
}
C_IN = {"c_ident": [128, 128], "c_maskT": [128, 128]}
L0_SCR = {"x_tm": ([16, 128, 2048], BF16), "szb_tm": ([16, 128, 2048], BF16), "yb_tm": ([16, 128, 2048], BF16),
          "gv_tm": ([16, 128, 2048], BF16)}
L1_SCR = {"qT_d": ([16, 128, 2048], BF16), "kT_d": ([16, 128, 2048], BF16), "sgT_d": ([16, 128, 2048], BF16),
          "x2T": ([16, 128, 2048], F32)}


def build_program(mode):
    nc = bass.Bass("TRN2", target_bir_lowering=False)
    T = {}
    ins = dict(C_IN)
    if mode in ("l0", "fused"):
        ins.update(L0_IN)
    if mode in ("l1", "fused"):
        ins.update(L1_IN)
    for n, shp in ins.items():
        T[n] = nc.dram_tensor(n, shp, F32, kind="ExternalInput").ap()
    scr = {}
    if mode in ("l0", "fused"):
        scr.update(L0_SCR)
    if mode in ("l1", "fused"):
        scr.update(L1_SCR)
    for n, (shp, dt) in scr.items():
        if DBG == "smallscr" and n in ("kT_d", "sgT_d", "x2T"):
            T[n] = T["qT_d"]
            continue
        T[n] = nc.dram_tensor(n, shp, dt, kind="Internal").ap()
    if mode == "l0":
        T["x1T"] = nc.dram_tensor("x1T", [16, 128, 2048], F32, kind="ExternalOutput").ap()
    elif mode == "l1":
        T["x1T"] = nc.dram_tensor("x1T", [16, 128, 2048], F32, kind="ExternalInput").ap()
    else:
        T["x1T"] = nc.dram_tensor("x1T", [16, 128, 2048], F32, kind="Internal").ap()
    if mode in ("l1", "fused"):
        T["outT"] = nc.dram_tensor("outT", [16, 128, 2048], F32, kind="ExternalOutput").ap()

    with ExitStack() as st:
        cx = Cx()
        cx.nc = nc
        cx.S = S = Sched(nc)
        sb = lambda n, shp, dt: st.enter_context(nc.sbuf_tensor(n, shp, dt))
        cx.small = lambda n, shp: sb(n, shp, F32)
        cx.smallbf = lambda n, shp: sb(n, shp, BF16)
        R1 = sb("R1", [128, 32768], BF16)
        cx.hT = R1[:, :].rearrange("p (k t) -> p k t", k=16)
        cx.hTb = [Buf("hT%d" % k) for k in range(16)]
        cx.R2 = sb("R2", [128, 34816], BF16)
        cx.r2all = []
        cx.W = [(sb("W%d" % i, [128, 4096], BF16), Buf("W%d" % i)) for i in range(2)]
        cx.F = [(sb("F%d" % i, [128, 2056], F32), Buf("F%d" % i)) for i in range(3)]
        cx.H = [(sb("H%d" % i, [128, 2048], BF16), Buf("H%d" % i)) for i in range(3)]
        cx.J = [(cx.R2[:, 24576 + i * 2048:24576 + (i + 1) * 2048], Buf("J%d" % i)) for i in range(4)]
        cx.PS = [(st.enter_context(nc.psum_tensor("PS%d" % i, [128, 512], F32)), Buf("PS%d" % i, excl=True)) for i in range(8)]
        cx.x1Tb = Buf("x1T")
        cx.fuse_stats = (mode == "fused")
        cx.fz = sb("fz", [128, 2], F32)
        cx.fzB = Buf("fz")
        cx.constb = Buf("const")
        cx.ones_f = sb("ones_f", [128, 128], F32)
        cx.tri_f = sb("tri_f", [128, 128], F32)
        cx.ident = sb("ident", [128, 128], BF16)
        cx.maskT = sb("maskT", [128, 128], BF16)
        cx.ones_row = sb("ones_row", [1, 128], BF16)
        memset(S, "pool", cx.ones_f[:, :], 1.0, [cx.constb])
        memset(S, "pool", cx.ones_row[:, :], 1.0, [cx.constb])
        S.dma(cx.tri_f[:, :], T["c_maskT"], writes=[cx.constb])
        S.dma(cx.ident[:, :], T["c_ident"], writes=[cx.constb], eng="pool")
        S.dma(cx.maskT[:, :], T["c_maskT"], writes=[cx.constb], eng="pool")
        if mode in ("l0", "fused"):
            emit_l0(cx, T)
        if mode in ("l1", "fused"):
            emit_l1(cx, T)
        S.emit(st)
    return nc


def _fm_tiles(Wc):
    K, N = Wc.shape
    n = N // 128
    return np.ascontiguousarray(Wc.reshape(K // 128, 128, n, 128).transpose(2, 1, 0, 3)).reshape(n, 128, -1)


def _tm_blocks(Wc, bw):
    K, N = Wc.shape
    n = N // bw
    return np.ascontiguousarray(Wc.reshape(K // 128, 128, n, bw).transpose(2, 1, 0, 3)).reshape(n, 128, -1)


def _col(v, n):
    return np.ascontiguousarray(np.asarray(v, np.float32).reshape(n, 128).T)


def prep_common():
    ident = np.eye(128, dtype=np.float32)
    maskT = np.triu(np.ones((128, 128), np.float32))
    return {"c_ident": ident, "c_maskT": maskT}


def prep_l0(inp):
    f = lambda n: np.asarray(inp[n], np.float32)
    W = f("l0_w_in")
    d = {}
    d["l0_g"] = _col(f("l0_norm_g"), 16)
    cwt = f("l0_conv_w")
    d["l0_cw"] = np.ascontiguousarray(cwt.reshape(4, 24, 128).transpose(2, 1, 0)).reshape(128, 96)
    d["l0_cb"] = _col(f("l0_conv_b"), 24)
    d["l0_dtb"] = f("l0_dt_bias").reshape(1, 32)
    d["l0_alog"] = f("l0_a_log").reshape(1, 32)
    d["l0_dsk"] = f("l0_d_skip").reshape(1, 32)
    d["l0_ssg"] = _col(f("l0_ssm_norm_g"), 16)
    d["l0_wxbc"] = _fm_tiles(W[:, 8192:11264])
    d["l0_wzb"] = _tm_blocks(W[:, 6144:8192], 512)
    d["l0_wdt"] = _tm_blocks(W[:, 11264:11296], 32)[0]
    wu = _fm_tiles(W[:, 0:2048]); wz = _fm_tiles(W[:, 4096:6144])
    d["l0_wuz"] = np.ascontiguousarray(np.concatenate([wu, wz], axis=2))
    d["l0_wv"] = _tm_blocks(W[:, 2048:4096], 512)
    d["l0_wsT"] = np.ascontiguousarray(f("l0_spatial_w").transpose(2, 0, 1)).reshape(128, 2048)
    d["l0_sb"] = f("l0_spatial_b").reshape(1, 2048)
    d["l0_lng"] = f("l0_gmlp_ln_g").reshape(1, 2048)
    d["l0_lnb"] = f("l0_gmlp_ln_b").reshape(1, 2048)
    Wo = f("l0_w_out")
    d["l0_wout"] = np.ascontiguousarray(Wo.reshape(32, 128, 16, 128).transpose(2, 1, 0, 3)).reshape(16, 128, 4096)
    return d


def prep_l1(inp):
    f = lambda n: np.asarray(inp[n], np.float32)
    W = f("l1_w_in")
    d = {}
    d["l1_g"] = _col(f("l1_norm_g"), 16)
    d["fin_g"] = _col(f("final_norm_g"), 16)
    for a, b in [("l1_lq1", "l1_lambda_q1"), ("l1_lk1", "l1_lambda_k1"), ("l1_lq2", "l1_lambda_q2"), ("l1_lk2", "l1_lambda_k2")]:
        d[a] = f(b).reshape(1, 64)
    d["l1_subg"] = f("l1_subln_g").reshape(1, 128)
    slopes = (2.0 ** (-8.0 * (np.arange(16, dtype=np.float64) + 1.0) / 16)).astype(np.float32)
    kl = np.arange(128, dtype=np.float32)[:, None, None]
    off = (np.arange(16, dtype=np.float32) - 12.0)[None, None, :]
    d["c_tab"] = (slopes[None, :, None] * (kl + 128.0 * off)).astype(np.float32).reshape(128, 256)
    d["c_shrow"] = np.tile(-slopes[:, None] * np.arange(512, dtype=np.float32)[None, :], (1, 4)).astype(np.float32)
    d["c_negmask"] = np.where(np.arange(128)[:, None] > np.arange(128)[None, :], -30000.0, 0.0).astype(np.float32)
    sel = np.zeros((128, 2, 128), np.float32); sel[0:64, 0, :] = 1.0; sel[64:128, 1, :] = 1.0
    d["c_sel"] = sel.reshape(128, 256)
    wq = _fm_tiles(W[:, 0:2048]); wk = _fm_tiles(W[:, 2048:4096])
    d["l1_wqk"] = np.ascontiguousarray(np.concatenate([wq, wk], axis=2))
    d["l1_wv"] = _tm_blocks(W[:, 4096:6144], 512)
    d["l1_wg"] = _fm_tiles(W[:, 6144:8192])
    Wo = f("l1_w_out")
    d["l1_wout"] = np.ascontiguousarray(Wo.reshape(16, 128, 16, 128).transpose(2, 1, 0, 3)).reshape(16, 128, 2048)
    return d


def _xT(x):
    return [np.ascontiguousarray(x[b].T).reshape(16, 128, 2048) for b in range(x.shape[0])]


MODE = "fused"
STOP = None
DBG = None
_PROG = {}


def _prog(mode):
    if mode not in _PROG:
        _PROG[mode] = build_program(mode)
    return _PROG[mode]


def kernel(**inputs):
    x = np.asarray(inputs["x"], np.float32)
    nb = x.shape[0]
    cores = list(range(nb))
    com = prep_common()
    xT = _xT(x)
    if MODE == "fused":
        shared = dict(com); shared.update(prep_l0(inputs)); shared.update(prep_l1(inputs))
        maps = [dict(shared, xT=xT[b]) for b in range(nb)]
        res = run_bass_kernel_spmd(_prog("fused"), maps, core_ids=cores)
        outT = [r["outT"] for r in res.results]
    else:
        s0 = dict(com); s0.update(prep_l0(inputs))
        res = run_bass_kernel_spmd(_prog("l0"), [dict(s0, xT=xT[b]) for b in range(nb)], core_ids=cores)
        x1T = [np.asarray(r["x1T"]) for r in res.results]
        s1 = dict(com); s1.update(prep_l1(inputs))
        res = run_bass_kernel_spmd(_prog("l1"), [dict(s1, x1T=x1T[b]) for b in range(nb)], core_ids=cores)
        outT = [r["outT"] for r in res.results]
    out = np.stack([np.asarray(o, np.float32).reshape(2048, 2048).T for o in outT], axis=0)
    return np.ascontiguousarray(out)
```

```python
import math
from contextlib import ExitStack
import numpy as np
import concourse.bass as bass
import concourse.mybir as mybir
from concourse.bass_utils import run_bass_kernel_spmd

F32 = mybir.dt.float32
BF16 = mybir.dt.bfloat16
AF = mybir.ActivationFunctionType
ALU = mybir.AluOpType
AX = mybir.AxisListType

ENGS = ["pe", "act", "dve", "pool", "sp"]
N_DMA_SEMS = 56
N_HW_SEMS = 40
EPS = 1e-5
D = 2048
L = 2048
LAMBDA_INIT = 0.8 - 0.6 * math.exp(-0.3 * 1)


class Buf:
    __slots__ = ("name", "w", "r", "excl")

    def __init__(self, name, excl=False):
        self.name = name
        self.w = None
        self.r = []
        self.excl = excl


class Op:
    __slots__ = ("eng", "fn", "deps", "signal", "sig", "is_dma", "slot", "cnt")

    def __init__(self, eng, fn, is_dma=False):
        self.eng = eng
        self.fn = fn
        self.deps = []
        self.signal = False
        self.sig = 0
        self.is_dma = is_dma
        self.slot = -1
        self.cnt = 0


class Sched:
    def __init__(self, nc):
        self.nc = nc
        self.ops = {e: [] for e in ENGS}
        self.n_dma = 0
        self.n_sw = 0
        self.slot_last = [None] * N_DMA_SEMS
        self.slot_uses = [0] * N_DMA_SEMS

    def _track(self, o, reads, writes):
        ex = [b for b in reads if b.excl]
        if ex:
            reads = [b for b in reads if not b.excl]
            writes = list(writes) + ex
        deps = []
        for b in reads:
            if b.w is not None:
                deps.append(b.w)
        for b in writes:
            if b.w is not None:
                deps.append(b.w)
            deps.extend(b.r)
        seen = set()
        for d in deps:
            if d is o or id(d) in seen:
                continue
            seen.add(id(d))
            if d.eng == "pe" and o.eng == "pe" and not d.is_dma and not o.is_dma:
                continue
            o.deps.append(d)
            d.signal = True
        for b in reads:
            if not o.is_dma:
                b.r = [x for x in b.r if x.is_dma or x.eng != o.eng]
            b.r.append(o)
        for b in writes:
            b.w = o
            b.r = []

    def op(self, eng, fn, reads=(), writes=()):
        o = Op(eng, fn)
        self._track(o, reads, writes)
        self.ops[eng].append(o)
        return o

    def dma(self, out, in_, reads=(), writes=(), eng="sp"):
        def fn(e, out=out, in_=in_):
            return e.dma_start(out=out, in_=in_)
        o = Op(eng, fn, is_dma=True)
        if eng == "pool":
            slot = N_HW_SEMS + self.n_sw % (N_DMA_SEMS - N_HW_SEMS)
            self.n_sw += 1
        else:
            slot = self.n_dma % N_HW_SEMS
            self.n_dma += 1
        o.slot = slot
        self.slot_uses[slot] += 1
        o.cnt = 16 * self.slot_uses[slot]
        prev = self.slot_last[slot]
        self._track(o, reads, writes)
        if prev is not None and all(d is not prev for d in o.deps):
            o.deps.append(prev)
        self.slot_last[slot] = o
        o.signal = True
        self.ops[eng].append(o)
        return o

    def emit(self, stack):
        nc = self.nc
        esem = {e: stack.enter_context(nc.semaphore("s_" + e)) for e in ENGS}
        dsem = [stack.enter_context(nc.semaphore("d%d" % i)) for i in range(N_DMA_SEMS)]
        for e in ENGS:
            c = 0
            for o in self.ops[e]:
                if o.is_dma:
                    continue
                if o.signal:
                    c += 1
                    o.sig = c
        block = stack.enter_context(nc.Block())

        def run(e, eng):
            waited = {}
            for o in self.ops[e]:
                for d in o.deps:
                    if d.is_dma:
                        key, val, sem = ("d", d.slot), d.cnt, dsem[d.slot]
                    else:
                        key, val, sem = ("e", d.eng), d.sig, esem[d.eng]
                    if waited.get(key, 0) >= val:
                        continue
                    waited[key] = val
                    eng.wait_ge(sem, val)
                ins = o.fn(eng)
                if o.is_dma:
                    ins.then_inc(dsem[o.slot], 16)
                elif o.signal:
                    ins.then_inc(esem[e], 1)
            if e == "sp":
                for s in range(N_DMA_SEMS):
                    if self.slot_uses[s]:
                        eng.wait_ge(dsem[s], 16 * self.slot_uses[s])

        @block.tensor
        def _(eng):
            run("pe", eng)

        @block.scalar
        def _(eng):
            run("act", eng)

        @block.vector
        def _(eng):
            run("dve", eng)

        @block.gpsimd
        def _(eng):
            run("pool", eng)

        @block.sync
        def _(eng):
            run("sp", eng)


class Cx:
    pass


def act(S, out, in_, func, reads, writes, **kw):
    S.op("act", lambda e: e.activation(out=out, in_=in_, func=func, **kw), reads, writes)


def tt(S, eng, out, in0, in1, op, reads, writes):
    S.op(eng, lambda e: e.tensor_tensor(out=out, in0=in0, in1=in1, op=op), reads, writes)


def ts(S, eng, out, in0, s1, s2, op0, op1, reads, writes):
    if op1 is None:
        S.op(eng, lambda e: e.tensor_scalar(out=out, in0=in0, scalar1=s1, scalar2=None, op0=op0), reads, writes)
    else:
        S.op(eng, lambda e: e.tensor_scalar(out=out, in0=in0, scalar1=s1, scalar2=s2, op0=op0, op1=op1), reads, writes)


def stt(S, out, in0, scalar, in1, op0, op1, reads, writes):
    S.op("dve", lambda e: e.scalar_tensor_tensor(out=out, in0=in0, scalar=scalar, in1=in1, op0=op0, op1=op1), reads, writes)


def cp(S, eng, out, in_, reads, writes):
    if eng == "act":
        S.op("act", lambda e: e.activation(out=out, in_=in_, func=AF.Copy), reads, writes)
    else:
        S.op(eng, lambda e: e.tensor_copy(out=out, in_=in_), reads, writes)


def recip(S, out, in_, reads, writes):
    S.op("dve", lambda e: e.reciprocal(out=out, in_=in_), reads, writes)


def memset(S, eng, ap, val, writes):
    S.op(eng, lambda e: e.memset(ap, val), (), writes)


def mmg(S, out, pairs, reads, writes, start=True, stop=True):
    n = len(pairs)

    def fn(e):
        ins = None
        for i, (l, r) in enumerate(pairs):
            ins = e.matmul(out, lhsT=l, rhs=r, start=(start and i == 0), stop=(stop and i == n - 1))
        return ins
    S.op("pe", fn, reads, writes)


def transp(S, out, in_, ident, reads, writes):
    S.op("pe", lambda e: e.transpose(out, in_, ident), reads, writes)


def bc(ap, shape):
    return ap.broadcast_to(shape)


def fence(cx, olds, news):
    cx.S.op("pool", lambda e: e.memset(cx.fz[:, 0:1], 0.0), (), list(olds) + list(news) + [cx.fzB])


def emit_rmsnorm_fm(cx, xT, gcol, gB, tag, have_rs=False, xdep=()):
    S, F, PS = cx.S, cx.F, cx.PS
    for kt in range(0 if have_rs else 16):
        xt, xb = F[kt % 2]
        S.dma(xt[:, 0:L], xT[kt], writes=[xb])
        sq, sqb = F[2]
        act(S, sq[:, 0:L], xt[:, 0:L], AF.Square, [xb], [sqb])
        for tc in range(4):
            ps, pb = PS[tc]
            S.op("pe", lambda e, ps=ps, sq=sq, tc=tc, kt=kt: e.matmul(
                ps[:, :], lhsT=cx.ones_f[:, :], rhs=sq[:, tc * 512:(tc + 1) * 512], start=(kt == 0), stop=(kt == 15)),
                [sqb, cx.constb], [pb])
    rs, rsb = F[2]
    for tc in range(0 if have_rs else 4):
        ps, pb = PS[tc]
        act(S, rs[:, tc * 512:(tc + 1) * 512], ps[:, :], AF.Sqrt, [pb], [rsb], scale=1.0 / D, bias=EPS)
    if not have_rs:
        recip(S, rs[:, 0:L], rs[:, 0:L], [rsb], [rsb])
    for kt in range(16):
        xt, xb = F[kt % 2]
        S.dma(xt[:, 0:L], xT[kt], reads=list(xdep), writes=[xb])
        stt(S, cx.hT[:, kt, :], xt[:, 0:L], gcol[:, kt:kt + 1], rs[:, 0:L], ALU.mult, ALU.mult,
            [xb, rsb, gB], [cx.hTb[kt]])


def load_w(cx, i, src, n):
    wt, wb = cx.W[i % 2]
    cx.S.dma(wt[:, 0:n], src, writes=[wb], eng="pool")
    return wt, wb


def tm_proj512(cx, wsrc, consume):
    S, PS = cx.S, cx.PS
    f0v = cx.F[0][0][:, :].bitcast(BF16)
    f1v = cx.F[1][0][:, :].bitcast(BF16)
    sets = [[(cx.W[0][0][:, 0:4096], cx.W[0][1]), (cx.W[1][0][:, 0:4096], cx.W[1][1])],
            [(f0v[:, 0:4096], cx.F[0][1]), (f1v[:, 0:4096], cx.F[1][1])]]
    for blk in range(4):
        halves = []
        for hf in range(2):
            wt, wb = sets[blk % 2][hf]
            S.dma(wt, wsrc[blk][:, hf * 4096:(hf + 1) * 4096], writes=[wb], eng="pool")
            halves.append((wt.rearrange("p (k c) -> p k c", k=8), wb))
        for t in range(16):
            ps, pb = PS[t % 8]
            pairs = [(cx.hT[:, kt, t * 128:(t + 1) * 128], halves[kt // 8][0][:, kt % 8, :]) for kt in range(16)]
            mmg(S, ps[:, :], pairs, [halves[0][1], halves[1][1]] + cx.hTb, [pb])
            consume(blk, t, ps, pb)


def emit_l0(cx, T):
    S, F, H, J, PS, W = cx.S, cx.F, cx.H, cx.J, cx.PS, cx.W
    nc = cx.nc
    hT, hTb = cx.hT, cx.hTb
    sm = cx.small

    g0 = sm("g0", [128, 16]); g0b = Buf("g0")
    S.dma(g0[:], T["l0_g"], writes=[g0b])
    cw = sm("cw", [128, 96]); cb_ = sm("cb", [128, 24]); cwb = Buf("cw")
    S.dma(cw[:], T["l0_cw"], writes=[cwb])
    S.dma(cb_[:], T["l0_cb"], writes=[cwb])
    dtb = sm("dtb", [128, 32]); a_b = sm("a_b", [128, 32]); dsk = sm("dsk", [128, 32]); pb3 = Buf("p3")
    S.dma(dtb[:], bc(T["l0_dtb"], [128, 32]), writes=[pb3])
    S.dma(a_b[:], bc(T["l0_alog"], [128, 32]), writes=[pb3])
    S.dma(dsk[:], bc(T["l0_dsk"], [128, 32]), writes=[pb3])
    act(S, a_b[:], a_b[:], AF.Exp, [pb3], [pb3])
    ts(S, "dve", a_b[:], a_b[:], -1.0, None, ALU.mult, None, [pb3], [pb3])
    ssg = sm("ssg", [128, 16]); ssgb = Buf("ssg")
    S.dma(ssg[:], T["l0_ssg"], writes=[ssgb])

    emit_rmsnorm_fm(cx, T["xT"], g0, g0b, "l0")

    if STOP == "a1":
        return
    R2 = cx.R2
    BT = R2[:, 0:8192].rearrange("p (g t) -> p g t", g=4)
    CT = R2[:, 8192:16384].rearrange("p (g t) -> p g t", g=4)
    Btm = R2[:, 16384:24576].rearrange("p (c g n) -> p c g n", c=16, g=4)
    BTb = [Buf("BT%d" % g) for g in range(4)]
    CTb = [Buf("CT%d" % g) for g in range(4)]
    Btmb = Buf("Btm")
    x_tm, szb_tm, yb_tm, gv_tm = T["x_tm"], T["szb_tm"], T["yb_tm"], T["gv_tm"]
    x_tmb, szb_tmb, yb_tmb, gv_tmb = Buf("x_tm"), Buf("szb_tm"), Buf("yb_tm"), Buf("gv_tm")

    for i in range(2):
        memset(S, "pool", F[i][0][:, 0:3], 0.0, [F[i][1]])
    p2tail = []
    for ft in range(24):
        wt, wb = load_w(cx, ft, T["l0_wxbc"][ft], 2048)
        wv = wt[:, 0:2048].rearrange("p (k c) -> p k c", k=16)
        xpad, xpb = F[ft % 2]
        for tc in range(4):
            ps, pb = PS[(ft * 4 + tc) % 4]
            mmg(S, ps[:, :], [(wv[:, kt, :], hT[:, kt, tc * 512:(tc + 1) * 512]) for kt in range(16)],
                [wb] + hTb, [pb])
            cp(S, "act", xpad[:, 3 + tc * 512:3 + (tc + 1) * 512], ps[:, :], [pb], [xpb])
        while p2tail:
            p2tail.pop(0)()
        acc, accb = F[2]
        ts(S, "dve", acc[:, 0:L], xpad[:, 0:L], cw[:, ft * 4:ft * 4 + 1], None, ALU.mult, None, [xpb, cwb], [accb])
        for k in range(1, 4):
            stt(S, acc[:, 0:L], xpad[:, k:k + L], cw[:, ft * 4 + k:ft * 4 + k + 1], acc[:, 0:L], ALU.mult, ALU.add,
                [xpb, cwb, accb], [accb])
        if ft < 16:
            xc, xcb = H[ft % 2]
            act(S, xc[:, :], acc[:, 0:L], AF.Silu, [accb, cwb], [xcb], bias=cb_[:, ft:ft + 1])

            def tail(ft=ft, xc=xc, xcb=xcb):
                stg, stgb = H[2]
                for half in range(2):
                    ps, pb = PS[4 + half]
                    psv = ps[:, :].bitcast(BF16)
                    for kk in range(8):
                        c = half * 8 + kk
                        transp(S, psv[:, kk * 128:(kk + 1) * 128], xc[:, c * 128:(c + 1) * 128], cx.ident[:, :],
                               [xcb, cx.constb], [pb])
                    cp(S, "dve", stg[:, half * 1024:(half + 1) * 1024], psv[:, 0:1024], [pb], [stgb])
                S.dma(x_tm.rearrange("c p n -> p c n")[:, :, ft * 128:(ft + 1) * 128],
                      stg[:, :].rearrange("p (c n) -> p c n", c=16), reads=[stgb], writes=[x_tmb])
            p2tail.append(tail)
        elif ft < 20:
            g = ft - 16
            act(S, BT[:, g, :], acc[:, 0:L], AF.Silu, [accb, cwb], [BTb[g]], bias=cb_[:, ft:ft + 1])

            def tail(g=g):
                for half in range(2):
                    ps, pb = PS[4 + half]
                    psv = ps[:, :].bitcast(BF16)
                    for kk in range(8):
                        c = half * 8 + kk
                        transp(S, psv[:, kk * 128:(kk + 1) * 128], BT[:, g, c * 128:(c + 1) * 128], cx.ident[:, :],
                               [BTb[g], cx.constb], [pb])
                    cp(S, "dve", Btm[:, half * 8:(half + 1) * 8, g, :], psv[:, 0:1024].rearrange("p (c n) -> p c n", c=8),
                       [pb], [Btmb])
            p2tail.append(tail)
        else:
            g = ft - 20
            act(S, CT[:, g, :], acc[:, 0:L], AF.Silu, [accb, cwb], [CTb[g]], bias=cb_[:, ft:ft + 1])
    while p2tail:
        p2tail.pop(0)()

    if STOP == "a2":
        return
    def zb_out(blk, t, ps, pb):
        stg, stgb = H[(t // 4) % 2]
        sl = stg[:, (t % 4) * 512:(t % 4 + 1) * 512]
        act(S, sl, ps[:, :], AF.Silu, [pb], [stgb])
        S.dma(szb_tm[t][:, blk * 512:(blk + 1) * 512], sl, reads=[stgb], writes=[szb_tmb])
    tm_proj512(cx, T["l0_wzb"], zb_out)
    dt_tm = sm("dt_tm", [128, 16, 32]); adt_tm = sm("adt_tm", [128, 16, 32]); dtB = Buf("dt")
    wt, wb = load_w(cx, 0, T["l0_wdt"], 512)
    wv = wt[:, 0:512].rearrange("p (k c) -> p k c", k=16)
    for t in range(16):
        ps, pb = PS[t % 4]
        mmg(S, ps[:, 0:32], [(hT[:, kt, t * 128:(t + 1) * 128], wv[:, kt, :]) for kt in range(16)], [wb] + hTb, [pb])
        tt(S, "dve", dt_tm[:, t, :], ps[:, 0:32], dtb[:], ALU.add, [pb, pb3], [dtB])
    act(S, dt_tm[:], dt_tm[:], AF.Exp, [dtB], [dtB])
    act(S, dt_tm[:], dt_tm[:], AF.Ln, [dtB], [dtB], bias=1.0)
    tt(S, "dve", adt_tm[:], dt_tm[:], a_b[:].unsqueeze(1).broadcast_to([128, 16, 32]), ALU.mult, [dtB, pb3], [dtB])

    if STOP == "a3":
        return
    prev, prevb_all = F[0]
    prevv = prev[:, 0:2048].rearrange("p (g n) -> p g n", g=4)
    memset(S, "pool", prev[:, 0:2048], 0.0, [prevb_all])
    prevB = [Buf("prev%d" % g) for g in range(4)]
    pbf_t, pbf_allb = W[1]
    pbf = pbf_t[:, 0:2048].rearrange("p (g n) -> p g n", g=4)
    memset(S, "pool", pbf_t[:, 0:2048], 0.0, [pbf_allb])
    pbfB = [Buf("pbf%d" % g) for g in range(4)]
    Mt_t, Mt_allb = W[0]
    Mt = [Mt_t[:, i * 1024:(i + 1) * 1024].rearrange("p (r l) -> p r l", r=8) for i in range(2)]
    MtB = [Buf("Mt0"), Buf("Mt1")]
    CBm = Mt_t[:, 2048:2560].rearrange("p (g l) -> p g l", g=4); CBmB = Buf("CBm")
    Et = [Mt_t[:, 2560 + i * 512:2560 + (i + 1) * 512].rearrange("p (r l) -> p r l", r=4) for i in range(2)]
    EtB = [Buf("Et0"), Buf("Et1")]
    f1, f1b_all = F[1]
    f2, f2b_all = F[2]
    t1s = [f1[:, 0:512], f1[:, 512:1024]]; t2s = [f1[:, 1024:1536], f1[:, 1536:2048]]
    yvs = [f2[:, 1024:1536], f2[:, 1536:2048]]
    junk = cx.R2[:, 32768:33280]
    t1Bs, t2Bs, yvBs = [Buf("t1a"), Buf("t1b")], [Buf("t2a"), Buf("t2b")], [Buf("yva"), Buf("yvb")]
    junkB = Buf("junk")
    seg = [f2[:, i * 512:(i + 1) * 512].rearrange("p (r l) -> p r l", r=4) for i in range(2)]
    segB = [Buf("seg0"), Buf("seg1")]
    acum = sm("acum", [128, 32]); nacum = sm("nacum", [128, 32]); alast = sm("alast", [128, 32])
    ea = sm("ea", [128, 32]); cdb = sm("cdb", [128, 32]); dte = sm("dte", [128, 32]); ss = sm("ss", [128, 4])
    rr = sm("rr", [128, 4])
    smB = Buf("ssd_small")
    ssBs = [Buf("ssd_ss0"), Buf("ssd_ss1")]
    p4sub = prevB + pbfB + MtB + [CBmB] + EtB + t1Bs + t2Bs + yvBs + segB
    p4whole = [prevb_all, pbf_allb, Mt_allb, f1b_all, f2b_all]
    fence(cx, p4whole, p4sub)
    xdt, xdtB = J[1]
    xdts, xdtsB = J[2]
    ybt, ybB = J[3]
    cnt = 0
    def p4_load(c):
        xc, xcb = H[c % 2]
        S.dma(xc[:, :], x_tm[c], reads=[x_tmb], writes=[xcb])
        zc, zcb = (H[2] if c % 2 == 0 else J[0])
        S.dma(zc[:, :], szb_tm[c], reads=[szb_tmb], writes=[zcb])
    p4_load(0)
    for c in range(16):
        cs = slice(c * 128, (c + 1) * 128)
        xc, xcb = H[c % 2]
        zc, zcb = (H[2] if c % 2 == 0 else J[0])
        if c < 15:
            p4_load(c + 1)
        psa, psab = PS[4]
        mmg(S, psa[:, 0:32], [(cx.tri_f[:, :], adt_tm[:, c, :])], [dtB, cx.constb], [psab])
        mmg(S, psa[:, 32:64], [(cx.ones_f[:, :], adt_tm[:, c, :])], [dtB, cx.constb], [psab])
        cp(S, "dve", acum[:], psa[:, 0:32], [psab], [smB])
        cp(S, "dve", alast[:], psa[:, 32:64], [psab], [smB])
        ts(S, "dve", nacum[:], acum[:], -1.0, None, ALU.mult, None, [smB], [smB])
        act(S, ea[:], acum[:], AF.Exp, [smB], [smB])
        act(S, cdb[:], alast[:], AF.Exp, [smB], [smB])
        tt(S, "dve", dte[:], alast[:], acum[:], ALU.subtract, [smB], [smB])
        act(S, dte[:], dte[:], AF.Exp, [smB], [smB])
        psc, pscb = PS[5]
        for g in range(4):
            mmg(S, psc[:, g * 128:(g + 1) * 128], [(BT[:, g, cs], CT[:, g, cs])], [BTb[g], CTb[g]], [pscb])
        tt(S, "dve", CBm, psc[:, :].rearrange("p (g l) -> p g l", g=4),
           cx.maskT[:, :].unsqueeze(1).broadcast_to([128, 4, 128]), ALU.mult, [pscb, cx.constb], [CBmB])
        tt(S, "dve", xdt[:, :].rearrange("p (h d) -> p h d", h=32), xc[:, :].rearrange("p (h d) -> p h d", h=32),
           dt_tm[:, c, :].unsqueeze(2).broadcast_to([128, 32, 64]), ALU.mult, [xcb, dtB], [xdtB])
        tt(S, "pool", xdts[:, :].rearrange("p (h d) -> p h d", h=32), xdt[:, :].rearrange("p (h d) -> p h d", h=32),
           dte[:].unsqueeze(2).broadcast_to([128, 32, 64]), ALU.mult, [xdtB, smB], [xdtsB])
        def stage1(g, c=c, cs=cs):
            M, MB = Mt[g % 2], MtB[g % 2]
            for half in range(2):
                psA, psAb = PS[6 + half]
                h0 = g * 8 + half * 4
                for hh in range(4):
                    h = h0 + hh
                    mmg(S, psA[:, hh * 128:(hh + 1) * 128],
                        [(adt_tm[:, c, h:h + 1].broadcast_to([128, 128]), cx.tri_f[:, :])], [dtB, cx.constb], [psAb])
                sg_, sgB_ = seg[half], segB[half]
                tt(S, "dve", sg_, psA[:, :].rearrange("p (r l) -> p r l", r=4),
                   acum[:, h0:h0 + 4].unsqueeze(2).broadcast_to([128, 4, 128]), ALU.min, [psAb, smB], [sgB_])
                E, EB = Et[half], EtB[half]
                for hh in range(4):
                    h = h0 + hh
                    act(S, E[:, hh, :], sg_[:, hh, :], AF.Exp, [sgB_, smB], [EB], bias=nacum[:, h:h + 1])
                tt(S, "pool", M[:, half * 4:(half + 1) * 4, :], E, CBm[:, g:g + 1, :].broadcast_to([128, 4, 128]),
                   ALU.mult, [EB, CBmB], [MB])

        def stage2(g, c=c, cs=cs, xc=xc, xcb=xcb, zc=zc, zcb=zcb):
            M, MB = Mt[g % 2], MtB[g % 2]
            psy, psyb = PS[g % 2]
            for r in range(8):
                h = g * 8 + r
                mmg(S, psy[:, r * 64:(r + 1) * 64], [(M[:, r, :], xdt[:, h * 64:(h + 1) * 64])], [MB, xdtB], [psyb])
            pso, psob = PS[2]
            mmg(S, pso[:, :], [(CT[:, g, cs], pbf[:, g, :])], [CTb[g], pbfB[g]], [psob])
            pss, pssb = PS[3]
            mmg(S, pss[:, :], [(Btm[:, c, g, :], xdts[:, g * 512:(g + 1) * 512])], [Btmb, xdtsB], [pssb])
            gsl = slice(g * 512, (g + 1) * 512)
            t1, t2, yv = t1s[g % 2], t2s[g % 2], yvs[g % 2]
            t1B, t2B, yvB, ssB = t1Bs[g % 2], t2Bs[g % 2], yvBs[g % 2], ssBs[g % 2]
            tt(S, "pool", t2.rearrange("p (r d) -> p r d", r=8), xc[:, gsl].rearrange("p (r d) -> p r d", r=8),
               dsk[:, g * 8:(g + 1) * 8].unsqueeze(2).broadcast_to([128, 8, 64]), ALU.mult, [xcb, pb3], [t2B])
            tt(S, "dve", t1.rearrange("p (r d) -> p r d", r=8), pso[:, :].rearrange("p (r d) -> p r d", r=8),
               ea[:, g * 8:(g + 1) * 8].unsqueeze(2).broadcast_to([128, 8, 64]), ALU.mult, [psob, smB], [t1B])
            tt(S, "dve", t1, t1, t2, ALU.add, [t1B, t2B], [t1B])
            tt(S, "dve", yv, psy[:, :], t1, ALU.add, [psyb, t1B], [yvB])
            tt(S, "dve", yv, yv, zc[:, gsl], ALU.mult, [yvB, zcb], [yvB])
            act(S, junk, yv, AF.Square, [yvB], [junkB, ssB], scale=512.0 ** -0.5, accum_out=ss[:, g:g + 1])
            act(S, rr[:, g:g + 1], ss[:, g:g + 1], AF.Ln, [ssB], [ssB], bias=EPS)
            act(S, rr[:, g:g + 1], rr[:, g:g + 1], AF.Exp, [ssB], [ssB], scale=-0.5)
            tt(S, "dve", prevv[:, g, :].rearrange("p (r d) -> p r d", r=8), prevv[:, g, :].rearrange("p (r d) -> p r d", r=8),
               cdb[:, g * 8:(g + 1) * 8].unsqueeze(2).broadcast_to([128, 8, 64]), ALU.mult, [prevB[g], smB], [prevB[g]])
            tt(S, "dve", prevv[:, g, :], prevv[:, g, :], pss[:, :], ALU.add, [prevB[g], pssb], [prevB[g]])
            ts(S, "dve", ybt[:, gsl], yv, rr[:, g:g + 1], None, ALU.mult, None, [yvB, ssB], [ybB])
            cp(S, "act", pbf[:, g, :], prevv[:, g, :], [prevB[g]], [pbfB[g]])

        stage1(0)
        for g in range(4):
            if g < 3:
                stage1(g + 1)
            stage2(g)
        S.dma(yb_tm[c], ybt[:, :], reads=[ybB], writes=[yb_tmb])

    if STOP == "a4":
        return
    gaT = cx.R2[:, 0:32768].rearrange("p (k t) -> p k t", k=16)
    gaB = [Buf("ga%d" % k) for k in range(16)]
    r2users = BTb + CTb + [Btmb] + [J[i][1] for i in range(4)]
    fence(cx, r2users, gaB)
    fence(cx, p4sub, p4whole)
    f0, f0b = F[0]
    f1, f1b = F[1]
    for ft in range(16):
        wt, wb = load_w(cx, ft, T["l0_wuz"][ft], 4096)
        wv = wt[:, 0:4096].rearrange("p (a k c) -> p a k c", a=2, k=16)
        for tc in range(4):
            tsl = slice(tc * 512, (tc + 1) * 512)
            psu, psub = PS[(2 * tc) % 8]
            psz, pszb = PS[(2 * tc + 1) % 8]
            mmg(S, psu[:, :], [(wv[:, 0, kt, :], hT[:, kt, tsl]) for kt in range(16)], [wb] + hTb, [psub])
            mmg(S, psz[:, :], [(wv[:, 1, kt, :], hT[:, kt, tsl]) for kt in range(16)], [wb] + hTb, [pszb])
            act(S, f0[:, tsl], psu[:, :], AF.Gelu_apprx_tanh, [psub], [f0b])
            act(S, f1[:, tsl], psz[:, :], AF.Silu, [pszb], [f1b])
            tt(S, "dve", gaT[:, ft, tsl], f0[:, tsl], f1[:, tsl], ALU.mult, [f0b, f1b], [gaB[ft]])
    def gv_out(blk, t, ps, pb):
        stg, stgb = H[(t // 4) % 2]
        sl = stg[:, (t % 4) * 512:(t % 4 + 1) * 512]
        act(S, sl, ps[:, :], AF.Gelu_apprx_tanh, [pb], [stgb])
        S.dma(gv_tm[t][:, blk * 512:(blk + 1) * 512], sl, reads=[stgb], writes=[gv_tmb])
    tm_proj512(cx, T["l0_wv"], gv_out)

    if STOP == "a5":
        return
    wst_t, wstb = W[0]
    WsT = wst_t[:, 0:2048].rearrange("p (g t) -> p g t", g=16)
    S.dma(wst_t[:, 0:2048], T["l0_wsT"], writes=[wstb], eng="pool")
    tt(S, "pool", WsT, WsT, cx.maskT[:, :].unsqueeze(1).broadcast_to([128, 16, 128]), ALU.mult, [wstb, cx.constb], [wstb])
    sbB = W[1][1]
    sbhi = W[1][0][0:1, 0:2048]; sblo = W[1][0][0:1, 2048:4096]
    sbf = F[2][0][0:1, 0:2048]; sbt = F[1][0][0:1, 0:2048]
    S.dma(sbf, T["l0_sb"], writes=[F[2][1]])
    cp(S, "dve", sbhi, sbf, [F[2][1]], [sbB])
    cp(S, "dve", sbt, sbhi, [sbB], [F[1][1]])
    tt(S, "dve", sblo, sbf, sbt, ALU.subtract, [F[2][1], F[1][1]], [sbB])
    lng, lngb = F[0]
    lnb, lnbb = F[1]
    S.dma(lng[:, 0:2048], bc(T["l0_lng"], [128, 2048]), writes=[lngb])
    S.dma(lnb[:, 0:2048], bc(T["l0_lnb"], [128, 2048]), writes=[lnbb])
    tmp, tmpb = F[2]
    st6 = sm("st6", [128, 24]); mv = sm("mv", [128, 2]); lnr = sm("lnr", [128, 1]); nmr = sm("nmr", [128, 1])
    lnB = Buf("lnsmall")
    vns = [H[2], (W[0][0][:, 2048:4096], Buf("vn1"))]
    fence(cx, [W[0][1]], [vns[1][1]])

    def ln_chunk(c):
        gv, gvb = H[c % 2]
        S.dma(gv[:, :], gv_tm[c], reads=[gv_tmb], writes=[gvb])
        for q in range(4):
            S.op("dve", lambda e, q=q, gv=gv: e.bn_stats(out=st6[:, q * 6:(q + 1) * 6], in_=gv[:, q * 512:(q + 1) * 512]),
                 [gvb], [lnB])
        S.op("dve", lambda e: e.bn_aggr(out=mv[:], in_=st6[:]), [lnB], [lnB])
        act(S, lnr[:], mv[:, 1:2], AF.Sqrt, [lnB], [lnB], bias=EPS)
        recip(S, lnr[:], lnr[:], [lnB], [lnB])
        ts(S, "dve", nmr[:], mv[:, 0:1], lnr[:, 0:1], -1.0, ALU.mult, ALU.mult, [lnB], [lnB])
        act(S, tmp[:, 0:2048], gv[:, :], AF.Identity, [gvb, lnB], [tmpb], scale=lnr[:, 0:1], bias=nmr[:, 0:1])
        tt(S, "dve", tmp[:, 0:2048], tmp[:, 0:2048], lng[:, 0:2048], ALU.mult, [tmpb, lngb], [tmpb])
        vn, vnb = vns[c % 2]
        tt(S, "pool", vn[:, :], tmp[:, 0:2048], lnb[:, 0:2048], ALU.add, [tmpb, lnbb], [vnb])

    ln_chunk(0)
    for c in range(16):
        cs = slice(c * 128, (c + 1) * 128)
        if c < 15:
            ln_chunk(c + 1)
        vn, vnb = vns[c % 2]
        for gq in range(4):
            ps, pb = PS[(c * 4 + gq) % 8]
            for gi in range(4):
                g = gq * 4 + gi
                gl = slice(g * 128, (g + 1) * 128)
                mmg(S, ps[:, gi * 128:(gi + 1) * 128],
                    [(vn[:, gl], WsT[:, g, :]), (cx.ones_row[0:1, :], sbhi[0:1, gl]), (cx.ones_row[0:1, :], sblo[0:1, gl])],
                    [vnb, wstb, sbB, cx.constb], [pb])
            tt(S, "dve", gaT[:, gq * 4:(gq + 1) * 4, cs], ps[:, :].rearrange("p (g t) -> p g t", g=4),
               gaT[:, gq * 4:(gq + 1) * 4, cs], ALU.mult, [pb] + gaB[gq * 4:(gq + 1) * 4], gaB[gq * 4:(gq + 1) * 4])
    fence(cx, [vns[1][1]], [W[0][1]])

    if STOP == "a6":
        return
    ybT = cx.hT
    for c in range(16):
        cs = slice(c * 128, (c + 1) * 128)
        yb, ybb = H[c % 2]
        S.dma(yb[:, :], yb_tm[c], reads=[yb_tmb], writes=[ybb])
        for half in range(2):
            ps, pb = PS[4 + half]
            psv = ps[:, :].bitcast(BF16)
            for kk in range(8):
                kt = half * 8 + kk
                transp(S, psv[:, kk * 128:(kk + 1) * 128], yb[:, kt * 128:(kt + 1) * 128], cx.ident[:, :], [ybb, cx.constb], [pb])
            tt(S, "dve", ybT[:, half * 8:(half + 1) * 8, cs], psv[:, 0:1024].rearrange("p (k t) -> p k t", k=8),
               ssg[:, half * 8:(half + 1) * 8].unsqueeze(2).broadcast_to([128, 8, 128]), ALU.mult,
               [pb, ssgb], hTb[half * 8:(half + 1) * 8])
    for dt_ in range(16):
        wt, wb = load_w(cx, dt_, T["l0_wout"][dt_], 4096)
        wv = wt[:, 0:4096].rearrange("p (k c) -> p k c", k=32)
        if dt_ == 0:
            S.dma(F[0][0][:, 0:L], T["xT"][0], writes=[F[0][1]])
        if dt_ < 15:
            S.dma(F[(dt_ + 1) % 2][0][:, 0:L], T["xT"][dt_ + 1], writes=[F[(dt_ + 1) % 2][1]])
        xt, xb = F[dt_ % 2]
        ot, otb = F[2]
        for tc in range(4):
            tsl = slice(tc * 512, (tc + 1) * 512)
            ps, pb = PS[(dt_ * 4 + tc) % 4]
            pairs = [(wv[:, kt, :], gaT[:, kt, tsl]) for kt in range(16)] + [(wv[:, 16 + kt, :], ybT[:, kt, tsl]) for kt in range(16)]
            mmg(S, ps[:, :], pairs, [wb] + gaB + hTb, [pb])
            tt(S, "dve", ot[:, tsl], ps[:, :], xt[:, tsl], ALU.add, [pb, xb], [otb])
        S.dma(T["x1T"][dt_], ot[:, 0:L], reads=[otb], writes=[cx.x1Tb])
        if cx.fuse_stats:
            act(S, xt[:, 0:L], ot[:, 0:L], AF.Square, [otb, xb], [xb])
            for tc in range(4):
                ps, pb = PS[4 + tc]
                S.op("pe", lambda e, ps=ps, xt=xt, tc=tc, dt_=dt_: e.matmul(
                    ps[:, :], lhsT=cx.ones_f[:, :], rhs=xt[:, tc * 512:(tc + 1) * 512], start=(dt_ == 0), stop=(dt_ == 15)),
                    [xb, cx.constb], [pb])
    if cx.fuse_stats:
        rs, rsb = F[2]
        for tc in range(4):
            ps, pb = PS[4 + tc]
            act(S, rs[:, tc * 512:(tc + 1) * 512], ps[:, :], AF.Sqrt, [pb], [rsb], scale=1.0 / D, bias=EPS)
        recip(S, rs[:, 0:L], rs[:, 0:L], [rsb], [rsb])
    cx.r2all = gaB + r2users + [junkB]


def emit_l1(cx, T):
    S, F, H, PS, W = cx.S, cx.F, cx.H, cx.PS, cx.W
    hT, hTb = cx.hT, cx.hTb
    sm = cx.small
    x1T = T["x1T"]

    g1 = sm("g1", [128, 16]); gf = sm("gf", [128, 16]); g1b = Buf("g1")
    S.dma(g1[:], T["l1_g"], writes=[g1b])
    S.dma(gf[:], T["fin_g"], writes=[g1b])
    lq = sm("lq", [128, 256]); lamB = Buf("lam")
    for i, n in enumerate(["l1_lq1", "l1_lk1", "l1_lq2", "l1_lk2"]):
        S.dma(lq[:, i * 64:(i + 1) * 64], bc(T[n], [128, 64]), writes=[lamB])
    lp = sm("lp", [128, 128]); l12 = sm("l12", [128, 2]); nlam = sm("nlam", [128, 1])
    tt(S, "dve", lp[:, 0:64], lq[:, 0:64], lq[:, 64:128], ALU.mult, [lamB], [lamB])
    tt(S, "dve", lp[:, 64:128], lq[:, 128:192], lq[:, 192:256], ALU.mult, [lamB], [lamB])
    S.op("dve", lambda e: e.tensor_reduce(out=l12[:], in_=lp[:].rearrange("p (a d) -> p a d", a=2), axis=AX.X, op=ALU.add),
         [lamB], [lamB])
    act(S, l12[:], l12[:], AF.Exp, [lamB], [lamB])
    tt(S, "dve", nlam[:], l12[:, 1:2], l12[:, 0:1], ALU.subtract, [lamB], [lamB])
    ts(S, "dve", nlam[:], nlam[:], -LAMBDA_INIT, None, ALU.add, None, [lamB], [lamB])
    if STOP == "s1":
        return
    gsub = sm("gsub", [128, 128]); gsB = Buf("gsub")
    S.dma(gsub[:], bc(T["l1_subg"], [128, 128]), writes=[gsB])
    ts(S, "dve", gsub[:], gsub[:], 1.0 - LAMBDA_INIT, None, ALU.mult, None, [gsB], [gsB])
    tab = sm("tab", [128, 256]); tabB = Buf("tab")
    S.dma(tab[:], T["c_tab"], writes=[tabB])
    if STOP == "s2":
        return
    nmask = cx.smallbf("nmask", [128, 128])
    cB = Buf("l1const")
    S.dma(nmask[:], T["c_negmask"], writes=[cB], eng="pool")
    sel = cx.smallbf("sel", [128, 256])
    S.dma(sel[:], T["c_sel"], writes=[cB], eng="pool")

    if STOP == "s3":
        return
    emit_rmsnorm_fm(cx, x1T, g1, g1b, "l1", have_rs=cx.fuse_stats, xdep=[cx.x1Tb])

    if STOP == "p1":
        return
    qT_d, kT_d, sgT_d = T["qT_d"], T["kT_d"], T["sgT_d"]
    qkB = Buf("qk_d"); sgdB = Buf("sg_d")
    n2 = sm("n2", [128, 256]); n2B = Buf("n2")
    Vt = cx.R2[:, 0:33280].rearrange("p (t h v) -> p t h v", t=16, h=16)
    VB = [Buf("V%d" % t) for t in range(16)]
    memset(S, "pool", cx.R2[:, 0:33280], 1.0, VB + cx.r2all)
    f2v_ = F[2][0][:, :].bitcast(BF16)
    SQ = [H[2], (f2v_[:, 0:2048], F[2][1])]
    qktail = []
    for h in range(16):
        wt, wb = load_w(cx, h, T["l1_wqk"][h], 4096)
        wv = wt[:, 0:4096].rearrange("p (a k c) -> p a k c", a=2, k=16)
        for a in range(2):
            stg, stgb = H[a]
            for tc in range(4):
                tsl = slice(tc * 512, (tc + 1) * 512)
                ps, pb = PS[(a * 4 + tc) % 4]
                mmg(S, ps[:, :], [(wv[:, a, kt, :], hT[:, kt, tsl]) for kt in range(16)], [wb] + hTb, [pb])
                if tc % 2 == 0:
                    ts(S, "dve", stg[:, tsl], ps[:, :], 0.125 if a == 0 else 1.0, None, ALU.mult, None, [pb], [stgb])
                else:
                    act(S, stg[:, tsl], ps[:, :], AF.Copy, [pb], [stgb], scale=0.125 if a == 0 else 1.0)
                if tc == 1 and qktail:
                    qktail.pop(0)()
            while qktail:
                qktail.pop(0)()
            sq, sqB = SQ[(h * 2 + a) % 2]
            act(S, sq[:, :], stg[:, :], AF.Square, [stgb], [sqB])
            S.dma((qT_d if a == 0 else kT_d)[h], stg[:, :], reads=[stgb], writes=[qkB])

            def tail(m, h=h, a=a, sq=sq, sqB=sqB):
                if True:
                    for tc in range(4):
                        pn, pnb = PS[4 + tc]
                        mmg(S, pn[:, :], [(sel[:, m * 128:(m + 1) * 128], sq[:, tc * 512:(tc + 1) * 512])], [sqB, cB], [pnb])
                        col = ((h * 2 + a) * 2 + m) * 4 + tc
                        S.op("dve", lambda e, pn=pn, col=col: e.tensor_reduce(out=n2[:, col:col + 1], in_=pn[:, :], axis=AX.X, op=ALU.max),
                             [pnb], [n2B])
            qktail.append(lambda tail=tail: tail(0))
            qktail.append(lambda tail=tail: tail(1))
    while qktail:
        qktail.pop(0)()
    if STOP == "p2a":
        return
    n2m = sm("n2m", [128, 64]); Bhm = sm("Bhm", [128, 32])
    S.op("dve", lambda e: e.tensor_reduce(out=n2m[:], in_=n2[:].rearrange("p (x t) -> p x t", t=4), axis=AX.X, op=ALU.max),
         [n2B], [n2B])
    n2v = n2m[:].rearrange("p (h a m) -> p h a m", h=16, a=2)
    tt(S, "dve", Bhm[:].rearrange("p (h m) -> p h m", h=16), n2v[:, :, 0, :], n2v[:, :, 1, :], ALU.mult, [n2B], [n2B])
    act(S, Bhm[:], Bhm[:], AF.Sqrt, [n2B], [n2B])
    if STOP == "p2b":
        return
    def v_out(blk, t, ps, pb):
        cp(S, "act", Vt[:, t, 4 * blk:4 * blk + 4, 0:128], ps[:, :].rearrange("p (h v) -> p h v", h=4), [pb], [VB[t]])
    tm_proj512(cx, T["l1_wv"], v_out)
    if STOP == "p2c":
        return
    for ft in range(16):
        wt, wb = load_w(cx, ft, T["l1_wg"][ft], 2048)
        wv = wt[:, 0:2048].rearrange("p (k c) -> p k c", k=16)
        stg, stgb = H[ft % 2]
        for tc in range(4):
            tsl = slice(tc * 512, (tc + 1) * 512)
            ps, pb = PS[tc]
            mmg(S, ps[:, :], [(wv[:, kt, :], hT[:, kt, tsl]) for kt in range(16)], [wb] + hTb, [pb])
            act(S, stg[:, tsl], ps[:, :], AF.Silu, [pb], [stgb])
        S.dma(sgT_d[ft], stg[:, :], reads=[stgb], writes=[sgdB])

    if STOP == "p2":
        return
    oT = cx.hT
    f1v = F[1][0][:, :].bitcast(BF16)
    f2v = F[2][0][:, :].bitcast(BF16)
    f1a, f1b_, f2a, f2b_ = Buf("f1a"), Buf("f1b"), Buf("f2a"), Buf("f2b")
    w0a, w0b, w1a = Buf("w0a"), Buf("w0b"), Buf("w1a")
    QK8 = [[(H[0][0][:, :], H[0][1]), (H[1][0][:, :], H[1][1]), (H[2][0][:, :], H[2][1]), (W[0][0][:, 0:2048], w0a)],
           [(f1v[:, 0:2048], f1a), (f1v[:, 2048:4096], f1b_), (f2v[:, 0:2048], f2a), (f2v[:, 2048:4096], f2b_)]]
    SG2 = [(W[0][0][:, 2048:4096], w0b), (W[1][0][:, 0:2048], w1a)]
    PTt = [W[1][0][:, 2048 + i * 512:2048 + (i + 1) * 512] for i in range(4)]
    PTB = [Buf("PT%d" % i) for i in range(4)]
    attsub = [w0a, w0b, w1a, f1a, f1b_, f2a, f2b_] + PTB
    biasB2 = sm("biasB", [128, 64]); bbB = [Buf("biasB0"), Buf("biasB1")]
    f0, f0b = F[0]
    t2 = f0[:, 0:512]
    ovs = [f0[:, 512:1024], f0[:, 1024:1536]]
    junk = f0[:, 1536:2048]
    onbf = cx.smallbf("onbf", [128, 512]); onB = Buf("onbf")
    t2B, junkB = Buf("t2"), Buf("junk")
    ovB = [Buf("ov0"), Buf("ov1")]
    attsub += [t2B, junkB] + ovB
    attwhole = [W[0][1], W[1][1], f0b, F[1][1], F[2][1]]
    fence(cx, attwhole, attsub)
    for par in range(2):
        for t_i, (tl, tb) in enumerate(QK8[par]):
            memset(S, "pool", tl[64:128, :], 0.0, [tb])
            if t_i >= 2:
                memset(S, "pool", tl[64:65, :], 1.0, [tb])
    rsum = sm("rsum", [128, 8]); rsB = Buf("rsum")
    msq = sm("msq", [128, 4]); msB = Buf("msq")
    v3 = lambda ap, n: ap.rearrange("p (b c) -> p b c", b=n)

    iters = [(h, I, m, j) for h in range(16) for I in range(4) for m in range(2) for j in range(4 * I + 4)]
    NIT = len(iters)
    LA = 3
    meta = [None] * NIT
    hstate = {}

    def head_setup(h):
        par = h % 2
        (QA, qab), (QB, qbb), (KA, kab), (KB, kbb) = QK8[par]
        sg, sgb = SG2[par]
        S.dma(QA[0:64, :], qT_d[h][0:64, :], reads=[qkB], writes=[qab])
        S.dma(QB[0:64, :], qT_d[h][64:128, :], reads=[qkB], writes=[qbb])
        S.dma(KA[0:64, :], kT_d[h][0:64, :], reads=[qkB], writes=[kab])
        S.dma(KB[0:64, :], kT_d[h][64:128, :], reads=[qkB], writes=[kbb])
        S.dma(sg[:, :], sgT_d[h], reads=[sgdB], writes=[sgb])
        S.dma(QA[64:65, :], T["c_shrow"][h:h + 1, :], writes=[qab], eng="pool")
        S.dma(QB[64:65, :], T["c_shrow"][h:h + 1, :], writes=[qbb], eng="pool")
        bias = biasB2[:, par * 32:(par + 1) * 32]
        for m in range(2):
            ts(S, "dve", bias[:, m * 16:(m + 1) * 16], tab[:, h * 16:(h + 1) * 16], Bhm[:, h * 2 + m:h * 2 + m + 1], None,
               ALU.subtract, None, [tabB, n2B], [bbB[par]])
        hstate[h] = ([(QA, qab), (QB, qbb)], [(KA, kab), (KB, kbb)], sg, sgb, bias)

    def emit_qk(n):
        h, I, m, j = iters[n]
        if h not in hstate:
            head_setup(h)
        Qs, Ks, sg, sgb, bias = hstate[h]
        qT, qb = Qs[m]
        kT, kb = Ks[m]
        qs = I * 512
        b0 = max(0, j - 4 * I)
        c0 = b0 * 128
        pss, pssb = PS[4 + n % 4]
        PT, PTb = PTt[n % 4], PTB[n % 4]
        meta[n] = (PT, PTb, b0)
        diag = j >= 4 * I
        l0, r0 = kT[:, j * 128:(j + 1) * 128], qT[:, qs + c0:qs + 512]

        def fn(e):
            ins = e.matmul(pss[:, c0:512], lhsT=l0, rhs=r0, start=True, stop=not diag)
            if diag:
                ins = e.matmul(pss[:, c0:c0 + 128], lhsT=cx.ident[:, :], rhs=nmask[:, :], start=False, stop=True)
            return ins
        S.op("pe", fn, [kb, qb, cB, cx.constb], [pssb])
        off = j - 4 * I + 12
        act(S, PT[:, c0:512], pss[:, c0:512], AF.Exp, [pssb, bbB[h % 2]], [PTb], bias=bias[:, m * 16 + off:m * 16 + off + 1])

    started = set()
    pendingB = []

    def epiA(h, I, ep):
        ov, ovb = ovs[ep % 2], ovB[ep % 2]
        for bank in range(4):
            pst_, pstb_ = PS[bank]
            S.op("dve", lambda e, bank=bank, pst_=pst_, rsum=rsum: e.reciprocal(
                out=rsum[:, 2 * bank:2 * bank + 2], in_=v3(pst_[:, :], 2)[:, :, 128]), [pstb_], [rsB])
        tt(S, "dve", rsum[:, 4:8], rsum[:, 4:8], nlam[:, 0:1].broadcast_to([128, 4]), ALU.mult, [rsB, lamB], [rsB])
        for bb in range(2):
            tt(S, "dve", v3(t2, 4)[:, 2 * bb:2 * bb + 2, :], v3(PS[2 + bb][0][:, :], 2)[:, :, 0:128],
               rsum[:, 4 + 2 * bb:6 + 2 * bb].unsqueeze(2).broadcast_to([128, 2, 128]), ALU.mult, [PS[2 + bb][1], rsB], [t2B])
            tt(S, "dve", v3(ov, 4)[:, 2 * bb:2 * bb + 2, :], v3(PS[bb][0][:, :], 2)[:, :, 0:128],
               rsum[:, 2 * bb:2 + 2 * bb].unsqueeze(2).broadcast_to([128, 2, 128]), ALU.mult, [PS[bb][1], rsB], [ovb])
        tt(S, "pool", ov, ov, t2, ALU.add, [ovb, t2B], [ovb])

    def epiB(h, I, ep):
        ov, ovb = ovs[ep % 2], ovB[ep % 2]
        sg, sgb = SG2[h % 2]
        qs = I * 512
        tt(S, "pool", junk, ov, ov, ALU.mult, [ovb], [junkB])
        S.op("dve", lambda e, msq=msq, jv=v3(junk, 4): e.tensor_reduce(out=msq[:], in_=jv, axis=AX.X, op=ALU.add), [junkB], [msB])
        act(S, msq[:], msq[:], AF.Ln, [msB], [msB], scale=1.0 / 128, bias=EPS)
        act(S, msq[:], msq[:], AF.Exp, [msB], [msB], scale=-0.5)
        tt(S, "dve", v3(ov, 4), v3(ov, 4), msq[:].unsqueeze(2).broadcast_to([128, 4, 128]), ALU.mult, [ovb, msB], [ovb])
        tt(S, "dve", v3(onbf[:, :], 4), v3(ov, 4), gsub[:, :].unsqueeze(1).broadcast_to([128, 4, 128]), ALU.mult, [ovb, gsB], [onB])
        pst, pstb = PS[7]
        pstv = pst[:, :].bitcast(BF16)
        for b_ in range(4):
            transp(S, pstv[:, b_ * 128:(b_ + 1) * 128], onbf[:, b_ * 128:(b_ + 1) * 128], cx.ident[:, :], [onB, cx.constb], [pstb])
        tt(S, "dve", oT[:, h, qs:qs + 512], pstv[:, 0:512], sg[:, qs:qs + 512], ALU.mult, [pstb, sgb], [hTb[h]])

    def emit_pv(n):
        h, I, m, j = iters[n]
        PT, PTb, b0 = meta[n]
        if m == 0 and j == 0:
            started.clear()
        for b_ in range(b0, 4):
            bank = m * 2 + b_ // 2
            acc = PS[bank][0][:, (b_ % 2) * 256:(b_ % 2) * 256 + 129]
            st_ = bank not in started
            started.add(bank)
            S.op("pe", lambda e, acc=acc, PT=PT, b_=b_, j=j, h=h, st_=st_, last=(j == 4 * I + b_): e.matmul(
                acc, lhsT=PT[:, b_ * 128:(b_ + 1) * 128], rhs=Vt[:, j, h, 0:129], start=st_, stop=last, skip_group_check=True),
                [PTb, VB[j]], [PS[bank][1]])
        ep = h * 4 + I
        if m == 1 and j == 4 * I + 3:
            epiA(h, I, ep)
            pendingB.append((h, I, ep))
        elif m == 0 and j == min(2, 4 * I + 3) and pendingB:
            epiB(*pendingB.pop(0))
        if I == 0 and m == 0 and j == 2 and h + 1 < 16 and (h + 1) not in hstate:
            head_setup(h + 1)

    for n in range(NIT + LA):
        if n < NIT:
            emit_qk(n)
        if n - LA >= 0:
            emit_pv(n - LA)
    while pendingB:
        epiB(*pendingB.pop(0))

    if STOP == "p3":
        return
    fence(cx, attsub, attwhole)
    x2T = T["x2T"]; x2B = Buf("x2T")
    r2f = cx.R2[:, :].bitcast(F32)
    NRES = 8
    slots = [(r2f[:, i * 2048:(i + 1) * 2048], Buf("x2s%d" % i)) for i in range(NRES)]
    fence(cx, VB, [b_ for _, b_ in slots])

    def ld_x1(dt_):
        xt_, xb_ = F[dt_ % 2]
        S.dma(xt_[:, 0:L], x1T[dt_], reads=[cx.x1Tb], writes=[xb_])
    ld_x1(0)
    for dt_ in range(16):
        wt, wb = load_w(cx, dt_, T["l1_wout"][dt_], 2048)
        wv = wt[:, 0:2048].rearrange("p (k c) -> p k c", k=16)
        if dt_ < 15:
            ld_x1(dt_ + 1)
        xt, xb = F[dt_ % 2]
        ot, otb = slots[dt_] if dt_ < NRES else F[2]
        for tc in range(4):
            tsl = slice(tc * 512, (tc + 1) * 512)
            ps, pb = PS[tc]
            mmg(S, ps[:, :], [(wv[:, kt, :], oT[:, kt, tsl]) for kt in range(16)], [wb] + hTb, [pb])
            tt(S, "dve", ot[:, tsl], ps[:, :], xt[:, tsl], ALU.add, [pb, xb], [otb])
        if dt_ >= NRES:
            S.dma(x2T[dt_], ot[:, 0:L], reads=[otb], writes=[x2B])
        act(S, xt[:, 0:L], ot[:, 0:L], AF.Square, [otb, xb], [xb])
        for tc in range(4):
            ps, pb = PS[4 + tc]
            S.op("pe", lambda e, ps=ps, xt=xt, tc=tc, dt_=dt_: e.matmul(
                ps[:, :], lhsT=cx.ones_f[:, :], rhs=xt[:, tc * 512:(tc + 1) * 512], start=(dt_ == 0), stop=(dt_ == 15)),
                [xb, cx.constb], [pb])
    rs, rsb = F[2]
    for tc in range(4):
        ps, pb = PS[4 + tc]
        act(S, rs[:, tc * 512:(tc + 1) * 512], ps[:, :], AF.Sqrt, [pb], [rsb], scale=1.0 / D, bias=EPS)
    recip(S, rs[:, 0:L], rs[:, 0:L], [rsb], [rsb])

    def ld_x2(dt_):
        xt_, xb_ = F[dt_ % 2]
        S.dma(xt_[:, 0:L], x2T[dt_], reads=[x2B], writes=[xb_])
    ld_x2(NRES)
    ld_x2(NRES + 1)
    for dt_ in range(NRES):
        sl, slb = slots[dt_]
        stt(S, sl, sl, gf[:, dt_:dt_ + 1], rs[:, 0:L], ALU.mult, ALU.mult, [slb, rsb, g1b], [slb])
        S.dma(T["outT"][dt_], sl, reads=[slb])
    for dt_ in range(NRES, 16):
        xt, xb = F[dt_ % 2]
        stt(S, xt[:, 0:L], xt[:, 0:L], gf[:, dt_:dt_ + 1], rs[:, 0:L], ALU.mult, ALU.mult, [xb, rsb, g1b], [xb])
        S.dma(T["outT"][dt_], xt[:, 0:L], reads=[xb])
        if dt_ + 2 < 16:
            ld_x2(dt_ + 2)


L0_IN = {
    "xT": [16, 128, 2048], "l0_g": [128, 16], "l0_cw": [128, 96], "l0_cb": [128, 24], "l0_dtb": [1, 32], "l0_alog": [1, 32],
    "l0_dsk": [1, 32], "l0_ssg": [128, 16], "l0_wxbc": [24, 128, 2048], "l0_wzb": [4, 128, 8192], "l0_wdt": [128, 512],
    "l0_wuz": [16, 128, 4096], "l0_wv": [4, 128, 8192], "l0_wsT": [128, 2048], "l0_sb": [1, 2048], "l0_lng": [1, 2048],
    "l0_lnb": [1, 2048], "l0_wout": [16, 128, 4096],
}
L1_IN = {
    "l1_g": [128, 16], "fin_g": [128, 16], "l1_lq1": [1, 64], "l1_lk1": [1, 64], "l1_lq2": [1, 64], "l1_lk2": [1, 64],
    "l1_subg": [1, 128], "c_tab": [128, 256], "c_shrow": [16, 2048], "c_negmask": [128, 128],
    "c_sel": [128, 256], "l1_wqk": [16, 128, 4096], "l1_wv": [4, 128, 8192], "l1_wg": [16, 128, 2048], "l1_wout": [16, 128, 2048],
}
C_IN = {"c_ident": [128, 128], "c_maskT": [128, 128]}
L0_SCR = {"x_tm": ([16, 128, 2048], BF16), "szb_tm": ([16, 128, 2048], BF16), "yb_tm": ([16, 128, 2048], BF16),
          "gv_tm": ([16, 128, 2048], BF16)}
L1_SCR = {"qT_d": ([16, 128, 2048], BF16), "kT_d": ([16, 128, 2048], BF16), "sgT_d": ([16, 128, 2048], BF16),
          "x2T": ([16, 128, 2048], F32)}


def build_program(mode):
    nc = bass.Bass("TRN2", target_bir_lowering=False)
    T = {}
    ins = dict(C_IN)
    if mode in ("l0", "fused"):
        ins.update(L0_IN)
    if mode in ("l1", "fused"):
        ins.update(L1_IN)
    for n, shp in ins.items():
        T[n] = nc.dram_tensor(n, shp, F32, kind="ExternalInput").ap()
    scr = {}
    if mode in ("l0", "fused"):
        scr.update(L0_SCR)
    if mode in ("l1", "fused"):
        scr.update(L1_SCR)
    for n, (shp, dt) in scr.items():
        if DBG == "smallscr" and n in ("kT_d", "sgT_d", "x2T"):
            T[n] = T["qT_d"]
            continue
        T[n] = nc.dram_tensor(n, shp, dt, kind="Internal").ap()
    if mode == "l0":
        T["x1T"] = nc.dram_tensor("x1T", [16, 128, 2048], F32, kind="ExternalOutput").ap()
    elif mode == "l1":
        T["x1T"] = nc.dram_tensor("x1T", [16, 128, 2048], F32, kind="ExternalInput").ap()
    else:
        T["x1T"] = nc.dram_tensor("x1T", [16, 128, 2048], F32, kind="Internal").ap()
    if mode in ("l1", "fused"):
        T["outT"] = nc.dram_tensor("outT", [16, 128, 2048], F32, kind="ExternalOutput").ap()

    with ExitStack() as st:
        cx = Cx()
        cx.nc = nc
        cx.S = S = Sched(nc)
        sb = lambda n, shp, dt: st.enter_context(nc.sbuf_tensor(n, shp, dt))
        cx.small = lambda n, shp: sb(n, shp, F32)
        cx.smallbf = lambda n, shp: sb(n, shp, BF16)
        R1 = sb("R1", [128, 32768], BF16)
        cx.hT = R1[:, :].rearrange("p (k t) -> p k t", k=16)
        cx.hTb = [Buf("hT%d" % k) for k in range(16)]
        cx.R2 = sb("R2", [128, 34816], BF16)
        cx.r2all = []
        cx.W = [(sb("W%d" % i, [128, 4096], BF16), Buf("W%d" % i)) for i in range(2)]
        cx.F = [(sb("F%d" % i, [128, 2056], F32), Buf("F%d" % i)) for i in range(3)]
        cx.H = [(sb("H%d" % i, [128, 2048], BF16), Buf("H%d" % i)) for i in range(3)]
        cx.J = [(cx.R2[:, 24576 + i * 2048:24576 + (i + 1) * 2048], Buf("J%d" % i)) for i in range(4)]
        cx.PS = [(st.enter_context(nc.psum_tensor("PS%d" % i, [128, 512], F32)), Buf("PS%d" % i, excl=True)) for i in range(8)]
        cx.x1Tb = Buf("x1T")
        cx.fuse_stats = (mode == "fused")
        cx.fz = sb("fz", [128, 2], F32)
        cx.fzB = Buf("fz")
        cx.constb = Buf("const")
        cx.ones_f = sb("ones_f", [128, 128], F32)
        cx.tri_f = sb("tri_f", [128, 128], F32)
        cx.ident = sb("ident", [128, 128], BF16)
        cx.maskT = sb("maskT", [128, 128], BF16)
        cx.ones_row = sb("ones_row", [1, 128], BF16)
        memset(S, "pool", cx.ones_f[:, :], 1.0, [cx.constb])
        memset(S, "pool", cx.ones_row[:, :], 1.0, [cx.constb])
        S.dma(cx.tri_f[:, :], T["c_maskT"], writes=[cx.constb])
        S.dma(cx.ident[:, :], T["c_ident"], writes=[cx.constb], eng="pool")
        S.dma(cx.maskT[:, :], T["c_maskT"], writes=[cx.constb], eng="pool")
        if mode in ("l0", "fused"):
            emit_l0(cx, T)
        if mode in ("l1", "fused"):
            emit_l1(cx, T)
        S.emit(st)
    return nc


def _fm_tiles(Wc):
    K, N = Wc.shape
    n = N // 128
    return np.ascontiguousarray(Wc.reshape(K // 128, 128, n, 128).transpose(2, 1, 0, 3)).reshape(n, 128, -1)


def _tm_blocks(Wc, bw):
    K, N = Wc.shape
    n = N // bw
    return np.ascontiguousarray(Wc.reshape(K // 128, 128, n, bw).transpose(2, 1, 0, 3)).reshape(n, 128, -1)


def _col(v, n):
    return np.ascontiguousarray(np.asarray(v, np.float32).reshape(n, 128).T)


def prep_common():
    ident = np.eye(128, dtype=np.float32)
    maskT = np.triu(np.ones((128, 128), np.float32))
    return {"c_ident": ident, "c_maskT": maskT}


def prep_l0(inp):
    f = lambda n: np.asarray(inp[n], np.float32)
    W = f("l0_w_in")
    d = {}
    d["l0_g"] = _col(f("l0_norm_g"), 16)
    cwt = f("l0_conv_w")
    d["l0_cw"] = np.ascontiguousarray(cwt.reshape(4, 24, 128).transpose(2, 1, 0)).reshape(128, 96)
    d["l0_cb"] = _col(f("l0_conv_b"), 24)
    d["l0_dtb"] = f("l0_dt_bias").reshape(1, 32)
    d["l0_alog"] = f("l0_a_log").reshape(1, 32)
    d["l0_dsk"] = f("l0_d_skip").reshape(1, 32)
    d["l0_ssg"] = _col(f("l0_ssm_norm_g"), 16)
    d["l0_wxbc"] = _fm_tiles(W[:, 8192:11264])
    d["l0_wzb"] = _tm_blocks(W[:, 6144:8192], 512)
    d["l0_wdt"] = _tm_blocks(W[:, 11264:11296], 32)[0]
    wu = _fm_tiles(W[:, 0:2048]); wz = _fm_tiles(W[:, 4096:6144])
    d["l0_wuz"] = np.ascontiguousarray(np.concatenate([wu, wz], axis=2))
    d["l0_wv"] = _tm_blocks(W[:, 2048:4096], 512)
    d["l0_wsT"] = np.ascontiguousarray(f("l0_spatial_w").transpose(2, 0, 1)).reshape(128, 2048)
    d["l0_sb"] = f("l0_spatial_b").reshape(1, 2048)
    d["l0_lng"] = f("l0_gmlp_ln_g").reshape(1, 2048)
    d["l0_lnb"] = f("l0_gmlp_ln_b").reshape(1, 2048)
    Wo = f("l0_w_out")
    d["l0_wout"] = np.ascontiguousarray(Wo.reshape(32, 128, 16, 128).transpose(2, 1, 0, 3)).reshape(16, 128, 4096)
    return d


def prep_l1(inp):
    f = lambda n: np.asarray(inp[n], np.float32)
    W = f("l1_w_in")
    d = {}
    d["l1_g"] = _col(f("l1_norm_g"), 16)
    d["fin_g"] = _col(f("final_norm_g"), 16)
    for a, b in [("l1_lq1", "l1_lambda_q1"), ("l1_lk1", "l1_lambda_k1"), ("l1_lq2", "l1_lambda_q2"), ("l1_lk2", "l1_lambda_k2")]:
        d[a] = f(b).reshape(1, 64)
    d["l1_subg"] = f("l1_subln_g").reshape(1, 128)
    slopes = (2.0 ** (-8.0 * (np.arange(16, dtype=np.float64) + 1.0) / 16)).astype(np.float32)
    kl = np.arange(128, dtype=np.float32)[:, None, None]
    off = (np.arange(16, dtype=np.float32) - 12.0)[None, None, :]
    d["c_tab"] = (slopes[None, :, None] * (kl + 128.0 * off)).astype(np.float32).reshape(128, 256)
    d["c_shrow"] = np.tile(-slopes[:, None] * np.arange(512, dtype=np.float32)[None, :], (1, 4)).astype(np.float32)
    d["c_negmask"] = np.where(np.arange(128)[:, None] > np.arange(128)[None, :], -30000.0, 0.0).astype(np.float32)
    sel = np.zeros((128, 2, 128), np.float32); sel[0:64, 0, :] = 1.0; sel[64:128, 1, :] = 1.0
    d["c_sel"] = sel.reshape(128, 256)
    wq = _fm_tiles(W[:, 0:2048]); wk = _fm_tiles(W[:, 2048:4096])
    d["l1_wqk"] = np.ascontiguousarray(np.concatenate([wq, wk], axis=2))
    d["l1_wv"] = _tm_blocks(W[:, 4096:6144], 512)
    d["l1_wg"] = _fm_tiles(W[:, 6144:8192])
    Wo = f("l1_w_out")
    d["l1_wout"] = np.ascontiguousarray(Wo.reshape(16, 128, 16, 128).transpose(2, 1, 0, 3)).reshape(16, 128, 2048)
    return d


def _xT(x):
    return [np.ascontiguousarray(x[b].T).reshape(16, 128, 2048) for b in range(x.shape[0])]


MODE = "fused"
STOP = None
DBG = None
_PROG = {}


def _prog(mode):
    if mode not in _PROG:
        _PROG[mode] = build_program(mode)
    return _PROG[mode]


def kernel(**inputs):
    x = np.asarray(inputs["x"], np.float32)
    nb = x.shape[0]
    cores = list(range(nb))
    com = prep_common()
    xT = _xT(x)
    if MODE == "fused":
        shared = dict(com); shared.update(prep_l0(inputs)); shared.update(prep_l1(inputs))
        maps = [dict(shared, xT=xT[b]) for b in range(nb)]
        res = run_bass_kernel_spmd(_prog("fused"), maps, core_ids=cores)
        outT = [r["outT"] for r in res.results]
    else:
        s0 = dict(com); s0.update(prep_l0(inputs))
        res = run_bass_kernel_spmd(_prog("l0"), [dict(s0, xT=xT[b]) for b in range(nb)], core_ids=cores)
        x1T = [np.asarray(r["x1T"]) for r in res.results]
        s1 = dict(com); s1.update(prep_l1(inputs))
        res = run_bass_kernel_spmd(_prog("l1"), [dict(s1, x1T=x1T[b]) for b in range(nb)], core_ids=cores)
        outT = [r["outT"] for r in res.results]
    out = np.stack([np.asarray(o, np.float32).reshape(2048, 2048).T for o in outT], axis=0)
    return np.ascontiguousarray(out)
```

```python
import math
from contextlib import ExitStack
import numpy as np
import concourse.bass as bass
import concourse.mybir as mybir
from concourse.bass_utils import run_bass_kernel_spmd

F32 = mybir.dt.float32
BF16 = mybir.dt.bfloat16
AF = mybir.ActivationFunctionType
ALU = mybir.AluOpType
AX = mybir.AxisListType

ENGS = ["pe", "act", "dve", "pool", "sp"]
N_DMA_SEMS = 56
N_HW_SEMS = 40
EPS = 1e-5
D = 2048
L = 2048
LAMBDA_INIT = 0.8 - 0.6 * math.exp(-0.3 * 1)


class Buf:
    __slots__ = ("name", "w", "r", "excl")

    def __init__(self, name, excl=False):
        self.name = name
        self.w = None
        self.r = []
        self.excl = excl


class Op:
    __slots__ = ("eng", "fn", "deps", "signal", "sig", "is_dma", "slot", "cnt")

    def __init__(self, eng, fn, is_dma=False):
        self.eng = eng
        self.fn = fn
        self.deps = []
        self.signal = False
        self.sig = 0
        self.is_dma = is_dma
        self.slot = -1
        self.cnt = 0


class Sched:
    def __init__(self, nc):
        self.nc = nc
        self.ops = {e: [] for e in ENGS}
        self.n_dma = 0
        self.n_sw = 0
        self.slot_last = [None] * N_DMA_SEMS
        self.slot_uses = [0] * N_DMA_SEMS

    def _track(self, o, reads, writes):
        ex = [b for b in reads if b.excl]
        if ex:
            reads = [b for b in reads if not b.excl]
            writes = list(writes) + ex
        deps = []
        for b in reads:
            if b.w is not None:
                deps.append(b.w)
        for b in writes:
            if b.w is not None:
                deps.append(b.w)
            deps.extend(b.r)
        seen = set()
        for d in deps:
            if d is o or id(d) in seen:
                continue
            seen.add(id(d))
            if d.eng == "pe" and o.eng == "pe" and not d.is_dma and not o.is_dma:
                continue
            o.deps.append(d)
            d.signal = True
        for b in reads:
            if not o.is_dma:
                b.r = [x for x in b.r if x.is_dma or x.eng != o.eng]
            b.r.append(o)
        for b in writes:
            b.w = o
            b.r = []

    def op(self, eng, fn, reads=(), writes=()):
        o = Op(eng, fn)
        self._track(o, reads, writes)
        self.ops[eng].append(o)
        return o

    def dma(self, out, in_, reads=(), writes=(), eng="sp"):
        def fn(e, out=out, in_=in_):
            return e.dma_start(out=out, in_=in_)
        o = Op(eng, fn, is_dma=True)
        if eng == "pool":
            slot = N_HW_SEMS + self.n_sw % (N_DMA_SEMS - N_HW_SEMS)
            self.n_sw += 1
        else:
            slot = self.n_dma % N_HW_SEMS
            self.n_dma += 1
        o.slot = slot
        self.slot_uses[slot] += 1
        o.cnt = 16 * self.slot_uses[slot]
        prev = self.slot_last[slot]
        self._track(o, reads, writes)
        if prev is not None and all(d is not prev for d in o.deps):
            o.deps.append(prev)
        self.slot_last[slot] = o
        o.signal = True
        self.ops[eng].append(o)
        return o

    def emit(self, stack):
        nc = self.nc
        esem = {e: stack.enter_context(nc.semaphore("s_" + e)) for e in ENGS}
        dsem = [stack.enter_context(nc.semaphore("d%d" % i)) for i in range(N_DMA_SEMS)]
        for e in ENGS:
            c = 0
            for o in self.ops[e]:
                if o.is_dma:
                    continue
                if o.signal:
                    c += 1
                    o.sig = c
        block = stack.enter_context(nc.Block())

        def run(e, eng):
            waited = {}
            for o in self.ops[e]:
                for d in o.deps:
                    if d.is_dma:
                        key, val, sem = ("d", d.slot), d.cnt, dsem[d.slot]
                    else:
                        key, val, sem = ("e", d.eng), d.sig, esem[d.eng]
                    if waited.get(key, 0) >= val:
                        continue
                    waited[key] = val
                    eng.wait_ge(sem, val)
                ins = o.fn(eng)
                if o.is_dma:
                    ins.then_inc(dsem[o.slot], 16)
                elif o.signal:
                    ins.then_inc(esem[e], 1)
            if e == "sp":
                for s in range(N_DMA_SEMS):
                    if self.slot_uses[s]:
                        eng.wait_ge(dsem[s], 16 * self.slot_uses[s])

        @block.tensor
        def _(eng):
            run("pe", eng)

        @block.scalar
        def _(eng):
            run("act", eng)

        @block.vector
        def _(eng):
            run("dve", eng)

        @block.gpsimd
        def _(eng):
            run("pool", eng)

        @block.sync
        def _(eng):
            run("sp", eng)


class Cx:
    pass


def act(S, out, in_, func, reads, writes, **kw):
    S.op("act", lambda e: e.activation(out=out, in_=in_, func=func, **kw), reads, writes)


def tt(S, eng, out, in0, in1, op, reads, writes):
    S.op(eng, lambda e: e.tensor_tensor(out=out, in0=in0, in1=in1, op=op), reads, writes)


def ts(S, eng, out, in0, s1, s2, op0, op1, reads, writes):
    if op1 is None:
        S.op(eng, lambda e: e.tensor_scalar(out=out, in0=in0, scalar1=s1, scalar2=None, op0=op0), reads, writes)
    else:
        S.op(eng, lambda e: e.tensor_scalar(out=out, in0=in0, scalar1=s1, scalar2=s2, op0=op0, op1=op1), reads, writes)


def stt(S, out, in0, scalar, in1, op0, op1, reads, writes):
    S.op("dve", lambda e: e.scalar_tensor_tensor(out=out, in0=in0, scalar=scalar, in1=in1, op0=op0, op1=op1), reads, writes)


def cp(S, eng, out, in_, reads, writes):
    if eng == "act":
        S.op("act", lambda e: e.activation(out=out, in_=in_, func=AF.Copy), reads, writes)
    else:
        S.op(eng, lambda e: e.tensor_copy(out=out, in_=in_), reads, writes)


def recip(S, out, in_, reads, writes):
    S.op("dve", lambda e: e.reciprocal(out=out, in_=in_), reads, writes)


def memset(S, eng, ap, val, writes):
    S.op(eng, lambda e: e.memset(ap, val), (), writes)


def mmg(S, out, pairs, reads, writes, start=True, stop=True):
    n = len(pairs)

    def fn(e):
        ins = None
        for i, (l, r) in enumerate(pairs):
            ins = e.matmul(out, lhsT=l, rhs=r, start=(start and i == 0), stop=(stop and i == n - 1))
        return ins
    S.op("pe", fn, reads, writes)


def transp(S, out, in_, ident, reads, writes):
    S.op("pe", lambda e: e.transpose(out, in_, ident), reads, writes)


def bc(ap, shape):
    return ap.broadcast_to(shape)


def fence(cx, olds, news):
    cx.S.op("pool", lambda e: e.memset(cx.fz[:, 0:1], 0.0), (), list(olds) + list(news) + [cx.fzB])


def emit_rmsnorm_fm(cx, xT, gcol, gB, tag, have_rs=False, xdep=()):
    S, F, PS = cx.S, cx.F, cx.PS
    for kt in range(0 if have_rs else 16):
        xt, xb = F[kt % 2]
        S.dma(xt[:, 0:L], xT[kt], writes=[xb])
        sq, sqb = F[2]
        act(S, sq[:, 0:L], xt[:, 0:L], AF.Square, [xb], [sqb])
        for tc in range(4):
            ps, pb = PS[tc]
            S.op("pe", lambda e, ps=ps, sq=sq, tc=tc, kt=kt: e.matmul(
                ps[:, :], lhsT=cx.ones_f[:, :], rhs=sq[:, tc * 512:(tc + 1) * 512], start=(kt == 0), stop=(kt == 15)),
                [sqb, cx.constb], [pb])
    rs, rsb = F[2]
    for tc in range(0 if have_rs else 4):
        ps, pb = PS[tc]
        act(S, rs[:, tc * 512:(tc + 1) * 512], ps[:, :], AF.Sqrt, [pb], [rsb], scale=1.0 / D, bias=EPS)
    if not have_rs:
        recip(S, rs[:, 0:L], rs[:, 0:L], [rsb], [rsb])
    for kt in range(16):
        xt, xb = F[kt % 2]
        S.dma(xt[:, 0:L], xT[kt], reads=list(xdep), writes=[xb])
        stt(S, cx.hT[:, kt, :], xt[:, 0:L], gcol[:, kt:kt + 1], rs[:, 0:L], ALU.mult, ALU.mult,
            [xb, rsb, gB], [cx.hTb[kt]])


def load_w(cx, i, src, n):
    wt, wb = cx.W[i % 2]
    cx.S.dma(wt[:, 0:n], src, writes=[wb], eng="pool")
    return wt, wb


def tm_proj512(cx, wsrc, consume):
    S, PS = cx.S, cx.PS
    f0v = cx.F[0][0][:, :].bitcast(BF16)
    f1v = cx.F[1][0][:, :].bitcast(BF16)
    sets = [[(cx.W[0][0][:, 0:4096], cx.W[0][1]), (cx.W[1][0][:, 0:4096], cx.W[1][1])],
            [(f0v[:, 0:4096], cx.F[0][1]), (f1v[:, 0:4096], cx.F[1][1])]]
    for blk in range(4):
        halves = []
        for hf in range(2):
            wt, wb = sets[blk % 2][hf]
            S.dma(wt, wsrc[blk][:, hf * 4096:(hf + 1) * 4096], writes=[wb], eng="pool")
            halves.append((wt.rearrange("p (k c) -> p k c", k=8), wb))
        for t in range(16):
            ps, pb = PS[t % 8]
            pairs = [(cx.hT[:, kt, t * 128:(t + 1) * 128], halves[kt // 8][0][:, kt % 8, :]) for kt in range(16)]
            mmg(S, ps[:, :], pairs, [halves[0][1], halves[1][1]] + cx.hTb, [pb])
            consume(blk, t, ps, pb)


def emit_l0(cx, T):
    S, F, H, J, PS, W = cx.S, cx.F, cx.H, cx.J, cx.PS, cx.W
    nc = cx.nc
    hT, hTb = cx.hT, cx.hTb
    sm = cx.small

    g0 = sm("g0", [128, 16]); g0b = Buf("g0")
    S.dma(g0[:], T["l0_g"], writes=[g0b])
    cw = sm("cw", [128, 96]); cb_ = sm("cb", [128, 24]); cwb = Buf("cw")
    S.dma(cw[:], T["l0_cw"], writes=[cwb])
    S.dma(cb_[:], T["l0_cb"], writes=[cwb])
    dtb = sm("dtb", [128, 32]); a_b = sm("a_b", [128, 32]); dsk = sm("dsk", [128, 32]); pb3 = Buf("p3")
    S.dma(dtb[:], bc(T["l0_dtb"], [128, 32]), writes=[pb3])
    S.dma(a_b[:], bc(T["l0_alog"], [128, 32]), writes=[pb3])
    S.dma(dsk[:], bc(T["l0_dsk"], [128, 32]), writes=[pb3])
    act(S, a_b[:], a_b[:], AF.Exp, [pb3], [pb3])
    ts(S, "dve", a_b[:], a_b[:], -1.0, None, ALU.mult, None, [pb3], [pb3])
    ssg = sm("ssg", [128, 16]); ssgb = Buf("ssg")
    S.dma(ssg[:], T["l0_ssg"], writes=[ssgb])

    emit_rmsnorm_fm(cx, T["xT"], g0, g0b, "l0")

    if STOP == "a1":
        return
    R2 = cx.R2
    BT = R2[:, 0:8192].rearrange("p (g t) -> p g t", g=4)
    CT = R2[:, 8192:16384].rearrange("p (g t) -> p g t", g=4)
    Btm = R2[:, 16384:24576].rearrange("p (c g n) -> p c g n", c=16, g=4)
    BTb = [Buf("BT%d" % g) for g in range(4)]
    CTb = [Buf("CT%d" % g) for g in range(4)]
    Btmb = Buf("Btm")
    x_tm, szb_tm, yb_tm, gv_tm = T["x_tm"], T["szb_tm"], T["yb_tm"], T["gv_tm"]
    x_tmb, szb_tmb, yb_tmb, gv_tmb = Buf("x_tm"), Buf("szb_tm"), Buf("yb_tm"), Buf("gv_tm")

    for i in range(2):
        memset(S, "pool", F[i][0][:, 0:3], 0.0, [F[i][1]])
    p2tail = []
    for ft in range(24):
        wt, wb = load_w(cx, ft, T["l0_wxbc"][ft], 2048)
        wv = wt[:, 0:2048].rearrange("p (k c) -> p k c", k=16)
        xpad, xpb = F[ft % 2]
        for tc in range(4):
            ps, pb = PS[(ft * 4 + tc) % 4]
            mmg(S, ps[:, :], [(wv[:, kt, :], hT[:, kt, tc * 512:(tc + 1) * 512]) for kt in range(16)],
                [wb] + hTb, [pb])
            cp(S, "act", xpad[:, 3 + tc * 512:3 + (tc + 1) * 512], ps[:, :], [pb], [xpb])
        while p2tail:
            p2tail.pop(0)()
        acc, accb = F[2]
        ts(S, "dve", acc[:, 0:L], xpad[:, 0:L], cw[:, ft * 4:ft * 4 + 1], None, ALU.mult, None, [xpb, cwb], [accb])
        for k in range(1, 4):
            stt(S, acc[:, 0:L], xpad[:, k:k + L], cw[:, ft * 4 + k:ft * 4 + k + 1], acc[:, 0:L], ALU.mult, ALU.add,
                [xpb, cwb, accb], [accb])
        if ft < 16:
            xc, xcb = H[ft % 2]
            act(S, xc[:, :], acc[:, 0:L], AF.Silu, [accb, cwb], [xcb], bias=cb_[:, ft:ft + 1])

            def tail(ft=ft, xc=xc, xcb=xcb):
                stg, stgb = H[2]
                for half in range(2):
                    ps, pb = PS[4 + half]
                    psv = ps[:, :].bitcast(BF16)
                    for kk in range(8):
                        c = half * 8 + kk
                        transp(S, psv[:, kk * 128:(kk + 1) * 128], xc[:, c * 128:(c + 1) * 128], cx.ident[:, :],
                               [xcb, cx.constb], [pb])
                    cp(S, "dve", stg[:, half * 1024:(half + 1) * 1024], psv[:, 0:1024], [pb], [stgb])
                S.dma(x_tm.rearrange("c p n -> p c n")[:, :, ft * 128:(ft + 1) * 128],
                      stg[:, :].rearrange("p (c n) -> p c n", c=16), reads=[stgb], writes=[x_tmb])
            p2tail.append(tail)
        elif ft < 20:
            g = ft - 16
            act(S, BT[:, g, :], acc[:, 0:L], AF.Silu, [accb, cwb], [BTb[g]], bias=cb_[:, ft:ft + 1])

            def tail(g=g):
                for half in range(2):
                    ps, pb = PS[4 + half]
                    psv = ps[:, :].bitcast(BF16)
                    for kk in range(8):
                        c = half * 8 + kk
                        transp(S, psv[:, kk * 128:(kk + 1) * 128], BT[:, g, c * 128:(c + 1) * 128], cx.ident[:, :],
                               [BTb[g], cx.constb], [pb])
                    cp(S, "dve", Btm[:, half * 8:(half + 1) * 8, g, :], psv[:, 0:1024].rearrange("p (c n) -> p c n", c=8),
                       [pb], [Btmb])
            p2tail.append(tail)
        else:
            g = ft - 20
            act(S, CT[:, g, :], acc[:, 0:L], AF.Silu, [accb, cwb], [CTb[g]], bias=cb_[:, ft:ft + 1])
    while p2tail:
        p2tail.pop(0)()

    if STOP == "a2":
        return
    def zb_out(blk, t, ps, pb):
        stg, stgb = H[(t // 4) % 2]
        sl = stg[:, (t % 4) * 512:(t % 4 + 1) * 512]
        act(S, sl, ps[:, :], AF.Silu, [pb], [stgb])
        S.dma(szb_tm[t][:, blk * 512:(blk + 1) * 512], sl, reads=[stgb], writes=[szb_tmb])
    tm_proj512(cx, T["l0_wzb"], zb_out)
    dt_tm = sm("dt_tm", [128, 16, 32]); adt_tm = sm("adt_tm", [128, 16, 32]); dtB = Buf("dt")
    wt, wb = load_w(cx, 0, T["l0_wdt"], 512)
    wv = wt[:, 0:512].rearrange("p (k c) -> p k c", k=16)
    for t in range(16):
        ps, pb = PS[t % 4]
        mmg(S, ps[:, 0:32], [(hT[:, kt, t * 128:(t + 1) * 128], wv[:, kt, :]) for kt in range(16)], [wb] + hTb, [pb])
        tt(S, "dve", dt_tm[:, t, :], ps[:, 0:32], dtb[:], ALU.add, [pb, pb3], [dtB])
    act(S, dt_tm[:], dt_tm[:], AF.Exp, [dtB], [dtB])
    act(S, dt_tm[:], dt_tm[:], AF.Ln, [dtB], [dtB], bias=1.0)
    tt(S, "dve", adt_tm[:], dt_tm[:], a_b[:].unsqueeze(1).broadcast_to([128, 16, 32]), ALU.mult, [dtB, pb3], [dtB])

    if STOP == "a3":
        return
    prev, prevb_all = F[0]
    prevv = prev[:, 0:2048].rearrange("p (g n) -> p g n", g=4)
    memset(S, "pool", prev[:, 0:2048], 0.0, [prevb_all])
    prevB = [Buf("prev%d" % g) for g in range(4)]
    pbf_t, pbf_allb = W[1]
    pbf = pbf_t[:, 0:2048].rearrange("p (g n) -> p g n", g=4)
    memset(S, "pool", pbf_t[:, 0:2048], 0.0, [pbf_allb])
    pbfB = [Buf("pbf%d" % g) for g in range(4)]
    Mt_t, Mt_allb = W[0]
    Mt = [Mt_t[:, i * 1024:(i + 1) * 1024].rearrange("p (r l) -> p r l", r=8) for i in range(2)]
    MtB = [Buf("Mt0"), Buf("Mt1")]
    CBm = Mt_t[:, 2048:2560].rearrange("p (g l) -> p g l", g=4); CBmB = Buf("CBm")
    Et = [Mt_t[:, 2560 + i * 512:2560 + (i + 1) * 512].rearrange("p (r l) -> p r l", r=4) for i in range(2)]
    EtB = [Buf("Et0"), Buf("Et1")]
    f1, f1b_all = F[1]
    f2, f2b_all = F[2]
    t1s = [f1[:, 0:512], f1[:, 512:1024]]; t2s = [f1[:, 1024:1536], f1[:, 1536:2048]]
    yvs = [f2[:, 1024:1536], f2[:, 1536:2048]]
    junk = cx.R2[:, 32768:33280]
    t1Bs, t2Bs, yvBs = [Buf("t1a"), Buf("t1b")], [Buf("t2a"), Buf("t2b")], [Buf("yva"), Buf("yvb")]
    junkB = Buf("junk")
    seg = [f2[:, i * 512:(i + 1) * 512].rearrange("p (r l) -> p r l", r=4) for i in range(2)]
    segB = [Buf("seg0"), Buf("seg1")]
    acum = sm("acum", [128, 32]); nacum = sm("nacum", [128, 32]); alast = sm("alast", [128, 32])
    ea = sm("ea", [128, 32]); cdb = sm("cdb", [128, 32]); dte = sm("dte", [128, 32]); ss = sm("ss", [128, 4])
    rr = sm("rr", [128, 4])
    smB = Buf("ssd_small")
    ssBs = [Buf("ssd_ss0"), Buf("ssd_ss1")]
    p4sub = prevB + pbfB + MtB + [CBmB] + EtB + t1Bs + t2Bs + yvBs + segB
    p4whole = [prevb_all, pbf_allb, Mt_allb, f1b_all, f2b_all]
    fence(cx, p4whole, p4sub)
    xdt, xdtB = J[1]
    xdts, xdtsB = J[2]
    ybt, ybB = J[3]
    cnt = 0
    def p4_load(c):
        xc, xcb = H[c % 2]
        S.dma(xc[:, :], x_tm[c], reads=[x_tmb], writes=[xcb])
        zc, zcb = (H[2] if c % 2 == 0 else J[0])
        S.dma(zc[:, :], szb_tm[c], reads=[szb_tmb], writes=[zcb])
    p4_load(0)
    for c in range(16):
        cs = slice(c * 128, (c + 1) * 128)
        xc, xcb = H[c % 2]
        zc, zcb = (H[2] if c % 2 == 0 else J[0])
        if c < 15:
            p4_load(c + 1)
        psa, psab = PS[4]
        mmg(S, psa[:, 0:32], [(cx.tri_f[:, :], adt_tm[:, c, :])], [dtB, cx.constb], [psab])
        mmg(S, psa[:, 32:64], [(cx.ones_f[:, :], adt_tm[:, c, :])], [dtB, cx.constb], [psab])
        cp(S, "dve", acum[:], psa[:, 0:32], [psab], [smB])
        cp(S, "dve", alast[:], psa[:, 32:64], [psab], [smB])
        ts(S, "dve", nacum[:], acum[:], -1.0, None, ALU.mult, None, [smB], [smB])
        act(S, ea[:], acum[:], AF.Exp, [smB], [smB])
        act(S, cdb[:], alast[:], AF.Exp, [smB], [smB])
        tt(S, "dve", dte[:], alast[:], acum[:], ALU.subtract, [smB], [smB])
        act(S, dte[:], dte[:], AF.Exp, [smB], [smB])
        psc, pscb = PS[5]
        for g in range(4):
            mmg(S, psc[:, g * 128:(g + 1) * 128], [(BT[:, g, cs], CT[:, g, cs])], [BTb[g], CTb[g]], [pscb])
        tt(S, "dve", CBm, psc[:, :].rearrange("p (g l) -> p g l", g=4),
           cx.maskT[:, :].unsqueeze(1).broadcast_to([128, 4, 128]), ALU.mult, [pscb, cx.constb], [CBmB])
        tt(S, "dve", xdt[:, :].rearrange("p (h d) -> p h d", h=32), xc[:, :].rearrange("p (h d) -> p h d", h=32),
           dt_tm[:, c, :].unsqueeze(2).broadcast_to([128, 32, 64]), ALU.mult, [xcb, dtB], [xdtB])
        tt(S, "pool", xdts[:, :].rearrange("p (h d) -> p h d", h=32), xdt[:, :].rearrange("p (h d) -> p h d", h=32),
           dte[:].unsqueeze(2).broadcast_to([128, 32, 64]), ALU.mult, [xdtB, smB], [xdtsB])
        def stage1(g, c=c, cs=cs):
            M, MB = Mt[g % 2], MtB[g % 2]
            for half in range(2):
                psA, psAb = PS[6 + half]
                h0 = g * 8 + half * 4
                for hh in range(4):
                    h = h0 + hh
                    mmg(S, psA[:, hh * 128:(hh + 1) * 128],
                        [(adt_tm[:, c, h:h + 1].broadcast_to([128, 128]), cx.tri_f[:, :])], [dtB, cx.constb], [psAb])
                sg_, sgB_ = seg[half], segB[half]
                tt(S, "dve", sg_, psA[:, :].rearrange("p (r l) -> p r l", r=4),
                   acum[:, h0:h0 + 4].unsqueeze(2).broadcast_to([128, 4, 128]), ALU.min, [psAb, smB], [sgB_])
                E, EB = Et[half], EtB[half]
                for hh in range(4):
                    h = h0 + hh
                    act(S, E[:, hh, :], sg_[:, hh, :], AF.Exp, [sgB_, smB], [EB], bias=nacum[:, h:h + 1])
                tt(S, "pool", M[:, half * 4:(half + 1) * 4, :], E, CBm[:, g:g + 1, :].broadcast_to([128, 4, 128]),
                   ALU.mult, [EB, CBmB], [MB])

        def stage2(g, c=c, cs=cs, xc=xc, xcb=xcb, zc=zc, zcb=zcb):
            M, MB = Mt[g % 2], MtB[g % 2]
            psy, psyb = PS[g % 2]
            for r in range(8):
                h = g * 8 + r
                mmg(S, psy[:, r * 64:(r + 1) * 64], [(M[:, r, :], xdt[:, h * 64:(h + 1) * 64])], [MB, xdtB], [psyb])
            pso, psob = PS[2]
            mmg(S, pso[:, :], [(CT[:, g, cs], pbf[:, g, :])], [CTb[g], pbfB[g]], [psob])
            pss, pssb = PS[3]
            mmg(S, pss[:, :], [(Btm[:, c, g, :], xdts[:, g * 512:(g + 1) * 512])], [Btmb, xdtsB], [pssb])
            gsl = slice(g * 512, (g + 1) * 512)
            t1, t2, yv = t1s[g % 2], t2s[g % 2], yvs[g % 2]
            t1B, t2B, yvB, ssB = t1Bs[g % 2], t2Bs[g % 2], yvBs[g % 2], ssBs[g % 2]
            tt(S, "pool", t2.rearrange("p (r d) -> p r d", r=8), xc[:, gsl].rearrange("p (r d) -> p r d", r=8),
               dsk[:, g * 8:(g + 1) * 8].unsqueeze(2).broadcast_to([128, 8, 64]), ALU.mult, [xcb, pb3], [t2B])
            tt(S, "dve", t1.rearrange("p (r d) -> p r d", r=8), pso[:, :].rearrange("p (r d) -> p r d", r=8),
               ea[:, g * 8:(g + 1) * 8].unsqueeze(2).broadcast_to([128, 8, 64]), ALU.mult, [psob, smB], [t1B])
            tt(S, "dve", t1, t1, t2, ALU.add, [t1B, t2B], [t1B])
            tt(S, "dve", yv, psy[:, :], t1, ALU.add, [psyb, t1B], [yvB])
            tt(S, "dve", yv, yv, zc[:, gsl], ALU.mult, [yvB, zcb], [yvB])
            act(S, junk, yv, AF.Square, [yvB], [junkB, ssB], scale=512.0 ** -0.5, accum_out=ss[:, g:g + 1])
            act(S, rr[:, g:g + 1], ss[:, g:g + 1], AF.Ln, [ssB], [ssB], bias=EPS)
            act(S, rr[:, g:g + 1], rr[:, g:g + 1], AF.Exp, [ssB], [ssB], scale=-0.5)
            tt(S, "dve", prevv[:, g, :].rearrange("p (r d) -> p r d", r=8), prevv[:, g, :].rearrange("p (r d) -> p r d", r=8),
               cdb[:, g * 8:(g + 1) * 8].unsqueeze(2).broadcast_to([128, 8, 64]), ALU.mult, [prevB[g], smB], [prevB[g]])
            tt(S, "dve", prevv[:, g, :], prevv[:, g, :], pss[:, :], ALU.add, [prevB[g], pssb], [prevB[g]])
            ts(S, "dve", ybt[:, gsl], yv, rr[:, g:g + 1], None, ALU.mult, None, [yvB, ssB], [ybB])
            cp(S, "act", pbf[:, g, :], prevv[:, g, :], [prevB[g]], [pbfB[g]])

        stage1(0)
        for g in range(4):
            if g < 3:
                stage1(g + 1)
            stage2(g)
        S.dma(yb_tm[c], ybt[:, :], reads=[ybB], writes=[yb_tmb])

    if STOP == "a4":
        return
    gaT = cx.R2[:, 0:32768].rearrange("p (k t) -> p k t", k=16)
    gaB = [Buf("ga%d" % k) for k in range(16)]
    r2users = BTb + CTb + [Btmb] + [J[i][1] for i in range(4)]
    fence(cx, r2users, gaB)
    fence(cx, p4sub, p4whole)
    f0, f0b = F[0]
    f1, f1b = F[1]
    for ft in range(16):
        wt, wb = load_w(cx, ft, T["l0_wuz"][ft], 4096)
        wv = wt[:, 0:4096].rearrange("p (a k c) -> p a k c", a=2, k=16)
        for tc in range(4):
            tsl = slice(tc * 512, (tc + 1) * 512)
            psu, psub = PS[(2 * tc) % 8]
            psz, pszb = PS[(2 * tc + 1) % 8]
            mmg(S, psu[:, :], [(wv[:, 0, kt, :], hT[:, kt, tsl]) for kt in range(16)], [wb] + hTb, [psub])
            mmg(S, psz[:, :], [(wv[:, 1, kt, :], hT[:, kt, tsl]) for kt in range(16)], [wb] + hTb, [pszb])
            act(S, f0[:, tsl], psu[:, :], AF.Gelu_apprx_tanh, [psub], [f0b])
            act(S, f1[:, tsl], psz[:, :], AF.Silu, [pszb], [f1b])
            tt(S, "dve", gaT[:, ft, tsl], f0[:, tsl], f1[:, tsl], ALU.mult, [f0b, f1b], [gaB[ft]])
    def gv_out(blk, t, ps, pb):
        stg, stgb = H[(t // 4) % 2]
        sl = stg[:, (t % 4) * 512:(t % 4 + 1) * 512]
        act(S, sl, ps[:, :], AF.Gelu_apprx_tanh, [pb], [stgb])
        S.dma(gv_tm[t][:, blk * 512:(blk + 1) * 512], sl, reads=[stgb], writes=[gv_tmb])
    tm_proj512(cx, T["l0_wv"], gv_out)

    if STOP == "a5":
        return
    wst_t, wstb = W[0]
    WsT = wst_t[:, 0:2048].rearrange("p (g t) -> p g t", g=16)
    S.dma(wst_t[:, 0:2048], T["l0_wsT"], writes=[wstb], eng="pool")
    tt(S, "pool", WsT, WsT, cx.maskT[:, :].unsqueeze(1).broadcast_to([128, 16, 128]), ALU.mult, [wstb, cx.constb], [wstb])
    sbB = W[1][1]
    sbhi = W[1][0][0:1, 0:2048]; sblo = W[1][0][0:1, 2048:4096]
    sbf = F[2][0][0:1, 0:2048]; sbt = F[1][0][0:1, 0:2048]
    S.dma(sbf, T["l0_sb"], writes=[F[2][1]])
    cp(S, "dve", sbhi, sbf, [F[2][1]], [sbB])
    cp(S, "dve", sbt, sbhi, [sbB], [F[1][1]])
    tt(S, "dve", sblo, sbf, sbt, ALU.subtract, [F[2][1], F[1][1]], [sbB])
    lng, lngb = F[0]
    lnb, lnbb = F[1]
    S.dma(lng[:, 0:2048], bc(T["l0_lng"], [128, 2048]), writes=[lngb])
    S.dma(lnb[:, 0:2048], bc(T["l0_lnb"], [128, 2048]), writes=[lnbb])
    tmp, tmpb = F[2]
    st6 = sm("st6", [128, 24]); mv = sm("mv", [128, 2]); lnr = sm("lnr", [128, 1]); nmr = sm("nmr", [128, 1])
    lnB = Buf("lnsmall")
    vns = [H[2], (W[0][0][:, 2048:4096], Buf("vn1"))]
    fence(cx, [W[0][1]], [vns[1][1]])

    def ln_chunk(c):
        gv, gvb = H[c % 2]
        S.dma(gv[:, :], gv_tm[c], reads=[gv_tmb], writes=[gvb])
        for q in range(4):
            S.op("dve", lambda e, q=q, gv=gv: e.bn_stats(out=st6[:, q * 6:(q + 1) * 6], in_=gv[:, q * 512:(q + 1) * 512]),
                 [gvb], [lnB])
        S.op("dve", lambda e: e.bn_aggr(out=mv[:], in_=st6[:]), [lnB], [lnB])
        act(S, lnr[:], mv[:, 1:2], AF.Sqrt, [lnB], [lnB], bias=EPS)
        recip(S, lnr[:], lnr[:], [lnB], [lnB])
        ts(S, "dve", nmr[:], mv[:, 0:1], lnr[:, 0:1], -1.0, ALU.mult, ALU.mult, [lnB], [lnB])
        act(S, tmp[:, 0:2048], gv[:, :], AF.Identity, [gvb, lnB], [tmpb], scale=lnr[:, 0:1], bias=nmr[:, 0:1])
        tt(S, "dve", tmp[:, 0:2048], tmp[:, 0:2048], lng[:, 0:2048], ALU.mult, [tmpb, lngb], [tmpb])
        vn, vnb = vns[c % 2]
        tt(S, "pool", vn[:, :], tmp[:, 0:2048], lnb[:, 0:2048], ALU.add, [tmpb, lnbb], [vnb])

    ln_chunk(0)
    for c in range(16):
        cs = slice(c * 128, (c + 1) * 128)
        if c < 15:
            ln_chunk(c + 1)
        vn, vnb = vns[c % 2]
        for gq in range(4):
            ps, pb = PS[(c * 4 + gq) % 8]
            for gi in range(4):
                g = gq * 4 + gi
                gl = slice(g * 128, (g + 1) * 128)
                mmg(S, ps[:, gi * 128:(gi + 1) * 128],
                    [(vn[:, gl], WsT[:, g, :]), (cx.ones_row[0:1, :], sbhi[0:1, gl]), (cx.ones_row[0:1, :], sblo[0:1, gl])],
                    [vnb, wstb, sbB, cx.constb], [pb])
            tt(S, "dve", gaT[:, gq * 4:(gq + 1) * 4, cs], ps[:, :].rearrange("p (g t) -> p g t", g=4),
               gaT[:, gq * 4:(gq + 1) * 4, cs], ALU.mult, [pb] + gaB[gq * 4:(gq + 1) * 4], gaB[gq * 4:(gq + 1) * 4])
    fence(cx, [vns[1][1]], [W[0][1]])

    if STOP == "a6":
        return
    ybT = cx.hT
    for c in range(16):
        cs = slice(c * 128, (c + 1) * 128)
        yb, ybb = H[c % 2]
        S.dma(yb[:, :], yb_tm[c], reads=[yb_tmb], writes=[ybb])
        for half in range(2):
            ps, pb = PS[4 + half]
            psv = ps[:, :].bitcast(BF16)
            for kk in range(8):
                kt = half * 8 + kk
                transp(S, psv[:, kk * 128:(kk + 1) * 128], yb[:, kt * 128:(kt + 1) * 128], cx.ident[:, :], [ybb, cx.constb], [pb])
            tt(S, "dve", ybT[:, half * 8:(half + 1) * 8, cs], psv[:, 0:1024].rearrange("p (k t) -> p k t", k=8),
               ssg[:, half * 8:(half + 1) * 8].unsqueeze(2).broadcast_to([128, 8, 128]), ALU.mult,
               [pb, ssgb], hTb[half * 8:(half + 1) * 8])
    for dt_ in range(16):
        wt, wb = load_w(cx, dt_, T["l0_wout"][dt_], 4096)
        wv = wt[:, 0:4096].rearrange("p (k c) -> p k c", k=32)
        if dt_ == 0:
            S.dma(F[0][0][:, 0:L], T["xT"][0], writes=[F[0][1]])
        if dt_ < 15:
            S.dma(F[(dt_ + 1) % 2][0][:, 0:L], T["xT"][dt_ + 1], writes=[F[(dt_ + 1) % 2][1]])
        xt, xb = F[dt_ % 2]
        ot, otb = F[2]
        for tc in range(4):
            tsl = slice(tc * 512, (tc + 1) * 512)
            ps, pb = PS[(dt_ * 4 + tc) % 4]
            pairs = [(wv[:, kt, :], gaT[:, kt, tsl]) for kt in range(16)] + [(wv[:, 16 + kt, :], ybT[:, kt, tsl]) for kt in range(16)]
            mmg(S, ps[:, :], pairs, [wb] + gaB + hTb, [pb])
            tt(S, "dve", ot[:, tsl], ps[:, :], xt[:, tsl], ALU.add, [pb, xb], [otb])
        S.dma(T["x1T"][dt_], ot[:, 0:L], reads=[otb], writes=[cx.x1Tb])
        if cx.fuse_stats:
            act(S, xt[:, 0:L], ot[:, 0:L], AF.Square, [otb, xb], [xb])
            for tc in range(4):
                ps, pb = PS[4 + tc]
                S.op("pe", lambda e, ps=ps, xt=xt, tc=tc, dt_=dt_: e.matmul(
                    ps[:, :], lhsT=cx.ones_f[:, :], rhs=xt[:, tc * 512:(tc + 1) * 512], start=(dt_ == 0), stop=(dt_ == 15)),
                    [xb, cx.constb], [pb])
    if cx.fuse_stats:
        rs, rsb = F[2]
        for tc in range(4):
            ps, pb = PS[4 + tc]
            act(S, rs[:, tc * 512:(tc + 1) * 512], ps[:, :], AF.Sqrt, [pb], [rsb], scale=1.0 / D, bias=EPS)
        recip(S, rs[:, 0:L], rs[:, 0:L], [rsb], [rsb])
    cx.r2all = gaB + r2users + [junkB]


def emit_l1(cx, T):
    S, F, H, PS, W = cx.S, cx.F, cx.H, cx.PS, cx.W
    hT, hTb = cx.hT, cx.hTb
    sm = cx.small
    x1T = T["x1T"]

    g1 = sm("g1", [128, 16]); gf = sm("gf", [128, 16]); g1b = Buf("g1")
    S.dma(g1[:], T["l1_g"], writes=[g1b])
    S.dma(gf[:], T["fin_g"], writes=[g1b])
    lq = sm("lq", [128, 256]); lamB = Buf("lam")
    for i, n in enumerate(["l1_lq1", "l1_lk1", "l1_lq2", "l1_lk2"]):
        S.dma(lq[:, i * 64:(i + 1) * 64], bc(T[n], [128, 64]), writes=[lamB])
    lp = sm("lp", [128, 128]); l12 = sm("l12", [128, 2]); nlam = sm("nlam", [128, 1])
    tt(S, "dve", lp[:, 0:64], lq[:, 0:64], lq[:, 64:128], ALU.mult, [lamB], [lamB])
    tt(S, "dve", lp[:, 64:128], lq[:, 128:192], lq[:, 192:256], ALU.mult, [lamB], [lamB])
    S.op("dve", lambda e: e.tensor_reduce(out=l12[:], in_=lp[:].rearrange("p (a d) -> p a d", a=2), axis=AX.X, op=ALU.add),
         [lamB], [lamB])
    act(S, l12[:], l12[:], AF.Exp, [lamB], [lamB])
    tt(S, "dve", nlam[:], l12[:, 1:2], l12[:, 0:1], ALU.subtract, [lamB], [lamB])
    ts(S, "dve", nlam[:], nlam[:], -LAMBDA_INIT, None, ALU.add, None, [lamB], [lamB])
    if STOP == "s1":
        return
    gsub = sm("gsub", [128, 128]); gsB = Buf("gsub")
    S.dma(gsub[:], bc(T["l1_subg"], [128, 128]), writes=[gsB])
    ts(S, "dve", gsub[:], gsub[:], 1.0 - LAMBDA_INIT, None, ALU.mult, None, [gsB], [gsB])
    tab = sm("tab", [128, 256]); tabB = Buf("tab")
    S.dma(tab[:], T["c_tab"], writes=[tabB])
    if STOP == "s2":
        return
    nmask = cx.smallbf("nmask", [128, 128])
    cB = Buf("l1const")
    S.dma(nmask[:], T["c_negmask"], writes=[cB], eng="pool")
    sel = cx.smallbf("sel", [128, 256])
    S.dma(sel[:], T["c_sel"], writes=[cB], eng="pool")

    if STOP == "s3":
        return
    emit_rmsnorm_fm(cx, x1T, g1, g1b, "l1", have_rs=cx.fuse_stats, xdep=[cx.x1Tb])

    if STOP == "p1":
        return
    qT_d, kT_d, sgT_d = T["qT_d"], T["kT_d"], T["sgT_d"]
    qkB = Buf("qk_d"); sgdB = Buf("sg_d")
    n2 = sm("n2", [128, 256]); n2B = Buf("n2")
    Vt = cx.R2[:, 0:33280].rearrange("p (t h v) -> p t h v", t=16, h=16)
    VB = [Buf("V%d" % t) for t in range(16)]
    memset(S, "pool", cx.R2[:, 0:33280], 1.0, VB + cx.r2all)
    f2v_ = F[2][0][:, :].bitcast(BF16)
    SQ = [H[2], (f2v_[:, 0:2048], F[2][1])]
    qktail = []
    for h in range(16):
        wt, wb = load_w(cx, h, T["l1_wqk"][h], 4096)
        wv = wt[:, 0:4096].rearrange("p (a k c) -> p a k c", a=2, k=16)
        for a in range(2):
            stg, stgb = H[a]
            for tc in range(4):
                tsl = slice(tc * 512, (tc + 1) * 512)
                ps, pb = PS[(a * 4 + tc) % 4]
                mmg(S, ps[:, :], [(wv[:, a, kt, :], hT[:, kt, tsl]) for kt in range(16)], [wb] + hTb, [pb])
                if tc % 2 == 0:
                    ts(S, "dve", stg[:, tsl], ps[:, :], 0.125 if a == 0 else 1.0, None, ALU.mult, None, [pb], [stgb])
                else:
                    act(S, stg[:, tsl], ps[:, :], AF.Copy, [pb], [stgb], scale=0.125 if a == 0 else 1.0)
                if tc == 1 and qktail:
                    qktail.pop(0)()
            while qktail:
                qktail.pop(0)()
            sq, sqB = SQ[(h * 2 + a) % 2]
            act(S, sq[:, :], stg[:, :], AF.Square, [stgb], [sqB])
            S.dma((qT_d if a == 0 else kT_d)[h], stg[:, :], reads=[stgb], writes=[qkB])

            def tail(m, h=h, a=a, sq=sq, sqB=sqB):
                if True:
                    for tc in range(4):
                        pn, pnb = PS[4 + tc]
                        mmg(S, pn[:, :], [(sel[:, m * 128:(m + 1) * 128], sq[:, tc * 512:(tc + 1) * 512])], [sqB, cB], [pnb])
                        col = ((h * 2 + a) * 2 + m) * 4 + tc
                        S.op("dve", lambda e, pn=pn, col=col: e.tensor_reduce(out=n2[:, col:col + 1], in_=pn[:, :], axis=AX.X, op=ALU.max),
                             [pnb], [n2B])
            qktail.append(lambda tail=tail: tail(0))
            qktail.append(lambda tail=tail: tail(1))
    while qktail:
        qktail.pop(0)()
    if STOP == "p2a":
        return
    n2m = sm("n2m", [128, 64]); Bhm = sm("Bhm", [128, 32])
    S.op("dve", lambda e: e.tensor_reduce(out=n2m[:], in_=n2[:].rearrange("p (x t) -> p x t", t=4), axis=AX.X, op=ALU.max),
         [n2B], [n2B])
    n2v = n2m[:].rearrange("p (h a m) -> p h a m", h=16, a=2)
    tt(S, "dve", Bhm[:].rearrange("p (h m) -> p h m", h=16), n2v[:, :, 0, :], n2v[:, :, 1, :], ALU.mult, [n2B], [n2B])
    act(S, Bhm[:], Bhm[:], AF.Sqrt, [n2B], [n2B])
    if STOP == "p2b":
        return
    def v_out(blk, t, ps, pb):
        cp(S, "act", Vt[:, t, 4 * blk:4 * blk + 4, 0:128], ps[:, :].rearrange("p (h v) -> p h v", h=4), [pb], [VB[t]])
    tm_proj512(cx, T["l1_wv"], v_out)
    if STOP == "p2c":
        return
    for ft in range(16):
        wt, wb = load_w(cx, ft, T["l1_wg"][ft], 2048)
        wv = wt[:, 0:2048].rearrange("p (k c) -> p k c", k=16)
        stg, stgb = H[ft % 2]
        for tc in range(4):
            tsl = slice(tc * 512, (tc + 1) * 512)
            ps, pb = PS[tc]
            mmg(S, ps[:, :], [(wv[:, kt, :], hT[:, kt, tsl]) for kt in range(16)], [wb] + hTb, [pb])
            act(S, stg[:, tsl], ps[:, :], AF.Silu, [pb], [stgb])
        S.dma(sgT_d[ft], stg[:, :], reads=[stgb], writes=[sgdB])

    if STOP == "p2":
        return
    oT = cx.hT
    f1v = F[1][0][:, :].bitcast(BF16)
    f2v = F[2][0][:, :].bitcast(BF16)
    f1a, f1b_, f2a, f2b_ = Buf("f1a"), Buf("f1b"), Buf("f2a"), Buf("f2b")
    w0a, w0b, w1a = Buf("w0a"), Buf("w0b"), Buf("w1a")
    QK8 = [[(H[0][0][:, :], H[0][1]), (H[1][0][:, :], H[1][1]), (H[2][0][:, :], H[2][1]), (W[0][0][:, 0:2048], w0a)],
           [(f1v[:, 0:2048], f1a), (f1v[:, 2048:4096], f1b_), (f2v[:, 0:2048], f2a), (f2v[:, 2048:4096], f2b_)]]
    SG2 = [(W[0][0][:, 2048:4096], w0b), (W[1][0][:, 0:2048], w1a)]
    PTt = [W[1][0][:, 2048 + i * 512:2048 + (i + 1) * 512] for i in range(4)]
    PTB = [Buf("PT%d" % i) for i in range(4)]
    attsub = [w0a, w0b, w1a, f1a, f1b_, f2a, f2b_] + PTB
    biasB2 = sm("biasB", [128, 64]); bbB = [Buf("biasB0"), Buf("biasB1")]
    f0, f0b = F[0]
    t2 = f0[:, 0:512]
    ovs = [f0[:, 512:1024], f0[:, 1024:1536]]
    junk = f0[:, 1536:2048]
    onbf = cx.smallbf("onbf", [128, 512]); onB = Buf("onbf")
    t2B, junkB = Buf("t2"), Buf("junk")
    ovB = [Buf("ov0"), Buf("ov1")]
    attsub += [t2B, junkB] + ovB
    attwhole = [W[0][1], W[1][1], f0b, F[1][1], F[2][1]]
    fence(cx, attwhole, attsub)
    for par in range(2):
        for t_i, (tl, tb) in enumerate(QK8[par]):
            memset(S, "pool", tl[64:128, :], 0.0, [tb])
            if t_i >= 2:
                memset(S, "pool", tl[64:65, :], 1.0, [tb])
    rsum = sm("rsum", [128, 8]); rsB = Buf("rsum")
    msq = sm("msq", [128, 4]); msB = Buf("msq")
    v3 = lambda ap, n: ap.rearrange("p (b c) -> p b c", b=n)

    iters = [(h, I, m, j) for h in range(16) for I in range(4) for m in range(2) for j in range(4 * I + 4)]
    NIT = len(iters)
    LA = 3
    meta = [None] * NIT
    hstate = {}

    def head_setup(h):
        par = h % 2
        (QA, qab), (QB, qbb), (KA, kab), (KB, kbb) = QK8[par]
        sg, sgb = SG2[par]
        S.dma(QA[0:64, :], qT_d[h][0:64, :], reads=[qkB], writes=[qab])
        S.dma(QB[0:64, :], qT_d[h][64:128, :], reads=[qkB], writes=[qbb])
        S.dma(KA[0:64, :], kT_d[h][0:64, :], reads=[qkB], writes=[kab])
        S.dma(KB[0:64, :], kT_d[h][64:128, :], reads=[qkB], writes=[kbb])
        S.dma(sg[:, :], sgT_d[h], reads=[sgdB], writes=[sgb])
        S.dma(QA[64:65, :], T["c_shrow"][h:h + 1, :], writes=[qab], eng="pool")
        S.dma(QB[64:65, :], T["c_shrow"][h:h + 1, :], writes=[qbb], eng="pool")
        bias = biasB2[:, par * 32:(par + 1) * 32]
        for m in range(2):
            ts(S, "dve", bias[:, m * 16:(m + 1) * 16], tab[:, h * 16:(h + 1) * 16], Bhm[:, h * 2 + m:h * 2 + m + 1], None,
               ALU.subtract, None, [tabB, n2B], [bbB[par]])
        hstate[h] = ([(QA, qab), (QB, qbb)], [(KA, kab), (KB, kbb)], sg, sgb, bias)

    def emit_qk(n):
        h, I, m, j = iters[n]
        if h not in hstate:
            head_setup(h)
        Qs, Ks, sg, sgb, bias = hstate[h]
        qT, qb = Qs[m]
        kT, kb = Ks[m]
        qs = I * 512
        b0 = max(0, j - 4 * I)
        c0 = b0 * 128
        pss, pssb = PS[4 + n % 4]
        PT, PTb = PTt[n % 4], PTB[n % 4]
        meta[n] = (PT, PTb, b0)
        diag = j >= 4 * I
        l0, r0 = kT[:, j * 128:(j + 1) * 128], qT[:, qs + c0:qs + 512]

        def fn(e):
            ins = e.matmul(pss[:, c0:512], lhsT=l0, rhs=r0, start=True, stop=not diag)
            if diag:
                ins = e.matmul(pss[:, c0:c0 + 128], lhsT=cx.ident[:, :], rhs=nmask[:, :], start=False, stop=True)
            return ins
        S.op("pe", fn, [kb, qb, cB, cx.constb], [pssb])
        off = j - 4 * I + 12
        act(S, PT[:, c0:512], pss[:, c0:512], AF.Exp, [pssb, bbB[h % 2]], [PTb], bias=bias[:, m * 16 + off:m * 16 + off + 1])

    started = set()
    pendingB = []

    def epiA(h, I, ep):
        ov, ovb = ovs[ep % 2], ovB[ep % 2]

        def rc(bank):
            pst_, pstb_ = PS[bank]
            S.op("dve", lambda e, bank=bank, pst_=pst_, rsum=rsum: e.reciprocal(
                out=rsum[:, 2 * bank:2 * bank + 2], in_=v3(pst_[:, :], 2)[:, :, 128]), [pstb_], [rsB])
        rc(0)
        rc(1)
        for bb in range(2):
            tt(S, "dve", v3(ov, 4)[:, 2 * bb:2 * bb + 2, :], v3(PS[bb][0][:, :], 2)[:, :, 0:128],
               rsum[:, 2 * bb:2 + 2 * bb].unsqueeze(2).broadcast_to([128, 2, 128]), ALU.mult, [PS[bb][1], rsB], [ovb])
        rc(2)
        rc(3)
        tt(S, "dve", rsum[:, 4:8], rsum[:, 4:8], nlam[:, 0:1].broadcast_to([128, 4]), ALU.mult, [rsB, lamB], [rsB])
        for bb in range(2):
            tt(S, "dve", v3(t2, 4)[:, 2 * bb:2 * bb + 2, :], v3(PS[2 + bb][0][:, :], 2)[:, :, 0:128],
               rsum[:, 4 + 2 * bb:6 + 2 * bb].unsqueeze(2).broadcast_to([128, 2, 128]), ALU.mult, [PS[2 + bb][1], rsB], [t2B])
        tt(S, "pool", ov, ov, t2, ALU.add, [ovb, t2B], [ovb])

    def epiB(h, I, ep):
        ov, ovb = ovs[ep % 2], ovB[ep % 2]
        sg, sgb = SG2[h % 2]
        qs = I * 512
        tt(S, "pool", junk, ov, ov, ALU.mult, [ovb], [junkB])
        S.op("dve", lambda e, msq=msq, jv=v3(junk, 4): e.tensor_reduce(out=msq[:], in_=jv, axis=AX.X, op=ALU.add), [junkB], [msB])
        act(S, msq[:], msq[:], AF.Ln, [msB], [msB], scale=1.0 / 128, bias=EPS)
        act(S, msq[:], msq[:], AF.Exp, [msB], [msB], scale=-0.5)
        tt(S, "dve", v3(ov, 4), v3(ov, 4), msq[:].unsqueeze(2).broadcast_to([128, 4, 128]), ALU.mult, [ovb, msB], [ovb])
        tt(S, "dve", v3(onbf[:, :], 4), v3(ov, 4), gsub[:, :].unsqueeze(1).broadcast_to([128, 4, 128]), ALU.mult, [ovb, gsB], [onB])
        pst, pstb = PS[7]
        pstv = pst[:, :].bitcast(BF16)
        for b_ in range(4):
            transp(S, pstv[:, b_ * 128:(b_ + 1) * 128], onbf[:, b_ * 128:(b_ + 1) * 128], cx.ident[:, :], [onB, cx.constb], [pstb])
        tt(S, "dve", oT[:, h, qs:qs + 512], pstv[:, 0:512], sg[:, qs:qs + 512], ALU.mult, [pstb, sgb], [hTb[h]])

    def emit_pv(n):
        h, I, m, j = iters[n]
        PT, PTb, b0 = meta[n]
        if m == 0 and j == 0:
            started.clear()
        for b_ in range(b0, 4):
            bank = m * 2 + b_ // 2
            acc = PS[bank][0][:, (b_ % 2) * 256:(b_ % 2) * 256 + 129]
            st_ = bank not in started
            started.add(bank)
            S.op("pe", lambda e, acc=acc, PT=PT, b_=b_, j=j, h=h, st_=st_, last=(j == 4 * I + b_): e.matmul(
                acc, lhsT=PT[:, b_ * 128:(b_ + 1) * 128], rhs=Vt[:, j, h, 0:129], start=st_, stop=last, skip_group_check=True),
                [PTb, VB[j]], [PS[bank][1]])
        ep = h * 4 + I
        if m == 1 and j == 4 * I + 3:
            epiA(h, I, ep)
            pendingB.append((h, I, ep))
        elif m == 0 and j == min(2, 4 * I + 3) and pendingB:
            epiB(*pendingB.pop(0))
        if I == 0 and m == 0 and j == 2 and h + 1 < 16 and (h + 1) not in hstate:
            head_setup(h + 1)

    for n in range(NIT + LA):
        if n < NIT:
            emit_qk(n)
        if n - LA >= 0:
            emit_pv(n - LA)
    while pendingB:
        epiB(*pendingB.pop(0))

    if STOP == "p3":
        return
    fence(cx, attsub, attwhole)
    x2T = T["x2T"]; x2B = Buf("x2T")
    r2f = cx.R2[:, :].bitcast(F32)
    NRES = 8
    slots = [(r2f[:, i * 2048:(i + 1) * 2048], Buf("x2s%d" % i)) for i in range(NRES)]
    fence(cx, VB, [b_ for _, b_ in slots])

    def ld_x1(dt_):
        xt_, xb_ = F[dt_ % 2]
        S.dma(xt_[:, 0:L], x1T[dt_], reads=[cx.x1Tb], writes=[xb_])
    ld_x1(0)
    for dt_ in range(16):
        wt, wb = load_w(cx, dt_, T["l1_wout"][dt_], 2048)
        wv = wt[:, 0:2048].rearrange("p (k c) -> p k c", k=16)
        if dt_ < 15:
            ld_x1(dt_ + 1)
        xt, xb = F[dt_ % 2]
        ot, otb = slots[dt_] if dt_ < NRES else F[2]
        for tc in range(4):
            tsl = slice(tc * 512, (tc + 1) * 512)
            ps, pb = PS[tc]
            mmg(S, ps[:, :], [(wv[:, kt, :], oT[:, kt, tsl]) for kt in range(16)], [wb] + hTb, [pb])
            tt(S, "dve", ot[:, tsl], ps[:, :], xt[:, tsl], ALU.add, [pb, xb], [otb])
        if dt_ >= NRES:
            S.dma(x2T[dt_], ot[:, 0:L], reads=[otb], writes=[x2B])
        act(S, xt[:, 0:L], ot[:, 0:L], AF.Square, [otb, xb], [xb])
        for tc in range(4):
            ps, pb = PS[4 + tc]
            S.op("pe", lambda e, ps=ps, xt=xt, tc=tc, dt_=dt_: e.matmul(
                ps[:, :], lhsT=cx.ones_f[:, :], rhs=xt[:, tc * 512:(tc + 1) * 512], start=(dt_ == 0), stop=(dt_ == 15)),
                [xb, cx.constb], [pb])
    rs, rsb = F[2]
    for tc in range(4):
        ps, pb = PS[4 + tc]
        act(S, rs[:, tc * 512:(tc + 1) * 512], ps[:, :], AF.Sqrt, [pb], [rsb], scale=1.0 / D, bias=EPS)
    recip(S, rs[:, 0:L], rs[:, 0:L], [rsb], [rsb])

    def ld_x2(dt_):
        xt_, xb_ = F[dt_ % 2]
        S.dma(xt_[:, 0:L], x2T[dt_], reads=[x2B], writes=[xb_])
    ld_x2(NRES)
    ld_x2(NRES + 1)
    for dt_ in range(NRES):
        sl, slb = slots[dt_]
        stt(S, sl, sl, gf[:, dt_:dt_ + 1], rs[:, 0:L], ALU.mult, ALU.mult, [slb, rsb, g1b], [slb])
        S.dma(T["outT"][dt_], sl, reads=[slb])
    for dt_ in range(NRES, 16):
        xt, xb = F[dt_ % 2]
        stt(S, xt[:, 0:L], xt[:, 0:L], gf[:, dt_:dt_ + 1], rs[:, 0:L], ALU.mult, ALU.mult, [xb, rsb, g1b], [xb])
        S.dma(T["outT"][dt_], xt[:, 0:L], reads=[xb])
        if dt_ + 2 < 16:
            ld_x2(dt_ + 2)


L0_IN = {
    "xT": [16, 128, 2048], "l0_g": [128, 16], "l0_cw": [128, 96], "l0_cb": [128, 24], "l0_dtb": [1, 32], "l0_alog": [1, 32],
    "l0_dsk": [1, 32], "l0_ssg": [128, 16], "l0_wxbc": [24, 128, 2048], "l0_wzb": [4, 128, 8192], "l0_wdt": [128, 512],
    "l0_wuz": [16, 128, 4096], "l0_wv": [4, 128, 8192], "l0_wsT": [128, 2048], "l0_sb": [1, 2048], "l0_lng": [1, 2048],
    "l0_lnb": [1, 2048], "l0_wout": [16, 128, 4096],
}
L1_IN = {
    "l1_g": [128, 16], "fin_g": [128, 16], "l1_lq1": [1, 64], "l1_lk1": [1, 64], "l1_lq2": [1, 64], "l1_lk2": [1, 64],
    "l1_subg": [1, 128], "c_tab": [128, 256], "c_shrow": [16, 2048], "c_negmask": [128, 128],
    "c_sel": [128, 256], "l1_wqk": [16, 128, 4096], "l1_wv": [4, 128, 8192], "l1_wg": [16, 128, 2048], "l1_wout": [16, 128, 2048],
}
C_IN = {"c_ident": [128, 128], "c_maskT": [128, 128]}
L0_SCR = {"x_tm": ([16, 128, 2048], BF16), "szb_tm": ([16, 128, 2048], BF16), "yb_tm": ([16, 128, 2048], BF16),
          "gv_tm": ([16, 128, 2048], BF16)}
L1_SCR = {"qT_d": ([16, 128, 2048], BF16), "kT_d": ([16, 128, 2048], BF16), "sgT_d": ([16, 128, 2048], BF16),
          "x2T": ([16, 128, 2048], F32)}


def build_program(mode):
    nc = bass.Bass("TRN2", target_bir_lowering=False)
    T = {}
    ins = dict(C_IN)
    if mode in ("l0", "fused"):
        ins.update(L0_IN)
    if mode in ("l1", "fused"):
        ins.update(L1_IN)
    for n, shp in ins.items():
        T[n] = nc.dram_tensor(n, shp, F32, kind="ExternalInput").ap()
    scr = {}
    if mode in ("l0", "fused"):
        scr.update(L0_SCR)
    if mode in ("l1", "fused"):
        scr.update(L1_SCR)
    for n, (shp, dt) in scr.items():
        if DBG == "smallscr" and n in ("kT_d", "sgT_d", "x2T"):
            T[n] = T["qT_d"]
            continue
        T[n] = nc.dram_tensor(n, shp, dt, kind="Internal").ap()
    if mode == "l0":
        T["x1T"] = nc.dram_tensor("x1T", [16, 128, 2048], F32, kind="ExternalOutput").ap()
    elif mode == "l1":
        T["x1T"] = nc.dram_tensor("x1T", [16, 128, 2048], F32, kind="ExternalInput").ap()
    else:
        T["x1T"] = nc.dram_tensor("x1T", [16, 128, 2048], F32, kind="Internal").ap()
    if mode in ("l1", "fused"):
        T["outT"] = nc.dram_tensor("outT", [16, 128, 2048], F32, kind="ExternalOutput").ap()

    with ExitStack() as st:
        cx = Cx()
        cx.nc = nc
        cx.S = S = Sched(nc)
        sb = lambda n, shp, dt: st.enter_context(nc.sbuf_tensor(n, shp, dt))
        cx.small = lambda n, shp: sb(n, shp, F32)
        cx.smallbf = lambda n, shp: sb(n, shp, BF16)
        R1 = sb("R1", [128, 32768], BF16)
        cx.hT = R1[:, :].rearrange("p (k t) -> p k t", k=16)
        cx.hTb = [Buf("hT%d" % k) for k in range(16)]
        cx.R2 = sb("R2", [128, 34816], BF16)
        cx.r2all = []
        cx.W = [(sb("W%d" % i, [128, 4096], BF16), Buf("W%d" % i)) for i in range(2)]
        cx.F = [(sb("F%d" % i, [128, 2056], F32), Buf("F%d" % i)) for i in range(3)]
        cx.H = [(sb("H%d" % i, [128, 2048], BF16), Buf("H%d" % i)) for i in range(3)]
        cx.J = [(cx.R2[:, 24576 + i * 2048:24576 + (i + 1) * 2048], Buf("J%d" % i)) for i in range(4)]
        cx.PS = [(st.enter_context(nc.psum_tensor("PS%d" % i, [128, 512], F32)), Buf("PS%d" % i, excl=True)) for i in range(8)]
        cx.x1Tb = Buf("x1T")
        cx.fuse_stats = (mode == "fused")
        cx.fz = sb("fz", [128, 2], F32)
        cx.fzB = Buf("fz")
        cx.constb = Buf("const")
        cx.ones_f = sb("ones_f", [128, 128], F32)
        cx.tri_f = sb("tri_f", [128, 128], F32)
        cx.ident = sb("ident", [128, 128], BF16)
        cx.maskT = sb("maskT", [128, 128], BF16)
        cx.ones_row = sb("ones_row", [1, 128], BF16)
        memset(S, "pool", cx.ones_f[:, :], 1.0, [cx.constb])
        memset(S, "pool", cx.ones_row[:, :], 1.0, [cx.constb])
        S.dma(cx.tri_f[:, :], T["c_maskT"], writes=[cx.constb])
        S.dma(cx.ident[:, :], T["c_ident"], writes=[cx.constb], eng="pool")
        S.dma(cx.maskT[:, :], T["c_maskT"], writes=[cx.constb], eng="pool")
        if mode in ("l0", "fused"):
            emit_l0(cx, T)
        if mode in ("l1", "fused"):
            emit_l1(cx, T)
        S.emit(st)
    return nc


def _fm_tiles(Wc):
    K, N = Wc.shape
    n = N // 128
    return np.ascontiguousarray(Wc.reshape(K // 128, 128, n, 128).transpose(2, 1, 0, 3)).reshape(n, 128, -1)


def _tm_blocks(Wc, bw):
    K, N = Wc.shape
    n = N // bw
    return np.ascontiguousarray(Wc.reshape(K // 128, 128, n, bw).transpose(2, 1, 0, 3)).reshape(n, 128, -1)


def _col(v, n):
    return np.ascontiguousarray(np.asarray(v, np.float32).reshape(n, 128).T)


def prep_common():
    ident = np.eye(128, dtype=np.float32)
    maskT = np.triu(np.ones((128, 128), np.float32))
    return {"c_ident": ident, "c_maskT": maskT}


def prep_l0(inp):
    f = lambda n: np.asarray(inp[n], np.float32)
    W = f("l0_w_in")
    d = {}
    d["l0_g"] = _col(f("l0_norm_g"), 16)
    cwt = f("l0_conv_w")
    d["l0_cw"] = np.ascontiguousarray(cwt.reshape(4, 24, 128).transpose(2, 1, 0)).reshape(128, 96)
    d["l0_cb"] = _col(f("l0_conv_b"), 24)
    d["l0_dtb"] = f("l0_dt_bias").reshape(1, 32)
    d["l0_alog"] = f("l0_a_log").reshape(1, 32)
    d["l0_dsk"] = f("l0_d_skip").reshape(1, 32)
    d["l0_ssg"] = _col(f("l0_ssm_norm_g"), 16)
    d["l0_wxbc"] = _fm_tiles(W[:, 8192:11264])
    d["l0_wzb"] = _tm_blocks(W[:, 6144:8192], 512)
    d["l0_wdt"] = _tm_blocks(W[:, 11264:11296], 32)[0]
    wu = _fm_tiles(W[:, 0:2048]); wz = _fm_tiles(W[:, 4096:6144])
    d["l0_wuz"] = np.ascontiguousarray(np.concatenate([wu, wz], axis=2))
    d["l0_wv"] = _tm_blocks(W[:, 2048:4096], 512)
    d["l0_wsT"] = np.ascontiguousarray(f("l0_spatial_w").transpose(2, 0, 1)).reshape(128, 2048)
    d["l0_sb"] = f("l0_spatial_b").reshape(1, 2048)
    d["l0_lng"] = f("l0_gmlp_ln_g").reshape(1, 2048)
    d["l0_lnb"] = f("l0_gmlp_ln_b").reshape(1, 2048)
    Wo = f("l0_w_out")
    d["l0_wout"] = np.ascontiguousarray(Wo.reshape(32, 128, 16, 128).transpose(2, 1, 0, 3)).reshape(16, 128, 4096)
    return d


def prep_l1(inp):
    f = lambda n: np.asarray(inp[n], np.float32)
    W = f("l1_w_in")
    d = {}
    d["l1_g"] = _col(f("l1_norm_g"), 16)
    d["fin_g"] = _col(f("final_norm_g"), 16)
    for a, b in [("l1_lq1", "l1_lambda_q1"), ("l1_lk1", "l1_lambda_k1"), ("l1_lq2", "l1_lambda_q2"), ("l1_lk2", "l1_lambda_k2")]:
        d[a] = f(b).reshape(1, 64)
    d["l1_subg"] = f("l1_subln_g").reshape(1, 128)
    slopes = (2.0 ** (-8.0 * (np.arange(16, dtype=np.float64) + 1.0) / 16)).astype(np.float32)
    kl = np.arange(128, dtype=np.float32)[:, None, None]
    off = (np.arange(16, dtype=np.float32) - 12.0)[None, None, :]
    d["c_tab"] = (slopes[None, :, None] * (kl + 128.0 * off)).astype(np.float32).reshape(128, 256)
    d["c_shrow"] = np.tile(-slopes[:, None] * np.arange(512, dtype=np.float32)[None, :], (1, 4)).astype(np.float32)
    d["c_negmask"] = np.where(np.arange(128)[:, None] > np.arange(128)[None, :], -30000.0, 0.0).astype(np.float32)
    sel = np.zeros((128, 2, 128), np.float32); sel[0:64, 0, :] = 1.0; sel[64:128, 1, :] = 1.0
    d["c_sel"] = sel.reshape(128, 256)
    wq = _fm_tiles(W[:, 0:2048]); wk = _fm_tiles(W[:, 2048:4096])
    d["l1_wqk"] = np.ascontiguousarray(np.concatenate([wq, wk], axis=2))
    d["l1_wv"] = _tm_blocks(W[:, 4096:6144], 512)
    d["l1_wg"] = _fm_tiles(W[:, 6144:8192])
    Wo = f("l1_w_out")
    d["l1_wout"] = np.ascontiguousarray(Wo.reshape(16, 128, 16, 128).transpose(2, 1, 0, 3)).reshape(16, 128, 2048)
    return d


def _xT(x):
    return [np.ascontiguousarray(x[b].T).reshape(16, 128, 2048) for b in range(x.shape[0])]


MODE = "fused"
STOP = None
DBG = None
_PROG = {}


def _prog(mode):
    if mode not in _PROG:
        _PROG[mode] = build_program(mode)
    return _PROG[mode]


def kernel(**inputs):
    x = np.asarray(inputs["x"], np.float32)
    nb = x.shape[0]
    cores = list(range(nb))
    com = prep_common()
    xT = _xT(x)
    if MODE == "fused":
        shared = dict(com); shared.update(prep_l0(inputs)); shared.update(prep_l1(inputs))
        maps = [dict(shared, xT=xT[b]) for b in range(nb)]
        res = run_bass_kernel_spmd(_prog("fused"), maps, core_ids=cores)
        outT = [r["outT"] for r in res.results]
    else:
        s0 = dict(com); s0.update(prep_l0(inputs))
        res = run_bass_kernel_spmd(_prog("l0"), [dict(s0, xT=xT[b]) for b in range(nb)], core_ids=cores)
        x1T = [np.asarray(r["x1T"]) for r in res.results]
        s1 = dict(com); s1.update(prep_l1(inputs))
        res = run_bass_kernel_spmd(_prog("l1"), [dict(s1, x1T=x1T[b]) for b in range(nb)], core_ids=cores)
        outT = [r["outT"] for r in res.results]
    out = np.stack([np.asarray(o, np.float32).reshape(2048, 2048).T for o in outT], axis=0)
    return np.ascontiguousarray(out)
```
